# Optimizing a Trainium2 kernel written in Bass

```python
import jax, jax.numpy as jnp
from jax import lax
import numpy as np

D_MODEL = 1024
BATCH = 8
SEQ = 2048
DEPTH = 4

CTX_LEN = 256
GRID_W = 64
N_MOD = 9
D_FF = 2816
EPS = 1e-6
GDN_HEADS = 4
GDN_HEAD_DIM = 128
GDN_WIDTH = GDN_HEADS * GDN_HEAD_DIM
CONV_K = 5
CONV_PAD = CONV_K // 2
CHUNK = 64
MLA_HEADS = 8
MLA_NOPE = 64
MLA_ROPE = 32
MLA_V = 64
MLA_WIDTH = MLA_HEADS * MLA_V
Q_RANK = 384
KV_RANK = 256
ROPE_BASE = 10000.0
AXIS_DIM = MLA_ROPE // 2
Q_BLOCK = 128
MLA_SCALE = (MLA_NOPE + MLA_ROPE) ** -0.5
MIX_WIDTH = GDN_WIDTH + MLA_WIDTH
OFF_Q = 0
OFF_K = GDN_WIDTH
OFF_V = 2 * GDN_WIDTH
OFF_Z = 3 * GDN_WIDTH
OFF_A = 4 * GDN_WIDTH
OFF_B = OFF_A + 2 * GDN_HEADS
OFF_CQ = OFF_B + 2 * GDN_HEADS
OFF_CKV = OFF_CQ + Q_RANK
OFF_KR = OFF_CKV + KV_RANK
IN_COLS = OFF_KR + MLA_ROPE

kernel_name = "hybrid_gdn_mla_macaron_dit"


def rmsnorm(x, g):
    xf = x.astype(jnp.float32)
    y = xf * lax.rsqrt(jnp.mean(xf * xf, axis=-1, keepdims=True) + EPS)
    return (y * g.astype(jnp.float32)).astype(x.dtype)


def l2norm(x):
    xf = x.astype(jnp.float32)
    return (xf * lax.rsqrt(jnp.sum(xf * xf, axis=-1, keepdims=True) + EPS)).astype(x.dtype)


def pre_mod(x, g, m, slot):
    return rmsnorm(x, g) * (1 + m[:, None, 3 * slot + 1]) + m[:, None, 3 * slot]


def post_residual(x, y, g, m, slot, weight):
    return x + weight * m[:, None, 3 * slot + 2] * rmsnorm(y, g)


def ffn_sublayer(x, m, slot, g_pre, g_post, wg, wu, wd):
    h = pre_mod(x, g_pre, m, slot)
    y = (jax.nn.silu(h @ wg) * (h @ wu)) @ wd
    return post_residual(x, y, g_post, m, slot, 0.5)


def short_conv(u, w):
    C = u.shape[-1]
    y = lax.conv_general_dilated(u, w[:, None, :].astype(u.dtype), window_strides=(1,),
                                 padding=[(CONV_PAD, CONV_PAD)],
                                 dimension_numbers=('NWC', 'WIO', 'NWC'),
                                 feature_group_count=C)
    return jax.nn.silu(y)


def gated_delta_chunked(q, k, v, g, beta, S0):
    out_dtype = v.dtype
    q, k, v, g, beta = [u.astype(jnp.float32) for u in (q, k, v, g, beta)]
    B2, T, H, dk = q.shape
    dv = v.shape[-1]
    n = T // CHUNK

    def blocks(u):
        return u.reshape(B2, n, CHUNK, H, -1).transpose(0, 3, 1, 2, 4)

    qb, kb, vb = blocks(q), blocks(k), blocks(v)
    gb = g.reshape(B2, n, CHUNK, H).transpose(0, 3, 1, 2)
    bb = beta.reshape(B2, n, CHUNK, H).transpose(0, 3, 1, 2)
    G = jnp.cumsum(gb, axis=-1)
    diff = G[..., :, None] - G[..., None, :]
    idx = jnp.arange(CHUNK)
    incl = idx[:, None] >= idx[None, :]
    strict = idx[:, None] > idx[None, :]
    decay = jnp.where(incl, jnp.exp(jnp.where(incl, diff, 0.0)), 0.0)
    kk = jnp.einsum('bhnid,bhnjd->bhnij', kb, kb)
    A = jnp.where(strict, bb[..., :, None] * kk * decay, 0.0)
    eyeA = A + jnp.eye(CHUNK, dtype=A.dtype)
    eG = jnp.exp(G)
    rhs = jnp.concatenate([bb[..., None] * vb, (bb * eG)[..., None] * kb], axis=-1)
    sol = lax.linalg.triangular_solve(eyeA, rhs, left_side=True, lower=True,
                                      unit_diagonal=True)
    u0, w = sol[..., :dv], sol[..., dv:]
    qk = jnp.einsum('bhnid,bhnjd->bhnij', qb, kb) * decay
    q_dec = qb * eG[..., None]
    G_last = G[..., -1]
    k_dec = kb * jnp.exp(G_last[..., None] - G)[..., None]
    chunk_dec = jnp.exp(G_last)
    xs = tuple(jnp.moveaxis(u, 2, 0) for u in (q_dec, k_dec, qk, u0, w, chunk_dec))

    def step(S, inp):
        qd, kd, qkc, u0c, wc, gl = inp
        U = u0c - jnp.einsum('bhcd,bhde->bhce', wc, S)
        o = jnp.einsum('bhcd,bhde->bhce', qd, S) + jnp.einsum('bhij,bhje->bhie', qkc, U)
        S = gl[..., None, None] * S + jnp.einsum('bhcd,bhce->bhde', kd, U)
        return S, o

    S_fin, o = lax.scan(step, S0, xs)
    o = o.transpose(1, 0, 3, 2, 4).reshape(B2, T, H, dv)
    return o.astype(out_dtype), S_fin


def gdn_group(qkv, z, a, b, conv_w, a_log, dt_bias, out_norm, L):
    B, T, _ = qkv.shape
    qkv = jnp.concatenate([short_conv(qkv[:, :L], conv_w), short_conv(qkv[:, L:], conv_w)], axis=1)
    q = qkv[..., :GDN_WIDTH].reshape(B, T, GDN_HEADS, GDN_HEAD_DIM)
    k = qkv[..., GDN_WIDTH:2 * GDN_WIDTH].reshape(B, T, GDN_HEADS, GDN_HEAD_DIM)
    v = qkv[..., 2 * GDN_WIDTH:].reshape(B, T, GDN_HEADS, GDN_HEAD_DIM)
    q = l2norm(q) * (GDN_HEAD_DIM ** -0.5)
    k = l2norm(k)
    a = a.reshape(B, T, 2, GDN_HEADS)
    b = b.reshape(B, T, 2, GDN_HEADS)
    g = -jnp.exp(a_log) * jax.nn.softplus(a + dt_bias)
    beta = jax.nn.sigmoid(b)

    def flip(u):
        return u[:, ::-1]

    def bidir(qs, ks, vs, gs, bs, S0):
        qq = jnp.concatenate([qs, flip(qs)], 0)
        kk = jnp.concatenate([ks, flip(ks)], 0)
        vv = jnp.concatenate([vs, flip(vs)], 0)
        gg = jnp.concatenate([gs[:, :, 0], flip(gs[:, :, 1])], 0)
        be = jnp.concatenate([bs[:, :, 0], flip(bs[:, :, 1])], 0)
        o, S = gated_delta_chunked(qq, kk, vv, gg, be, S0)
        return o[:B] + flip(o[B:]), S

    S_zero = jnp.zeros((2 * B, GDN_HEADS, GDN_HEAD_DIM, GDN_HEAD_DIM), jnp.float32)
    o_c, S_c = bidir(q[:, :L], k[:, :L], v[:, :L], g[:, :L], beta[:, :L], S_zero)
    o_l, _ = bidir(q[:, L:], k[:, L:], v[:, L:], g[:, L:], beta[:, L:], S_c)
    o = jnp.concatenate([o_c, o_l], axis=1)
    o = rmsnorm(o, out_norm) * jax.nn.silu(z.reshape(B, T, GDN_HEADS, GDN_HEAD_DIM))
    return o.reshape(B, T, GDN_WIDTH)


def rope2d(x, cos, sin):
    half = MLA_ROPE // 2
    x1, x2 = x[..., :half], x[..., half:]
    return jnp.concatenate([x1 * cos - x2 * sin, x2 * cos + x1 * sin], axis=-1).astype(x.dtype)


def attend(q, k, v):
    s = jnp.einsum('bqhd,bkhd->bhqk', q, k).astype(jnp.float32) * MLA_SCALE
    p = jax.nn.softmax(s, axis=-1)
    return jnp.einsum('bhqk,bkhd->bqhd', p.astype(v.dtype), v)


def mla_group(c_q, c_kv, k_rope, q_norm, kv_norm, w_uq, w_ukv, cos, sin, L):
    B, T, _ = c_q.shape
    q = (rmsnorm(c_q, q_norm) @ w_uq).reshape(B, T, MLA_HEADS, MLA_NOPE + MLA_ROPE)
    kv = (rmsnorm(c_kv, kv_norm) @ w_ukv).reshape(B, T, MLA_HEADS, MLA_NOPE + MLA_V)
    q_nope, q_rope = q[..., :MLA_NOPE], q[..., MLA_NOPE:]
    k_nope, v = kv[..., :MLA_NOPE], kv[..., MLA_NOPE:]
    q_rope = jnp.concatenate([q_rope[:, :L], rope2d(q_rope[:, L:], cos[:, None], sin[:, None])], axis=1)
    k_rope = jnp.concatenate([k_rope[:, :L], rope2d(k_rope[:, L:], cos, sin)], axis=1)
    qf = jnp.concatenate([q_nope, q_rope], axis=-1)
    kf = jnp.concatenate([k_nope, jnp.broadcast_to(k_rope[:, :, None], (B, T, MLA_HEADS, MLA_ROPE))], axis=-1)
    o_c = attend(qf[:, :L], kf[:, :L], v[:, :L])
    S = T - L
    nb = S // Q_BLOCK
    ql = qf[:, L:].reshape(B, nb, Q_BLOCK, MLA_HEADS, MLA_NOPE + MLA_ROPE).transpose(1, 0, 2, 3, 4)
    o_l = lax.map(lambda qb: attend(qb, kf, v), ql)
    o_l = o_l.transpose(1, 0, 2, 3, 4).reshape(B, S, MLA_HEADS, MLA_V)
    return jnp.concatenate([o_c, o_l], axis=1).reshape(B, T, MLA_WIDTH)


def mixer(h_c, h_l, w_in, conv_w, a_log, dt_bias, out_norm, q_norm, kv_norm, w_uq, w_ukv, w_out, cos, sin):
    L = h_c.shape[1]
    h = jnp.concatenate([h_c, h_l], axis=1)
    p = h @ w_in
    o_gdn = gdn_group(p[..., OFF_Q:OFF_Z], p[..., OFF_Z:OFF_A], p[..., OFF_A:OFF_B],
                      p[..., OFF_B:OFF_CQ], conv_w, a_log, dt_bias, out_norm, L)
    o_mla = mla_group(p[..., OFF_CQ:OFF_CKV], p[..., OFF_CKV:OFF_KR], p[..., OFF_KR:IN_COLS],
                      q_norm, kv_norm, w_uq, w_ukv, cos, sin, L)
    o = jnp.concatenate([o_gdn, o_mla], axis=-1) @ w_out
    return o[:, :L], o[:, L:]


def setup_inputs(seed: int = 0) -> dict:
    key = jax.random.key(seed)
    ks = jax.random.split(key, 24)
    f32 = jnp.float32

    def nrm(k, shape, fan):
        return jax.random.normal(k, shape, f32) * (fan ** -0.5)

    def gain(k, shape):
        return 1.0 + 0.05 * jax.random.normal(k, shape, f32)

    dt = jnp.exp(jax.random.uniform(ks[10], (DEPTH, 2, GDN_HEADS), f32,
                                    minval=np.log(0.001), maxval=np.log(0.1)))
    return {
        "x": jax.random.normal(ks[0], (BATCH, SEQ, D_MODEL), f32),
        "c": jax.random.normal(ks[1], (BATCH, D_MODEL), f32),
        "ctx": jax.random.normal(ks[2], (BATCH, CTX_LEN, D_MODEL), f32),
        "c_ctx": jax.random.normal(ks[3], (D_MODEL,), f32),
        "w_ada": 0.5 * nrm(ks[4], (DEPTH, D_MODEL, N_MOD * D_MODEL), D_MODEL),
        "b_ada": 0.02 * jax.random.normal(ks[5], (DEPTH, N_MOD * D_MODEL), f32),
        "norm_pre": gain(ks[6], (DEPTH, 3, D_MODEL)),
        "norm_post": gain(ks[7], (DEPTH, 3, D_MODEL)),
        "ffn_w_gate": nrm(ks[8], (DEPTH, 2, D_MODEL, D_FF), D_MODEL),
        "ffn_w_up": nrm(ks[9], (DEPTH, 2, D_MODEL, D_FF), D_MODEL),
        "ffn_w_down": nrm(ks[11], (DEPTH, 2, D_FF, D_MODEL), D_FF),
        "w_in": nrm(ks[12], (DEPTH, D_MODEL, IN_COLS), D_MODEL),
        "gdn_conv": nrm(ks[13], (DEPTH, CONV_K, 3 * GDN_WIDTH), CONV_K),
        "gdn_a_log": jnp.log(jax.random.uniform(ks[14], (DEPTH, 2, GDN_HEADS), f32, minval=1.0, maxval=16.0)),
        "gdn_dt_bias": dt + jnp.log(-jnp.expm1(-dt)),
        "gdn_out_norm": gain(ks[15], (DEPTH, GDN_HEAD_DIM)),
        "mla_q_norm": gain(ks[16], (DEPTH, Q_RANK)),
        "mla_kv_norm": gain(ks[17], (DEPTH, KV_RANK)),
        "mla_w_uq": nrm(ks[18], (DEPTH, Q_RANK, MLA_HEADS * (MLA_NOPE + MLA_ROPE)), Q_RANK),
        "mla_w_ukv": nrm(ks[19], (DEPTH, KV_RANK, MLA_HEADS * (MLA_NOPE + MLA_V)), KV_RANK),
        "w_out": nrm(ks[20], (DEPTH, MIX_WIDTH, D_MODEL), MIX_WIDTH),
    }


def reference(x, c, ctx, c_ctx, w_ada, b_ada, norm_pre, norm_post, ffn_w_gate, ffn_w_up,
              ffn_w_down, w_in, gdn_conv, gdn_a_log, gdn_dt_bias, gdn_out_norm, mla_q_norm,
              mla_kv_norm, mla_w_uq, mla_w_ukv, w_out):
    B, S, D = x.shape
    ROWS = S // GRID_W
    row = jnp.repeat(jnp.arange(ROWS), GRID_W).astype(jnp.float32)
    col = jnp.tile(jnp.arange(GRID_W), ROWS).astype(jnp.float32)
    inv_freq = jnp.power(ROPE_BASE, -jnp.arange(0, AXIS_DIM, 2, dtype=jnp.float32) / AXIS_DIM)
    ang = jnp.concatenate([row[:, None] * inv_freq, col[:, None] * inv_freq], axis=-1)
    cos, sin = jnp.cos(ang), jnp.sin(ang)

    s_lat = jax.nn.silu(c)
    s_ctx = jax.nn.silu(c_ctx)[None]
    xc, xl = ctx, x
    for l in range(DEPTH):
        last = l == DEPTH - 1
        m_l = (s_lat @ w_ada[l] + b_ada[l]).reshape(B, N_MOD, D)
        m_c = (s_ctx @ w_ada[l] + b_ada[l]).reshape(1, N_MOD, D)
        xc = ffn_sublayer(xc, m_c, 0, norm_pre[l, 0], norm_post[l, 0], ffn_w_gate[l, 0], ffn_w_up[l, 0], ffn_w_down[l, 0])
        xl = ffn_sublayer(xl, m_l, 0, norm_pre[l, 0], norm_post[l, 0], ffn_w_gate[l, 0], ffn_w_up[l, 0], ffn_w_down[l, 0])
        hc = pre_mod(xc, norm_pre[l, 1], m_c, 1)
        hl = pre_mod(xl, norm_pre[l, 1], m_l, 1)
        oc, ol = mixer(hc, hl, w_in[l], gdn_conv[l], gdn_a_log[l], gdn_dt_bias[l], gdn_out_norm[l],
                       mla_q_norm[l], mla_kv_norm[l], mla_w_uq[l], mla_w_ukv[l], w_out[l], cos, sin)
        xl = post_residual(xl, ol, norm_post[l, 1], m_l, 1, 1.0)
        xl = ffn_sublayer(xl, m_l, 2, norm_pre[l, 2], norm_post[l, 2], ffn_w_gate[l, 1], ffn_w_up[l, 1], ffn_w_down[l, 1])
        if not last:
            xc = post_residual(xc, oc, norm_post[l, 1], m_c, 1, 1.0)
            xc = ffn_sublayer(xc, m_c, 2, norm_pre[l, 2], norm_post[l, 2], ffn_w_gate[l, 1], ffn_w_up[l, 1], ffn_w_down[l, 1])
    return xl
```

```python
from contextlib import ExitStack
import numpy as np
import concourse.bass as bass
import concourse.mybir as mybir
from concourse.bass_utils import run_bass_kernel_spmd

F32 = mybir.dt.float32
BF16 = mybir.dt.bfloat16
AF = mybir.ActivationFunctionType
ALU = mybir.AluOpType
AX = mybir.AxisListType

ENGS = ("pe", "act", "dve", "pool", "sp")
DMA_K = 8


class Buf:
    __slots__ = ("name", "w", "r", "excl")

    def __init__(self, name="", excl=False):
        self.name = name
        self.w = None
        self.r = {}
        self.excl = excl


class Op:
    __slots__ = ("eng", "fn", "deps", "inc", "sem", "val", "dma")

    def __init__(self, eng, fn, dma):
        self.eng = eng
        self.fn = fn
        self.deps = []
        self.inc = False
        self.sem = None
        self.val = 0
        self.dma = dma


class Prog:
    def __init__(self, nc, stack):
        self.nc = nc
        self.ops = {e: [] for e in ENGS}
        self.dma_hist = {e: [] for e in ENGS}
        self.esem = {e: stack.enter_context(nc.semaphore("es_" + e)) for e in ENGS}
        self.dsem = {
            e: [stack.enter_context(nc.semaphore("ds_%s%d" % (e, i))) for i in range(DMA_K)]
            for e in ("sp", "pool", "act")
        }
        self.cnt = {e: 0 for e in ENGS}
        self.nd = {e: 0 for e in ENGS}
        self.known = {e: {} for e in ENGS}
        self.carry = {}
        self.bufs = []
        self.stats = {e: [0, 0] for e in ENGS}

    def buf(self, name="", excl=False):
        b = Buf(name, excl)
        self.bufs.append(b)
        return b

    def bufs_n(self, name, n):
        return [self.buf("%s%d" % (name, i)) for i in range(n)]

    def _add(self, eng, fn, r, w, dma=False):
        op = Op(eng, fn, dma)
        deps = []
        xr = [b for b in r if b.excl]
        if xr:
            r = [b for b in r if not b.excl]
            w = list(w) + [b for b in xr if b not in w]
        for b in r:
            if b.w is not None:
                deps.append(b.w)
        for b in w:
            if b.w is not None and (dma or b.w.dma or b.w.eng != eng):
                deps.append(b.w)
            for o in b.r.values():
                if dma or o.dma or o.eng != eng:
                    deps.append(o)
        for b in r:
            b.r[eng] = op
        for b in w:
            b.w = op
            b.r = {}
        if dma:
            h = self.dma_hist[eng]
            if len(h) >= DMA_K:
                deps.append(h[-DMA_K])
            h.append(op)
            op.inc = True
        seen = set()
        for d in deps:
            if d is op or id(d) in seen:
                continue
            if (not d.dma) and d.eng == "pe" and eng == "pe" and not dma:
                continue
            seen.add(id(d))
            d.inc = True
            op.deps.append(d)
        self.ops[eng].append(op)
        return op

    def pe(self, fn, r=(), w=()):
        return self._add("pe", fn, r, w)

    def act(self, fn, r=(), w=()):
        return self._add("act", fn, r, w)

    def dve(self, fn, r=(), w=()):
        return self._add("dve", fn, r, w)

    def pool(self, fn, r=(), w=()):
        return self._add("pool", fn, r, w)

    def dma(self, q, out, in_, r=(), w=(), **kw):
        return self._add(q, lambda e: e.dma_start(out=out, in_=in_, **kw), r, w, dma=True)

    def flush(self):
        nc = self.nc
        newcarry = {}
        for e in ENGS:
            lastc = None
            for op in self.ops[e]:
                if not op.dma:
                    lastc = op
            if lastc is not None:
                lastc.inc = True
            for op in self.ops[e]:
                if op.dma:
                    n = self.nd[e]
                    op.sem = self.dsem[e][n % DMA_K]
                    op.val = 16 * (n // DMA_K + 1)
                    self.nd[e] = n + 1
                    newcarry[op.sem.num] = (op.sem, op.val)
                elif op.inc:
                    self.cnt[e] += 1
                    op.sem = self.esem[e]
                    op.val = self.cnt[e]
                    newcarry[op.sem.num] = (op.sem, op.val)
        carry = dict(self.carry)
        with nc.Block() as block:
            def emit(e):
                def body(eng):
                    known = self.known[e]
                    nwait = 0
                    for k, (sm, v) in carry.items():
                        if known.get(k, 0) < v:
                            eng.wait_ge(sm, v)
                            known[k] = v
                            nwait += 1
                    for op in self.ops[e]:
                        need = {}
                        for d in op.deps:
                            k = d.sem.num
                            if need.get(k, (None, 0))[1] < d.val:
                                need[k] = (d.sem, d.val)
                        for k, (sm, v) in need.items():
                            if known.get(k, 0) < v:
                                eng.wait_ge(sm, v)
                                known[k] = v
                                nwait += 1
                        ins = op.fn(eng)
                        if op.inc:
                            ins.then_inc(op.sem, 16 if op.dma else 1)
                    self.stats[e][0] += len(self.ops[e])
                    self.stats[e][1] += nwait
                return body

            block.tensor(emit("pe"))
            block.scalar(emit("act"))
            block.vector(emit("dve"))
            block.gpsimd(emit("pool"))
            block.sync(emit("sp"))
        for k, v in newcarry.items():
            self.carry[k] = v
        self.ops = {e: [] for e in ENGS}
        self.dma_hist = {e: [] for e in ENGS}
        for b in self.bufs:
            b.w = None
            b.r = {}

    def finish(self):
        self.flush()
        nc = self.nc
        carry = dict(self.carry)
        with nc.Block() as block:
            def emit(e):
                def body(eng):
                    known = self.known[e]
                    for k, (sm, v) in carry.items():
                        if known.get(k, 0) < v:
                            eng.wait_ge(sm, v)
                            known[k] = v
                return body

            block.tensor(emit("pe"))
            block.scalar(emit("act"))
            block.vector(emit("dve"))
            block.gpsimd(emit("pool"))
            block.sync(emit("sp"))

D = 1024
KD = 8
NT = 18
NCT = 2
T = NT * 128
DEPTH = 4
DFF = 2816
NF = 22
IN_COLS = 2736
EPS = 1e-6
OFF_Q, OFF_K, OFF_V, OFF_Z, OFF_A, OFF_B, OFF_CQ, OFF_CKV, OFF_KR = 0, 512, 1024, 1536, 2048, 2056, 2064, 2448, 2704


def _mm(P, out, lhsT, rhs, start, stop, r, w):
    return P.pe(lambda e: e.matmul(out, lhsT=lhsT, rhs=rhs, start=start, stop=stop), r, w)


def _tp(P, out, in_, ident, r, w):
    return P.pe(lambda e: e.transpose(out, in_, ident), r, w)


def _act(P, out, in_, func, r, w, bias=None, scale=None, accum_out=None):
    kw = {}
    if bias is not None:
        kw["bias"] = bias
    if scale is not None:
        kw["scale"] = scale
    if accum_out is not None:
        kw["accum_out"] = accum_out
    return P.act(lambda e: e.activation(out=out, in_=in_, func=func, **kw), r, w)


def _ts(P, out, in0, s1, s2, op0, op1, r, w, eng="dve", accum_out=None):
    kw = {}
    if op1 is not None:
        kw["op1"] = op1
    if accum_out is not None:
        kw["accum_out"] = accum_out
    f = lambda e: e.tensor_scalar(out=out, in0=in0, scalar1=s1, scalar2=s2, op0=op0, **kw)
    return P._add(eng, f, r, w)


def _stt(P, out, in0, scalar, in1, op0, op1, r, w):
    return P.dve(lambda e: e.scalar_tensor_tensor(out=out, in0=in0, scalar=scalar, in1=in1, op0=op0, op1=op1), r, w)


def _tt(P, out, in0, in1, op, r, w, eng="dve"):
    return P._add(eng, lambda e: e.tensor_tensor(out=out, in0=in0, in1=in1, op=op), r, w)


def _cp(P, out, in_, r, w, eng="dve"):
    if eng == "act":
        return P.act(lambda e: e.activation(out=out, in_=in_, func=AF.Copy), r, w)
    return P._add(eng, lambda e: e.tensor_copy(out=out, in_=in_), r, w)


def _rstd(P, ss, n, inv_n, eps, r, w):
    _ts(P, ss, ss, inv_n, eps, ALU.mult, ALU.add, r, w)
    _act(P, ss, ss, AF.Sqrt, r, w)
    P.dve(lambda e: e.reciprocal(out=ss, in_=ss), r, w)


class K:
    pass


def build(nlayers=DEPTH, dbg=None, stages=("ada", "ffn0", "mix", "ffn1")):
    nc = bass.Bass("TRN2", target_bir_lowering=False)
    k = K()
    k.nc = nc
    k.dbgmode = dbg

    def din(name, shape):
        return nc.dram_tensor(name, list(shape), F32, kind="ExternalInput").ap()

    k.xcat = din("xcat", [T, D])
    k.cc_fm = din("cc_fm", [128, KD * 2])
    k.w_ada = din("w_ada", [DEPTH, D, 9 * D])
    k.b_fm = din("b_fm", [128, DEPTH * 72])
    k.gpre_fm = din("gpre_fm", [128, DEPTH * 3 * KD])
    k.gpost_fm = din("gpost_fm", [128, DEPTH * 3 * KD])
    k.wg = din("ffn_w_gate", [DEPTH, 2, D, DFF])
    k.wu = din("ffn_w_up", [DEPTH, 2, D, DFF])
    k.wd = din("ffn_w_down", [DEPTH, 2, DFF, D])
    k.cst = din("cst", [128, 128 * 13])
    k.w_in = din("w_in", [DEPTH, D, IN_COLS])
    k.w_out = din("w_out", [DEPTH, D, D])
    k.gdnc_d = din("gdnc", [128, DEPTH * 2 * NT * 8])
    k.convw_d = din("convw", [128, DEPTH * 60])
    k.gnb_d = din("gnb", [128, DEPTH * 128])
    k.qn_d = din("qn_fm", [128, DEPTH * 3])
    k.kvn_d = din("kvn_fm", [128, DEPTH * 2])
    k.w_uq = din("w_uq", [DEPTH, 384, 768])
    k.w_uq_sw = din("w_uq_sw", [DEPTH, 384, 768])
    k.w_ukv_k = din("w_ukv_k", [DEPTH, 256, 512])
    k.w_ukv_v = din("w_ukv_v", [DEPTH, 256, 512])
    k.cosq = din("cosq", [128, T])
    k.sinq = din("sinq", [128, T])
    k.oscr = nc.dram_tensor("oscr", [T, D], BF16, kind="Internal").ap()
    k.hscr = nc.dram_tensor("hscr", [128, KD, T], BF16, kind="Internal").ap()
    k.wscr = [nc.dram_tensor("wscr%d" % i, [NF // 2, 128, KD * 256], BF16, kind="Internal").ap() for i in range(2)]
    k.out = nc.dram_tensor("out", [2048, D], F32, kind="ExternalOutput").ap()
    if dbg:
        k.dbg = nc.dram_tensor("dbg", [T, D], F32, kind="ExternalOutput").ap()
        k.dbg2 = nc.dram_tensor("dbg2", [T, D], BF16, kind="ExternalOutput").ap()
        k.dbg3 = nc.dram_tensor("dbg3", [3, 128, T], BF16, kind="ExternalOutput").ap()
        k.dbg4 = nc.dram_tensor("dbg4", [128, NT * 128], F32, kind="ExternalOutput").ap()
        k.dbg5 = nc.dram_tensor("dbg5", [128, NT * 48], F32, kind="ExternalOutput").ap()

    with ExitStack() as st:
        P = Prog(nc, st)
        k.P = P
        k.st = st

        k.uid = 0

        def S(name, shape, dt, stack=st):
            k.uid += 1
            return stack.enter_context(nc.sbuf_tensor("%s_%d" % (name, k.uid), list(shape), dt))

        k.S = S
        k.ps_tp = st.enter_context(nc.psum_tensor("ps_tp", [128, 1024], BF16))
        k.b_tp = P.buf("ps_tp", excl=True)
        k.ps = [st.enter_context(nc.psum_tensor("ps%d" % i, [128, 512], F32)) for i in range(7)]
        k.b_ps = [P.buf("ps%d" % i, excl=True) for i in range(7)]

        k.x = S("x_res", [128, NT, D], F32)
        k.b_x = P.bufs_n("x", NT)
        k.cstt = S("cstt", [128, 13, 128], F32)
        k.ones_b = S("ones_b", [128, 128], BF16)
        k.convw = S("convw_s", [128, DEPTH * 60], F32)
        k.gnb = S("gnb_s", [128, DEPTH * 128], F32)
        k.qn = S("qn", [128, DEPTH * 3], F32)
        k.kvn = S("kvn", [128, DEPTH * 2], F32)
        k.b_oscr = P.buf("oscr")
        k.ident_b = S("ident_b", [128, 128], BF16)
        k.ones_f = k.cstt[:, 1, :]
        k.ident_f = k.cstt[:, 0, :]
        k.b_cst = P.buf("cst")
        k.s_fm = S("s_fm", [128, KD, 2], BF16)
        k.ccf = S("ccf", [128, KD * 2], F32)
        k.bfm = S("bfm", [128, DEPTH * 72], F32)
        k.gpre = S("gpre", [128, DEPTH * 3 * KD], F32)
        k.gpost = S("gpost", [128, DEPTH * 3 * KD], F32)
        k.b_small = P.buf("small")
        k.m_fm = S("m_fm", [128, 72, 2], F32)
        k.sc1p = S("sc1p", [128, 3, 2, KD], F32)
        k.gatev = S("gatev", [128, 3, 2, KD], F32)
        k.G = S("Gb", [128, 6, D], BF16)
        k.b_mod = P.buf("mod")
        k.b_G = P.buf("G")

        for t in range(NT):
            P.dma("sp", k.x[:, t, :], k.xcat[t * 128:(t + 1) * 128, :], w=[k.b_x[t]])
        P.dma("sp", k.cstt[:], k.cst.rearrange("p (a b) -> p a b", a=13), w=[k.b_cst])
        P.dma("sp", k.convw[:], k.convw_d, w=[k.b_small])
        P.dma("sp", k.gnb[:], k.gnb_d, w=[k.b_small])
        P.dma("sp", k.qn[:], k.qn_d, w=[k.b_small])
        P.dma("sp", k.kvn[:], k.kvn_d, w=[k.b_small])
        P.dma("sp", k.ccf[:], k.cc_fm, w=[k.b_small])
        P.dma("sp", k.bfm[:], k.b_fm, w=[k.b_small])
        P.dma("sp", k.gpre[:], k.gpre_fm, w=[k.b_small])
        P.dma("sp", k.gpost[:], k.gpost_fm, w=[k.b_small])
        _cp(P, k.ident_b[:], k.ident_f, [k.b_cst], [k.b_cst])
        _cp(P, k.ones_b[:], k.ones_f, [k.b_cst], [k.b_cst])
        _act(P, k.s_fm[:].rearrange("p a b -> p (a b)"), k.ccf[:], AF.Silu, [k.b_small], [k.b_small])
        P.flush()

        for l in range(nlayers):
            last = l == DEPTH - 1
            if "ada" in stages:
                ada_phase(k, l)
            if "ffn0" in stages:
                ffn_phase(k, l, 0, 0, list(range(NT)))
            if "mix" in stages:
                mix_phase(k, l)
            if "ffn1" in stages:
                ffn_phase(k, l, 1, 2, list(range(NCT, NT)) if last else list(range(NT)))

        for t in range(NCT, NT):
            P.dma("sp", k.out[(t - NCT) * 128:(t - NCT + 1) * 128, :], k.x[:, t, :], r=[k.b_x[t]])
        if dbg:
            for t in range(NT):
                P.dma("sp", k.dbg[t * 128:(t + 1) * 128, :], k.x[:, t, :], r=[k.b_x[t]])
            P.dma("sp", k.dbg2, k.oscr, r=[k.b_oscr])
        P.finish()
        k.stats = P.stats
    return nc, k


def ada_phase(k, l):
    nc, P = k.nc, k.P
    with ExitStack() as ph:
        wa = [k.S("wa%d" % i, [128, 9 * D], BF16, ph) for i in range(2)]
        b_wa = P.bufs_n("wa", 2)
        diag = [k.S("diag%d" % i, [128, 128], F32, ph) for i in range(2)]
        b_diag = P.bufs_n("diag", 2)
        psm = k.ps[6]
        b_psm = k.b_ps[6]
        for kk in range(KD):
            s = kk % 2
            P.dma("pool", wa[s][:], k.w_ada[l, kk * 128:(kk + 1) * 128, :], w=[b_wa[s]])
            for j in range(72):
                P.pe(lambda e, o=psm[:, j * 2:(j + 1) * 2], a=wa[s][:, j * 128:(j + 1) * 128], r_=k.s_fm[:, kk, :],
                     st=(kk == 0 and j == 0), sp=(kk == KD - 1):
                     e.matmul(o, lhsT=a, rhs=r_, start=st, stop=sp, skip_group_check=True), [b_wa[s], k.b_small], [b_psm])
        psm3 = psm[:, 0:144].rearrange("p (j w) -> p j w", w=2)
        for w in range(2):
            _tt(P, k.m_fm[:, :, w], psm3[:, :, w], k.bfm[:, l * 72:(l + 1) * 72], ALU.add, [b_psm, k.b_small], [k.b_mod])
        for s in range(3):
            wt = 1.0 if s == 1 else 0.5
            for w in range(2):
                gp = k.gpre[:, (l * 3 + s) * KD:(l * 3 + s + 1) * KD]
                go = k.gpost[:, (l * 3 + s) * KD:(l * 3 + s + 1) * KD]
                _stt(P, k.sc1p[:, s, w, :], k.m_fm[:, (3 * s + 1) * KD:(3 * s + 2) * KD, w], 1.0, gp, ALU.add, ALU.mult,
                     [k.b_mod, k.b_small], [k.b_mod])
                _stt(P, k.gatev[:, s, w, :], k.m_fm[:, (3 * s + 2) * KD:(3 * s + 3) * KD, w], wt, go, ALU.mult, ALU.mult,
                     [k.b_mod, k.b_small], [k.b_mod])
        n = 0
        for s in range(3):
            for w in range(2):
                for half in range(2):
                    pb = k.ps[4 + half]
                    bpb = k.b_ps[4 + half]
                    for q in range(4):
                        kk = half * 4 + q
                        dg = diag[n % 2]
                        _ts(P, dg[:], k.ident_f, k.gatev[:, s, w, kk:kk + 1], None, ALU.mult, None,
                            [k.b_mod, k.b_cst], [b_diag[n % 2]])
                        _mm(P, pb[:, q * 128:(q + 1) * 128], k.ones_f, dg[:], True, True, [b_diag[n % 2], k.b_cst], [bpb])
                        n += 1
                    _act(P, k.G[:, s * 2 + w, half * 512:(half + 1) * 512], pb[:], AF.Copy, [bpb], [k.b_G])
        P.flush()


def premod_tile(k, t, slot, hT_dst, b_hT, xn, b_xn, ss, b_ss, junk, b_junk):
    P = k.P
    w = 1 if t < NCT else 0
    xt = k.x[:, t, :]
    _act(P, junk, xt, AF.Square, [k.b_x[t]], [b_junk, b_ss], accum_out=ss)
    _rstd(P, ss, 1, 1.0 / D, EPS, [b_ss], [b_ss])
    _ts(P, xn, xt, ss, None, ALU.mult, None, [k.b_x[t], b_ss], [b_xn])
    for kk in range(KD):
        _tp(P, k.ps_tp[:, kk * 128:(kk + 1) * 128], xn[:, kk * 128:(kk + 1) * 128], k.ident_b[:], [b_xn, k.b_cst], [k.b_tp])
    for kk in range(KD):
        _act(P, hT_dst[:, kk, :], k.ps_tp[:, kk * 128:(kk + 1) * 128], AF.Identity, [k.b_tp, k.b_mod], [b_hT],
             bias=k.m_fm[:, 3 * slot * KD + kk, w:w + 1], scale=k.sc1p[:, slot, w, kk:kk + 1])


def post_tile(k, t, slot, ysb, b_ysb, ss2, b_ss2):
    P = k.P
    w = 1 if t < NCT else 0
    _tt(P, ss2[:, 0:1], ss2[:, 0:1], ss2[:, 1:2], ALU.add, [b_ss2], [b_ss2])
    _rstd(P, ss2[:, 0:1], 1, 1.0 / D, EPS, [b_ss2], [b_ss2])
    _stt(P, ysb, ysb, ss2[:, 0:1], k.G[:, slot * 2 + w, :], ALU.mult, ALU.mult, [b_ysb, b_ss2, k.b_G], [b_ysb])
    _tt(P, k.x[:, t, :], k.x[:, t, :], ysb, ALU.add, [k.b_x[t], b_ysb], [k.b_x[t]])


def ffn_phase(k, l, f, slot, tiles):
    nc, P = k.nc, k.P
    FB = 2
    nfb = NF // FB
    with ExitStack() as ph:
        S = lambda n, sh, dt: k.S(n, sh, dt, ph)
        wd = S("wd", [128, NF, D], BF16)
        b_wd = P.bufs_n("wd", NF // 2)
        wgs = [S("wg%d" % i, [128, KD, FB * 128], BF16) for i in range(2)]
        wus = [S("wu%d" % i, [128, KD, FB * 128], BF16) for i in range(2)]
        b_wg = P.bufs_n("wg", 2)
        b_wu = P.bufs_n("wu", 2)
        hTs = [S("hT%d" % i, [128, KD, 512], BF16) for i in range(2)]
        b_hTs = [P.bufs_n("hT%d_" % i, 4) for i in range(2)]
        aT = S("aT", [128, NF, 512], BF16)
        b_aT = P.bufs_n("aT", NF)
        xn = [S("xn%d" % i, [128, D], BF16) for i in range(2)]
        b_xn = P.bufs_n("xn", 2)
        junk = S("junk", [128, D], BF16)
        b_junk = P.buf("junk")
        ss = [S("ss%d" % i, [128, 1], F32) for i in range(2)]
        b_ss = P.bufs_n("ss", 2)
        sg = [S("sg%d" % i, [128, 512], BF16) for i in range(2)]
        b_sg = P.bufs_n("sg", 2)
        ysb = [S("ysb%d" % i, [128, D], F32) for i in range(1)] * 2
        b_ysb = P.bufs_n("ysb", 1) * 2
        ss2 = [S("ss2%d" % i, [128, 2], F32) for i in range(2)]
        b_ss2 = P.bufs_n("ss2", 2)

        wd_src = k.wd[l, f]

        def load_wd(c):
            P.dma("pool", wd[:, 2 * c:2 * c + 2, :], wd_src[c * 256:(c + 1) * 256, :].rearrange("(c p) n -> p c n", p=128),
                  w=[b_wd[c]])

        blocks = [tiles[i:i + 4] for i in range(0, len(tiles), 4)]
        nblk = len(blocks)

        b_wscr = [P.bufs_n("wscr%d_" % i, nfb) for i in range(2)]

        def load_w(idx):
            fb = idx % nfb
            s = idx % 2
            if idx < nfb:
                P.dma("pool", wgs[s][:], k.wg[l, f, :, fb * 256:(fb + 1) * 256].rearrange("(k p) n -> p k n", p=128), w=[b_wg[s]])
                P.dma("pool", wus[s][:], k.wu[l, f, :, fb * 256:(fb + 1) * 256].rearrange("(k p) n -> p k n", p=128), w=[b_wu[s]])
                P.dma("sp", k.wscr[0][fb], wgs[s][:].rearrange("p a b -> p (a b)"), r=[b_wg[s]], w=[b_wscr[0][fb]])
                P.dma("sp", k.wscr[1][fb], wus[s][:].rearrange("p a b -> p (a b)"), r=[b_wu[s]], w=[b_wscr[1][fb]])
            else:
                P.dma("sp", wgs[s][:].rearrange("p a b -> p (a b)"), k.wscr[0][fb], r=[b_wscr[0][fb]], w=[b_wg[s]])
                P.dma("sp", wus[s][:].rearrange("p a b -> p (a b)"), k.wscr[1][fb], r=[b_wscr[1][fb]], w=[b_wu[s]])

        total = nblk * nfb
        load_w(0)
        it = 0
        nt_state = [0]

        def premod_blk_tile(bi, j):
            t = blocks[bi][j]
            n_ = nt_state[0]
            premod_tile(k, t, slot, hTs[bi % 2][:, :, j * 128:(j + 1) * 128], b_hTs[bi % 2][j], xn[n_ % 2][:], b_xn[n_ % 2],
                        ss[n_ % 2][:], b_ss[n_ % 2], junk[:], b_junk)
            nt_state[0] += 1

        for j in range(len(blocks[0])):
            premod_blk_tile(0, j)
        for bi, blk in enumerate(blocks):
            ntok = len(blk) * 128
            hT = hTs[bi % 2]
            hbufs = [b_hTs[bi % 2][j] for j in range(len(blk))]
            for fb in range(nfb):
                if it + 1 < total:
                    load_w(it + 1)
                if bi == 0:
                    load_wd(fb)
                s = it % 2
                for ci in range(FB):
                    c = fb * FB + ci
                    pset = (it * FB + ci) % 2
                    pg, pu = k.ps[pset * 2], k.ps[pset * 2 + 1]
                    bpg, bpu = k.b_ps[pset * 2], k.b_ps[pset * 2 + 1]
                    for kk in range(KD):
                        _mm(P, pg[:, 0:ntok], wgs[s][:, kk, ci * 128:(ci + 1) * 128], hT[:, kk, 0:ntok], kk == 0, kk == KD - 1,
                            [b_wg[s]] + hbufs, [bpg])
                    for kk in range(KD):
                        _mm(P, pu[:, 0:ntok], wus[s][:, kk, ci * 128:(ci + 1) * 128], hT[:, kk, 0:ntok], kk == 0, kk == KD - 1,
                            [b_wu[s]] + hbufs, [bpu])
                    _act(P, sg[pset][:, 0:ntok], pg[:, 0:ntok], AF.Silu, [bpg], [b_sg[pset]])
                    _tt(P, aT[:, c, 0:ntok], sg[pset][:, 0:ntok], pu[:, 0:ntok], ALU.mult, [b_sg[pset], bpu], [b_aT[c]])
                it += 1
                if bi + 1 < nblk and fb % 2 == 1 and fb // 2 < len(blocks[bi + 1]):
                    premod_blk_tile(bi + 1, fb // 2)
            for j, t in enumerate(blk):
                yi = t % 2
                for half in range(2):
                    py = k.ps[4 + half]
                    bpy = k.b_ps[4 + half]
                    for c in range(NF):
                        _mm(P, py[:], aT[:, c, j * 128:(j + 1) * 128], wd[:, c, half * 512:(half + 1) * 512], c == 0, c == NF - 1,
                            [b_aT[c], b_wd[c // 2]], [bpy])
                    _cp(P, ysb[yi][:, half * 512:(half + 1) * 512], py[:], [bpy], [b_ysb[yi]])
                    _act(P, junk[:, 0:512], ysb[yi][:, half * 512:(half + 1) * 512], AF.Square, [b_ysb[yi]], [b_junk, b_ss2[yi]],
                         accum_out=ss2[yi][:, half:half + 1])
                post_tile(k, t, slot, ysb[yi][:], b_ysb[yi], ss2[yi], b_ss2[yi])
        P.flush()


TBLK = [(0, 256), (256, 512), (768, 512), (1280, 512), (1792, 512)]
C_ID, C_ONE, C_LE, C_GE, C_LT, C_GT, C_MK, C_N = 0, 1, 2, 3, 4, 5, 6, 13


def _rawcol(tok):
    return tok + 2 if tok < 256 else tok + 6


def _memset(P, ap, val, w, eng="dve"):
    return P._add(eng, lambda e: e.memset(ap, val), (), w)


def mix_phase(k, l):
    nc, P = k.nc, k.P
    last = l == DEPTH - 1
    with ExitStack() as mx:
        S = lambda n, sh, dt, stack=mx: k.S(n, sh, dt, stack)
        m = K()
        m.gb = S("gb", [128, NT, 16], F32)
        m.nb = S("nbeta", [128, NT, 8], F32)
        m.EX = S("EX", [128, NT, 24], F32)
        m.b_gb = P.buf("gb")
        m.b_hscr = P.bufs_n("hscr", len(TBLK))
        with ExitStack() as ph:
            Sp = lambda n, sh, dt: k.S(n, sh, dt, ph)
            xn = [Sp("xn%d" % i, [128, D], BF16) for i in range(2)]
            b_xn = P.bufs_n("xn", 2)
            junk = Sp("junk", [128, D], BF16)
            b_junk = P.buf("junk")
            ss = [Sp("ss%d" % i, [128, 1], F32) for i in range(2)]
            b_ss = P.bufs_n("ss", 2)
            hblk = [Sp("hblkA%d" % i, [128, KD, 512], BF16) for i in range(2)]
            b_hblk = [P.bufs_n("hblkA%d_" % i, 4) for i in range(2)]
            wab = Sp("wab", [128, KD, 16], BF16)
            b_wab = P.buf("wab")
            wk1 = Sp("wk1", [128, NT, 8], F32)
            wk2 = Sp("wk2", [128, NT, 8], F32)
            b_wk = P.buf("wk")
            P.dma("pool", wab[:], k.w_in[l, :, OFF_A:OFF_A + 16].rearrange("(k p) n -> p k n", p=128), w=[b_wab])
            pab = k.ps[6]
            bpab = k.b_ps[6]
            for bi, (t0, nt) in enumerate(TBLK):
                s = bi % 2
                for j in range(nt // 128):
                    t = t0 // 128 + j
                    premod_tile(k, t, 1, hblk[s][:, :, j * 128:(j + 1) * 128], b_hblk[s][j], xn[t % 2][:], b_xn[t % 2],
                                ss[t % 2][:], b_ss[t % 2], junk[:], b_junk)
                    for kk in range(KD):
                        _mm(P, pab[:, t * 16:(t + 1) * 16], hblk[s][:, kk, j * 128:(j + 1) * 128], wab[:, kk, :], kk == 0, kk == KD - 1,
                            [b_hblk[s][j], b_wab], [bpab])
                P.dma("sp", k.hscr[:, :, t0:t0 + nt], hblk[s][:, :, 0:nt], r=b_hblk[s][0:nt // 128], w=[m.b_hscr[bi]])
            gbf = m.gb[:].rearrange("p a b -> p (a b)")
            _cp(P, gbf, pab[:, 0:NT * 16], [bpab], [m.b_gb])
            a3 = m.gb[:, :, 0:8]
            b3 = m.gb[:, :, 8:16]
            gd = Sp("gd", [128, 2, NT * 8], F32)
            P.dma("sp", gd[:].rearrange("p a b -> p (a b)"), k.gdnc_d[:, l * 2 * NT * 8:(l + 1) * 2 * NT * 8], w=[b_wk])
            dtb3 = gd[:, 0, :].rearrange("p (a b) -> p a b", b=8)
            alg3 = gd[:, 1, :].rearrange("p (a b) -> p a b", b=8)
            rw = [m.b_gb, b_wk, k.b_small]
            _tt(P, wk1[:], a3, dtb3, ALU.add, rw, [b_wk])
            _stt(P, wk2[:], wk1[:], -1.0, wk1[:], ALU.mult, ALU.max, rw, [b_wk])
            _act(P, wk2[:], wk2[:], AF.Exp, rw, [b_wk], scale=-1.0)
            _act(P, wk2[:], wk2[:], AF.Ln, rw, [b_wk], bias=1.0)
            _ts(P, wk1[:], wk1[:], 0.0, None, ALU.max, None, rw, [b_wk])
            _tt(P, wk1[:], wk1[:], wk2[:], ALU.add, rw, [b_wk])
            _act(P, wk2[:], alg3, AF.Exp, rw, [b_wk])
            _stt(P, a3, wk1[:], -1.0, wk2[:], ALU.mult, ALU.mult, rw, [m.b_gb])
            _act(P, b3, b3, AF.Sigmoid, rw, [m.b_gb])
            _ts(P, m.nb[:], b3, -1.0, None, ALU.mult, None, rw, [m.b_gb])
            pex = k.ps[5]
            bpex = k.b_ps[5]
            cs = k.cstt
            for t in range(NT):
                base = t * 24
                gf = m.gb[:, t, 0:4]
                gbw = m.gb[:, t, 4:8]
                rr = [m.b_gb, k.b_cst]
                _mm(P, pex[:, base + 0:base + 4], cs[:, C_LE, :], gf, True, True, rr, [bpex])
                _mm(P, pex[:, base + 4:base + 8], cs[:, C_GE, :], gbw, True, True, rr, [bpex])
                _mm(P, pex[:, base + 8:base + 12], cs[:, C_GT, :], gf, True, True, rr, [bpex])
                _mm(P, pex[:, base + 12:base + 16], cs[:, C_LT, :], gbw, True, True, rr, [bpex])
                _mm(P, pex[:, base + 16:base + 24], cs[:, C_ONE, :], m.gb[:, t, 0:8], True, True, rr, [bpex])
            _act(P, m.EX[:].rearrange("p a b -> p (a b)"), pex[:, 0:NT * 24], AF.Exp, [bpex], [m.b_gb])
            P.flush()
        for h in range(4):
            gdn_head(k, m, l, h)
        mla_group(k, m, l)
    out_proj(k, l)


def gdn_head(k, m, l, h):
    nc, P = k.nc, k.P
    cs = k.cstt
    with ExitStack() as hd:
        S = lambda n, sh, dt, stack=hd: k.S(n, sh, dt, stack)
        cT = [S("cT%d" % i, [128, T], BF16) for i in range(3)]
        b_cT = [P.bufs_n("cT%d_" % i, len(TBLK)) for i in range(3)]
        kn_tok = S("kn_tok", [128, NT, 128], BF16)
        v_tok = S("v_tok", [128, NT, 128], BF16)
        b_tok = P.buf("tok")
        zs = S("zs", [128, NT, 128], BF16)
        b_zs = P.buf("zs")
        o_acc = S("o_acc", [128, NT, 128], F32)
        b_o = P.bufs_n("oacc", NT)
        with ExitStack() as g1:
            Sp = lambda n, sh, dt: k.S(n, sh, dt, g1)
            raw = [Sp("raw%d" % i, [128, T + 8], BF16) for i in range(3)]
            b_raw = P.bufs_n("raw", 3)
            acc = Sp("acc", [128, T], F32)
            b_acc = P.buf("acc")
            wq = [Sp("wq%d" % i, [128, KD, 128], BF16) for i in range(4)]
            b_wq = P.bufs_n("wq", 4)
            sqall = Sp("sqall", [128, T], BF16)
            b_sqall = P.buf("sqall")
            rsall = Sp("rsall", [128, T], F32)
            b_rsall = P.buf("rsall")
            for i, c in enumerate((h, 4 + h, 8 + h, 12 + h)):
                P.dma("pool", wq[i][:], k.w_in[l, :, c * 128:(c + 1) * 128].rearrange("(k p) n -> p k n", p=128), w=[b_wq[i]])
            for i in range(3):
                _memset(P, raw[i][:], 0.0, [b_raw[i]])
            hblk = [Sp("hblkG%d" % i, [128, KD, 512], BF16) for i in range(2)]
            b_hblk = P.bufs_n("hblkG", 2)
            for bi, (t0, nt) in enumerate(TBLK):
                s = bi % 2
                P.dma("sp", hblk[s][:, :, 0:nt], k.hscr[:, :, t0:t0 + nt], r=[m.b_hscr[bi]], w=[b_hblk[s]])
                for i in range(3):
                    pp = k.ps[i]
                    bpp = k.b_ps[i]
                    for kk in range(KD):
                        _mm(P, pp[:, 0:nt], wq[i][:, kk, :], hblk[s][:, kk, 0:nt], kk == 0, kk == KD - 1, [b_wq[i], b_hblk[s]], [bpp])
                    rc = _rawcol(t0)
                    _act(P, raw[i][:, rc:rc + nt], pp[:, 0:nt], AF.Copy, [bpp], [b_raw[i]])
                pp = k.ps[3]
                bpp = k.b_ps[3]
                ntl = nt // 128
                for q in range(ntl):
                    for kk in range(KD):
                        _mm(P, pp[:, q * 128:(q + 1) * 128], hblk[s][:, kk, q * 128:(q + 1) * 128], wq[3][:, kk, :], kk == 0, kk == KD - 1,
                            [b_hblk[s], b_wq[3]], [bpp])
                tt0 = t0 // 128
                _act(P, zs[:, tt0:tt0 + ntl, :].rearrange("p a b -> p (a b)"), pp[:, 0:ntl * 128], AF.Silu, [bpp], [b_zs])
            n = 0
            for i in range(3):
                c = (h, 4 + h, 8 + h)[i]
                for (s0, s1, base) in ((0, 256, 0), (256, T, 260)):
                    ln = s1 - s0
                    for j in range(5):
                        cw = k.convw[:, (l * 12 + c) * 5 + j:(l * 12 + c) * 5 + j + 1]
                        src = raw[i][:, base + j:base + j + ln]
                        if j == 0:
                            _ts(P, acc[:, s0:s1], src, cw, None, ALU.mult, None, [b_raw[i], k.b_small], [b_acc])
                        else:
                            _stt(P, acc[:, s0:s1], src, cw, acc[:, s0:s1], ALU.mult, ALU.add, [b_raw[i], k.b_small, b_acc], [b_acc])
                if i == 2:
                    _act(P, cT[2][:], acc[:], AF.Silu, [b_acc], b_cT[2])
                else:
                    _act(P, acc[:], acc[:], AF.Silu, [b_acc], [b_acc])
                    scale = (128.0 ** -0.5) if i == 0 else 1.0
                    _act(P, sqall[:], acc[:], AF.Square, [b_acc], [b_sqall])
                    for bi, (t0, nt) in enumerate(TBLK):
                        pp = k.ps[2 + bi]
                        bpp = k.b_ps[2 + bi]
                        _mm(P, pp[:, 0:nt], k.ones_b[:], sqall[:, t0:t0 + nt], True, True, [b_sqall, k.b_cst], [bpp])
                        _ts(P, rsall[:, t0:t0 + nt], pp[:, 0:nt], EPS, None, ALU.add, None, [bpp], [b_rsall])
                    _act(P, rsall[:], rsall[:], AF.Sqrt, [b_rsall], [b_rsall])
                    P.dve(lambda e: e.reciprocal(out=rsall[:], in_=rsall[:]), [b_rsall], [b_rsall])
                    _stt(P, cT[i][:], acc[:], scale, rsall[:], ALU.mult, ALU.mult, [b_acc, b_rsall], b_cT[i])
            for (src_i, dst) in ((1, kn_tok), (2, v_tok)):
                for t0 in range(0, NT, 8):
                    ntl = min(8, NT - t0)
                    for q in range(ntl):
                        t = t0 + q
                        _tp(P, k.ps_tp[:, q * 128:(q + 1) * 128], cT[src_i][:, t * 128:(t + 1) * 128], k.ident_b[:],
                            b_cT[src_i] + [k.b_cst], [k.b_tp])
                    _cp(P, dst[:, t0:t0 + ntl, :].rearrange("p a b -> p (a b)"), k.ps_tp[:, 0:ntl * 128], [k.b_tp], [b_tok])
            for t in range(NT):
                _memset(P, o_acc[:, t, :], 0.0, [b_o[t]])
            import os
            if k.dbgmode and h == int(os.environ.get("DBG_HEAD", "0")):
                for i in range(3):
                    P.dma("sp", k.dbg3[i], cT[i][:], r=b_cT[i])
                P.dma("sp", k.dbg5[:, 0:NT * 16], m.gb[:].rearrange("p a b -> p (a b)"), r=[m.b_gb])
                P.dma("sp", k.dbg5[:, NT * 16:NT * 40], m.EX[:].rearrange("p a b -> p (a b)"), r=[m.b_gb])
            P.flush()
        with ExitStack() as g2:
            Sp = lambda n, sh, dt: k.S(n, sh, dt, g2)
            NJ = 3
            NS = 4
            lhsE = [[Sp("lhsE%d_%d" % (d, j), [128, 128], F32) for j in range(NJ)] for d in range(2)]
            DMi = [Sp("DMi%d" % j, [128, 256], F32) for j in range(NJ)]
            DMs = [Sp("DMs%d" % j, [128, 256], F32) for j in range(NJ)]
            MM_ = [Sp("MM%d" % j, [128, 512], F32) for j in range(NJ)]
            BB_ = [[Sp("BB%d_%d" % (j, i), [128, 512], F32) for i in range(2)] for j in range(NJ)]
            XX_ = [Sp("XX%d" % j, [128, 512], F32) for j in range(NJ)]
            TT_ = [Sp("TT%d" % j, [128, 512], F32) for j in range(NJ)]
            b_MM_ = [P.buf("MM%d" % j) for j in range(NJ)]
            b_BB_ = [[P.buf("BB%d_%d" % (j, i)) for i in range(2)] for j in range(NJ)]
            tp32 = k.ps_tp[:].bitcast(F32)
            JB = [((k.ps[0], k.b_ps[0]), (k.ps[2], k.b_ps[2])), ((k.ps[3], k.b_ps[3]), (k.ps[4], k.b_ps[4])),
                  ((k.ps[5], k.b_ps[5]), (tp32, k.b_tp))]
            R = [[Sp("R%d_%d" % (d, j), [128, 256], F32) for j in range(NJ)] for d in range(2)]
            wtok = [[Sp("wtok%d_%d" % (d, j), [128, 128], F32) for j in range(NJ)] for d in range(2)]
            bj = lambda nm: [P.buf("%s%d" % (nm, j)) for j in range(NJ)]
            b_XX_, b_TT_, b_DM = bj("XX"), bj("TT"), bj("DM")
            Bj = lambda nm: [[P.buf("%s%d_%d" % (nm, d, j)) for j in range(NJ)] for d in range(2)]
            b_lhsE, b_R, b_wtok = Bj("lhsE"), Bj("R"), Bj("wtok")
            QK = [[Sp("QK%d_%d" % (d, s), [128, 128], BF16) for s in range(NS)] for d in range(2)]
            kdec = [[Sp("kdec%d_%d" % (d, s), [128, 128], F32) for s in range(NS)] for d in range(2)]
            u0b = [[Sp("u0b%d_%d" % (d, s), [128, 128], F32) for s in range(NS)] for d in range(2)]
            nwT = [[Sp("nwT%d_%d" % (d, s), [128, 128], F32) for s in range(NS)] for d in range(2)]
            B = lambda nm: [[P.buf("%s%d_%d" % (nm, d, s)) for s in range(NS)] for d in range(2)]
            b_QK, b_kdec, b_u0b, b_nwT = B("QK"), B("kdec"), B("u0b"), B("nwT")
            Sst = [Sp("Sst%d" % d, [128, 128], F32) for d in range(2)]
            Sbf = [Sp("Sbf%d" % d, [128, 128], BF16) for d in range(2)]
            Usb = [Sp("Usb%d" % d, [128, 128], BF16) for d in range(2)]
            Usf = [Sp("Usf%d" % d, [128, 128], F32) for d in range(2)]
            b_Uf = P.bufs_n("Usf", 2)
            b_S = P.bufs_n("Sst", 2)
            b_Sbf = P.bufs_n("Sbf", 2)
            b_U = P.bufs_n("Usb", 2)
            for d in range(2):
                _memset(P, Sst[d][:], 0.0, [b_S[d]])
                _memset(P, Sbf[d][:], 0.0, [b_Sbf[d]])
            order = [list(range(NT)), [1, 0] + list(range(NT - 1, 1, -1))]
            cb = lambda c: [b_cT[0][_blk_of(c)], b_cT[1][_blk_of(c)]]

            def pre(step, j):
                s = step % NS
                cc = [order[0][step], order[1][step]]
                MM, XX, TT = MM_[j], XX_[j], TT_[j]
                b_MM, b_XX, b_TT = b_MM_[j], b_XX_[j], b_TT_[j]
                BB, b_BB = BB_[j], b_BB_[j]
                pE, bpE = JB[j][0]
                pA, bpA = JB[j][1]
                pB, bpB = pE, bpE
                MM3 = MM[:].rearrange("p (a b) -> p a b", a=4)

                def mask_level(lv):
                    mk = cs[:, C_MK + lv, :].unsqueeze(1).to_broadcast([128, 4, 128])
                    _tt(P, BB[lv % 2][:].rearrange("p (a b) -> p a b", a=4), MM3, mk, ALU.mult, [b_MM, k.b_cst], [b_BB[lv % 2]], eng="pool")

                for d in range(2):
                    c = cc[d]
                    gcol = m.gb[:, c, d * 4 + h:d * 4 + h + 1]
                    msk = cs[:, C_GT, :] if d == 0 else cs[:, C_LT, :]
                    _ts(P, lhsE[d][j][:], msk, gcol, None, ALU.mult, None, [m.b_gb, k.b_cst], [b_lhsE[d][j]])
                yield
                for d in range(2):
                    c = cc[d]
                    rhs = cs[:, C_LE, :] if d == 0 else cs[:, C_GE, :]
                    _mm(P, pE[:, d * 128:(d + 1) * 128], lhsE[d][j][:], rhs, True, True, [b_lhsE[d][j], k.b_cst], [bpE])
                    knc = cT[1][:, c * 128:(c + 1) * 128]
                    _mm(P, pE[:, 256 + d * 128:256 + (d + 1) * 128], knc, knc, True, True, cb(c), [bpE])
                yield
                _act(P, DMi[j][:], pE[:, 0:256], AF.Exp, [bpE], [b_DM[j]])
                yield
                inc2 = cs[:, C_LE:C_GE + 1, :].rearrange("p a b -> p (a b)")
                str2 = cs[:, C_LT:C_GT + 1, :].rearrange("p a b -> p (a b)")
                _tt(P, DMs[j][:], DMi[j][:], str2, ALU.mult, [b_DM[j], k.b_cst], [b_DM[j]])
                _tt(P, DMi[j][:], DMi[j][:], inc2, ALU.mult, [b_DM[j], k.b_cst], [b_DM[j]])
                for d in range(2):
                    c = cc[d]
                    bcol = m.gb[:, c, 8 + d * 4 + h:8 + d * 4 + h + 1]
                    _stt(P, MM[:, d * 128:(d + 1) * 128], pE[:, 256 + d * 128:256 + (d + 1) * 128], bcol, DMs[j][:, d * 128:(d + 1) * 128],
                         ALU.mult, ALU.mult, [bpE, m.b_gb, b_DM[j]], [b_MM])
                yield
                for d in range(2):
                    c = cc[d]
                    knc = cT[1][:, c * 128:(c + 1) * 128]
                    qnc = cT[0][:, c * 128:(c + 1) * 128]
                    _mm(P, pE[:, d * 128:(d + 1) * 128], knc, qnc, True, True, cb(c), [bpE])
                for d in range(2):
                    _tp(P, pA[:, d * 128:(d + 1) * 128], MM[:, d * 128:(d + 1) * 128], k.ident_f, [b_MM, k.b_cst], [bpA])
                yield
                for d in range(2):
                    _tt(P, QK[d][s][:], pE[:, d * 128:(d + 1) * 128], DMi[j][:, d * 128:(d + 1) * 128], ALU.mult, [bpE, b_DM[j]],
                        [b_QK[d][s]])
                _cp(P, MM[:, 256:512], pA[:, 0:256], [bpA], [b_MM], eng="act")
                yield
                mask_level(0)
                for d in range(2):
                    c = cc[d]
                    eG = m.EX[:, c, d * 4 + h:d * 4 + h + 1]
                    ekd = m.EX[:, c, 8 + d * 4 + h:8 + d * 4 + h + 1]
                    _act(P, R[d][j][:, 128:256], kn_tok[:, c, :], AF.Copy, [b_tok, m.b_gb], [b_R[d][j]], scale=eG)
                    _act(P, kdec[d][s][:], kn_tok[:, c, :], AF.Copy, [b_tok, m.b_gb], [b_kdec[d][s]], scale=ekd)
                yield
                id4 = cs[:, C_ID, :].unsqueeze(1).to_broadcast([128, 4, 128])
                _tt(P, XX[:].rearrange("p (a b) -> p a b", a=4), id4, BB[0][:].rearrange("p (a b) -> p a b", a=4), ALU.subtract,
                    [b_BB[0], k.b_cst], [b_XX])
                mask_level(1)
                for d in range(2):
                    c = cc[d]
                    _cp(P, R[d][j][:, 0:128], v_tok[:, c, :], [b_tok], [b_R[d][j]], eng="pool")
                yield
                for lv in range(1, 7):
                    lastlv = lv == 6
                    Bl = BB[lv % 2]
                    bBl = b_BB[lv % 2]
                    for d in range(2):
                        _mm(P, pA[:, d * 128:(d + 1) * 128], Bl[:, (2 + d) * 128:(3 + d) * 128], XX[:, d * 128:(d + 1) * 128], True, True,
                            [bBl, b_XX], [bpA])
                    if not lastlv:
                        for d in range(2):
                            _mm(P, pA[:, (2 + d) * 128:(3 + d) * 128], Bl[:, d * 128:(d + 1) * 128], XX[:, (2 + d) * 128:(3 + d) * 128],
                                True, True, [bBl, b_XX], [bpA])
                    yield
                    w_ = 256 if lastlv else 512
                    _cp(P, TT[:, 0:w_], pA[:, 0:w_], [bpA], [b_TT], eng="act")
                    if not lastlv:
                        mask_level(lv + 1)
                    yield
                    for d in range(2):
                        _mm(P, pB[:, d * 128:(d + 1) * 128], XX[:, (2 + d) * 128:(3 + d) * 128], TT[:, d * 128:(d + 1) * 128], True, True,
                            [b_XX, b_TT], [bpB])
                    if not lastlv:
                        for d in range(2):
                            _mm(P, pB[:, (2 + d) * 128:(3 + d) * 128], XX[:, d * 128:(d + 1) * 128], TT[:, (2 + d) * 128:(3 + d) * 128], True, True,
                                [b_XX, b_TT], [bpB])
                    yield
                    _tt(P, XX[:, 0:w_], XX[:, 0:w_], pB[:, 0:w_], ALU.subtract, [b_XX, bpB], [b_XX])
                    yield
                for d in range(2):
                    _mm(P, pA[:, d * 256:(d + 1) * 256], XX[:, d * 128:(d + 1) * 128], R[d][j][:], True, True, [b_XX, b_R[d][j]], [bpA])
                yield
                for d in range(2):
                    c = cc[d]
                    bcol = m.gb[:, c, 8 + d * 4 + h:8 + d * 4 + h + 1]
                    _act(P, wtok[d][j][:], pA[:, d * 256 + 128:d * 256 + 256], AF.Copy, [bpA], [b_wtok[d][j]], scale=-1.0)
                    _ts(P, u0b[d][s][:], pA[:, d * 256:d * 256 + 128], bcol, None, ALU.mult, None, [bpA, m.b_gb], [b_u0b[d][s]])
                yield
                for d in range(2):
                    _tp(P, pB[:, d * 128:(d + 1) * 128], wtok[d][j][:], k.ident_f, [b_wtok[d][j], k.b_cst], [bpB])
                yield
                for d in range(2):
                    _cp(P, nwT[d][s][:], pB[:, d * 128:(d + 1) * 128], [bpB], [b_nwT[d][s]], eng=("act" if d == 0 else "dve"))
                yield

            def scan(step):
                s = step % NS
                pSs = [k.ps[6], k.ps[1]]
                bpSs = [k.b_ps[6], k.b_ps[1]]
                for d in range(2):
                    c = order[d][step]
                    pS, bpS = pSs[d], bpSs[d]
                    qnc = cT[0][:, c * 128:(c + 1) * 128]
                    _mm(P, pS[:, 0:128], nwT[d][s][:], Sst[d][:], True, True, [b_nwT[d][s], b_S[d]], [bpS])
                    _mm(P, pS[:, 128:256], qnc, Sbf[d][:], True, True, [b_cT[0][_blk_of(c)], b_Sbf[d]], [bpS])
                yield
                for d in range(2):
                    c = order[d][step]
                    pS, bpS = pSs[d], bpSs[d]
                    bcol = m.gb[:, c, 8 + d * 4 + h:8 + d * 4 + h + 1]
                    _stt(P, Usf[d][:], pS[:, 0:128], bcol, u0b[d][s][:], ALU.mult, ALU.add, [bpS, m.b_gb, b_u0b[d][s]], [b_Uf[d]])
                yield
                for d in range(2):
                    _cp(P, Usb[d][:], Usf[d][:], [b_Uf[d]], [b_U[d]], eng="act")
                    pS, bpS = pSs[d], bpSs[d]
                    _mm(P, pS[:, 384:512], kdec[d][s][:], Usf[d][:], True, True, [b_kdec[d][s], b_Uf[d]], [bpS])
                yield
                for d in range(2):
                    pS, bpS = pSs[d], bpSs[d]
                    _mm(P, pS[:, 256:384], QK[d][s][:], Usb[d][:], True, True, [b_QK[d][s], b_U[d]], [bpS])
                yield
                for d in range(2):
                    c = order[d][step]
                    pS, bpS = pSs[d], bpSs[d]
                    eG = m.EX[:, c, d * 4 + h:d * 4 + h + 1]
                    cdec = m.EX[:, c, 16 + d * 4 + h:16 + d * 4 + h + 1]
                    oc = o_acc[:, c, :]
                    _stt(P, Sst[d][:], Sst[d][:], cdec, pS[:, 384:512], ALU.mult, ALU.add, [bpS, m.b_gb, b_S[d]], [b_S[d]])
                    _stt(P, oc, pS[:, 128:256], eG, oc, ALU.mult, ALU.add, [bpS, m.b_gb, b_o[c]], [b_o[c]])
                    _tt(P, oc, pS[:, 256:384], oc, ALU.add, [bpS, b_o[c]], [b_o[c]])
                yield
                for d in range(2):
                    _cp(P, Sbf[d][:], Sst[d][:], [b_S[d]], [b_Sbf[d]], eng="act")
                yield

            pres = {}
            pre_step = {}
            done_pre = set()
            nxt = 0
            scan_pos = 0
            scan_gen = None
            while scan_pos < NT:
                for j in range(NJ):
                    if j not in pres and nxt < NT and nxt < scan_pos + NS:
                        pres[j] = pre(nxt, j)
                        pre_step[j] = nxt
                        nxt += 1
                for j in list(pres):
                    try:
                        next(pres[j])
                    except StopIteration:
                        done_pre.add(pre_step[j])
                        del pres[j]
                if scan_gen is None and scan_pos in done_pre:
                    scan_gen = scan(scan_pos)
                if scan_gen is not None:
                    try:
                        next(scan_gen)
                    except StopIteration:
                        scan_gen = None
                        scan_pos += 1
            P.flush()
        with ExitStack() as g3:
            Sp = lambda n, sh, dt: k.S(n, sh, dt, g3)
            ssn = Sp("ssn", [128, NT], F32)
            b_ssn = P.buf("ssn")
            junk = Sp("junkg", [128, 128], BF16)
            b_junk = P.buf("junkg")
            tmpo = [Sp("tmpo%d" % i, [128, 128], F32) for i in range(2)]
            b_tmpo = P.bufs_n("tmpo", 2)
            obuf = Sp("obuf", [128, NT, 128], BF16)
            b_obuf = P.buf("obuf")
            import os
            if k.dbgmode and h == int(os.environ.get("DBG_HEAD", "0")):
                P.dma("sp", k.dbg4, o_acc[:].rearrange("p a b -> p (a b)"), r=b_o)
            sq3 = Sp("sq3", [128, NT, 128], F32)
            b_sq3 = P.buf("sq3")
            _act(P, sq3[:].rearrange("p a b -> p (a b)"), o_acc[:].rearrange("p a b -> p (a b)"), AF.Square, b_o, [b_sq3])
            P.dve(lambda e: e.reduce_sum(out=ssn[:], in_=sq3[:], axis=AX.X), [b_sq3], [b_ssn])
            _rstd(P, ssn[:], NT, 1.0 / 128, EPS, [b_ssn], [b_ssn])
            gnb = k.gnb[:, l * 128:(l + 1) * 128]
            _tt(P, sq3[:], o_acc[:], gnb.unsqueeze(1).to_broadcast([128, NT, 128]), ALU.mult, b_o + [k.b_small], [b_sq3])
            _tt(P, sq3[:], sq3[:], zs[:], ALU.mult, [b_sq3, b_zs], [b_sq3])
            _tt(P, obuf[:], sq3[:], ssn[:].unsqueeze(2).to_broadcast([128, NT, 128]), ALU.mult, [b_sq3, b_ssn], [b_obuf])
            P.dma("sp", k.oscr[:, h * 128:(h + 1) * 128].rearrange("(t p) c -> p t c", p=128), obuf[:], r=[b_obuf], w=[k.b_oscr])
            P.flush()


def _blk_of(c):
    tok = c * 128
    for bi, (t0, nt) in enumerate(TBLK):
        if t0 <= tok < t0 + nt:
            return bi
    raise ValueError


def mla_group(k, m, l):
    nc, P = k.nc, k.P
    SC = float(96.0 ** -0.5)
    with ExitStack() as ml:
        S = lambda n, sh, dt, stack=ml: k.S(n, sh, dt, stack)
        cn = S("cn", [128, 5, T], BF16)
        b_cn = P.bufs_n("cn", len(TBLK))
        Vall = S("Vall", [128, NT, 8, 65], BF16)
        b_V = P.buf("Vall")
        krr = S("krr", [96, T], BF16)
        b_krr = P.buf("krr")
        cos = S("cosq", [96, T], BF16)
        sin = S("sinq", [96, T], BF16)
        b_rope = P.buf("rope")
        P.dma("pool", cos[:], k.cosq[0:96, :], w=[b_rope])
        P.dma("pool", sin[:], k.sinq[0:96, :], w=[b_rope])
        with ExitStack() as p1:
            Sp = lambda n, sh, dt: k.S(n, sh, dt, p1)
            wc = Sp("wc", [128, KD, 640], BF16)
            b_wc = P.buf("wc")
            wkr = Sp("wkr", [128, KD, 2, 96], BF16)
            b_wkr = P.buf("wkr")
            wv = Sp("wv", [128, 2, 512], BF16)
            b_wv = P.buf("wv")
            rawc = [Sp("rawc%d" % i, [128, 5, 512], F32) for i in range(1)] * 2
            b_rawc = P.bufs_n("rawc", 1) * 2
            sqb = [Sp("sqc%d" % i, [128, 5, 512], BF16) for i in range(1)] * 2
            b_sqb = P.bufs_n("sqc", 1) * 2
            rsb = [Sp("rsc%d" % i, [128, 2, 512], F32) for i in range(1)] * 2
            b_rsb = P.bufs_n("rsc", 1) * 2
            hblk = [Sp("hblkM%d" % i, [128, KD, 512], BF16) for i in range(1)] * 2
            b_hblk = P.bufs_n("hblkM", 1) * 2
            t1 = Sp("t1", [96, 512], F32)
            t2 = Sp("t2", [96, 512], F32)
            b_t = P.buf("t12")
            P.dma("pool", wc[:], k.w_in[l, :, OFF_CQ:OFF_CQ + 640].rearrange("(k p) n -> p k n", p=128), w=[b_wc])
            _memset(P, wkr[:].rearrange("p a b c -> p (a b c)"), 0.0, [b_wkr])
            wsrc = k.w_in[l, :, OFF_KR:OFF_KR + 32].rearrange("(k p) n -> p k n", p=128)
            P.dma("pool", wkr[:, :, 0, 64:96], wsrc, w=[b_wkr])
            P.dma("pool", wkr[:, :, 1, 64:80], wsrc[:, :, 16:32], w=[b_wkr])
            P.dma("pool", wkr[:, :, 1, 80:96], wsrc[:, :, 0:16], w=[b_wkr])
            P.dma("pool", wv[:], k.w_ukv_v[l].rearrange("(k p) n -> p k n", p=128), w=[b_wv])
            _memset(P, Vall[:].rearrange("p a b c -> p (a b c)"), 1.0, [b_V])
            for bi, (t0, nt) in enumerate(TBLK):
                s = bi % 2
                hb = [b_hblk[s]]
                hT_ = hblk[s]
                P.dma("sp", hblk[s][:, :, 0:nt], k.hscr[:, :, t0:t0 + nt], r=[m.b_hscr[bi]], w=[b_hblk[s]])
                for c in range(5):
                    pp = k.ps[c % 2]
                    bpp = k.b_ps[c % 2]
                    for kk in range(KD):
                        _mm(P, pp[:, 0:nt], wc[:, kk, c * 128:(c + 1) * 128], hT_[:, kk, 0:nt], kk == 0, kk == KD - 1, [b_wc] + hb, [bpp])
                    _cp(P, rawc[s][:, c, 0:nt], pp[:, 0:nt], [bpp], [b_rawc[s]])
                    _act(P, sqb[s][:, c, 0:nt], rawc[s][:, c, 0:nt], AF.Square, [b_rawc[s]], [b_sqb[s]])
                for gi, (c0, c1, nfeat) in enumerate(((0, 3, 384.0), (3, 5, 256.0))):
                    pp = k.ps[2 + gi]
                    bpp = k.b_ps[2 + gi]
                    for c in range(c0, c1):
                        _mm(P, pp[:, 0:nt], k.ones_b[:], sqb[s][:, c, 0:nt], c == c0, c == c1 - 1, [b_sqb[s], k.b_cst], [bpp])
                    rs = rsb[s][:, gi, 0:nt]
                    _ts(P, rs, pp[:, 0:nt], 1.0 / nfeat, EPS, ALU.mult, ALU.add, [bpp], [b_rsb[s]])
                    _act(P, rs, rs, AF.Sqrt, [b_rsb[s]], [b_rsb[s]])
                    P.dve(lambda e, a=rs: e.reciprocal(out=a, in_=a), [b_rsb[s]], [b_rsb[s]])
                    for c in range(c0, c1):
                        gcol = (k.qn[:, l * 3 + c:l * 3 + c + 1] if gi == 0 else k.kvn[:, l * 2 + (c - 3):l * 2 + (c - 3) + 1])
                        _stt(P, cn[:, c, t0:t0 + nt], rawc[s][:, c, 0:nt], gcol, rs, ALU.mult, ALU.mult, [b_rawc[s], b_rsb[s], k.b_small],
                             [b_cn[bi]])
                pk, bpk = k.ps[4], k.b_ps[4]
                pks, bpks = k.ps[5], k.b_ps[5]
                for kk in range(KD):
                    _mm(P, pk[0:96, 0:nt], wkr[:, kk, 0, :], hT_[:, kk, 0:nt], kk == 0, kk == KD - 1, [b_wkr] + hb, [bpk])
                for kk in range(KD):
                    _mm(P, pks[0:96, 0:nt], wkr[:, kk, 1, :], hT_[:, kk, 0:nt], kk == 0, kk == KD - 1, [b_wkr] + hb, [bpks])
                _tt(P, t1[64:96, 0:nt], pk[64:96, 0:nt], cos[64:96, t0:t0 + nt], ALU.mult, [bpk, b_rope], [b_t])
                _tt(P, t2[64:96, 0:nt], pks[64:96, 0:nt], sin[64:96, t0:t0 + nt], ALU.mult, [bpks, b_rope], [b_t])
                _tt(P, krr[64:96, t0:t0 + nt], t1[64:96, 0:nt], t2[64:96, 0:nt], ALU.add, [b_t], [b_krr])
                for t in range(t0 // 128, (t0 + nt) // 128):
                    pv, bpv = k.ps[6], k.b_ps[6]
                    for c in range(2):
                        _mm(P, pv[:, 0:512], cn[:, 3 + c, t * 128:(t + 1) * 128], wv[:, c, :], c == 0, c == 1, [b_cn[bi], b_wv], [bpv])
                    _act(P, Vall[:, t, :, 0:64], pv[:, 0:512].rearrange("p (a b) -> p a b", b=64), AF.Copy, [bpv], [b_V])
            P.flush()
        with ExitStack() as p2:
            Sp = lambda n, sh, dt: k.S(n, sh, dt, p2)
            wuq = [Sp("wuq%d" % i, [128, 3, 2, 96], BF16) for i in range(2)]
            b_wuq = P.bufs_n("wuq", 2)
            wuk = [Sp("wuk%d" % i, [128, 2, 64], BF16) for i in range(2)]
            b_wuk = P.bufs_n("wuk", 2)
            Qf = [Sp("Qf%d" % i, [96, T], BF16) for i in range(2)]
            b_Qf = P.bufs_n("Qf", 2)
            Kf = [Sp("Kf%d" % i, [96, T], BF16) for i in range(2)]
            b_Kf = P.bufs_n("Kf", 2)
            t1 = Sp("t1b", [96, 512], F32)
            t2 = Sp("t2b", [96, 512], F32)
            b_t = P.buf("t12b")
            PT = [Sp("PT%d" % i, [128, 512], BF16) for i in range(3)]
            b_PT = P.bufs_n("PT", 3)
            rec = Sp("rec", [128, 4], F32)
            b_rec = P.buf("rec")
            omla = Sp("omla", [128, NT, 512], BF16)
            b_om = P.buf("omla")
            npt = 0
            for hh in range(8):
                s = hh % 2
                P.dma("pool", wuq[s][:, :, 0, :], k.w_uq[l, :, hh * 96:(hh + 1) * 96].rearrange("(k p) n -> p k n", p=128), w=[b_wuq[s]])
                P.dma("pool", wuq[s][:, :, 1, :], k.w_uq_sw[l, :, hh * 96:(hh + 1) * 96].rearrange("(k p) n -> p k n", p=128), w=[b_wuq[s]])
                P.dma("pool", wuk[s][:], k.w_ukv_k[l, :, hh * 64:(hh + 1) * 64].rearrange("(k p) n -> p k n", p=128), w=[b_wuk[s]])
                for bi, (t0, nt) in enumerate(TBLK):
                    pq, bpq = k.ps[0], k.b_ps[0]
                    pqs, bpqs = k.ps[1], k.b_ps[1]
                    pkn, bpkn = k.ps[2], k.b_ps[2]
                    for c in range(3):
                        _mm(P, pq[0:96, 0:nt], wuq[s][:, c, 0, :], cn[:, c, t0:t0 + nt], c == 0, c == 2, [b_wuq[s], b_cn[bi]], [bpq])
                    for c in range(3):
                        _mm(P, pqs[0:96, 0:nt], wuq[s][:, c, 1, :], cn[:, c, t0:t0 + nt], c == 0, c == 2, [b_wuq[s], b_cn[bi]], [bpqs])
                    for c in range(2):
                        _mm(P, pkn[0:64, 0:nt], wuk[s][:, c, :], cn[:, 3 + c, t0:t0 + nt], c == 0, c == 1, [b_wuk[s], b_cn[bi]], [bpkn])
                    _tt(P, t1[:, 0:nt], pq[0:96, 0:nt], cos[:, t0:t0 + nt], ALU.mult, [bpq, b_rope], [b_t])
                    _tt(P, t2[:, 0:nt], pqs[0:96, 0:nt], sin[:, t0:t0 + nt], ALU.mult, [bpqs, b_rope], [b_t])
                    _tt(P, Qf[s][:, t0:t0 + nt], t1[:, 0:nt], t2[:, 0:nt], ALU.add, [b_t], [b_Qf[s]])
                    _act(P, Kf[s][0:64, t0:t0 + nt], pkn[0:64, 0:nt], AF.Copy, [bpkn], [b_Kf[s]])
                _cp(P, Kf[s][64:96, :], krr[64:96, :], [b_krr], [b_Kf[s]])
                for bi, (t0, nt) in enumerate(TBLK):
                    ktiles = [0, 1] if bi == 0 else list(range(NT))
                    nq = nt // 128
                    po, bpo = k.ps[6], k.b_ps[6]
                    po3 = po[:, 0:4 * 65].rearrange("p (a b) -> p a b", b=65)
                    def st_exp(kt):
                        nonlocal npt
                        pst, bpst = k.ps[3 + (npt % 3)], k.b_ps[3 + (npt % 3)]
                        pt, bpt = PT[npt % 3], b_PT[npt % 3]
                        npt += 1
                        _mm(P, pst[:, 0:nt], Kf[s][:, kt * 128:(kt + 1) * 128], Qf[s][:, t0:t0 + nt], True, True, [b_Kf[s], b_Qf[s]], [bpst])
                        _act(P, pt[:, 0:nt], pst[:, 0:nt], AF.Exp, [bpst], [bpt], scale=SC)
                        return pt, bpt

                    nxt_pt = st_exp(ktiles[0])
                    for ki, kt in enumerate(ktiles):
                        pt, bpt = nxt_pt
                        if ki + 1 < len(ktiles):
                            nxt_pt = st_exp(ktiles[ki + 1])
                        for qi in range(nq):
                            P.pe(lambda e, o=po3[:, qi, :], a=pt[:, qi * 128:(qi + 1) * 128], b=Vall[:, kt, hh, :],
                                 st=(ki == 0 and qi == 0), sp=(ki == len(ktiles) - 1):
                                 e.matmul(o, lhsT=a, rhs=b, start=st, stop=sp, skip_group_check=True), [bpt, b_V], [bpo])
                    P.dve(lambda e, o=rec[:, 0:nq], a=po3[:, 0:nq, 64]: e.reciprocal(out=o, in_=a), [bpo], [b_rec])
                    for qi in range(nq):
                        t = t0 // 128 + qi
                        _ts(P, omla[:, t, hh * 64:(hh + 1) * 64], po3[:, qi, 0:64], rec[:, qi:qi + 1], None, ALU.mult, None, [bpo, b_rec], [b_om])
            P.dma("sp", k.oscr[:, 512:1024].rearrange("(t p) c -> p t c", p=128), omla[:], r=[b_om], w=[k.b_oscr])
            P.flush()


def out_proj(k, l):
    nc, P = k.nc, k.P
    last = l == DEPTH - 1
    with ExitStack() as ph:
        S = lambda n, sh, dt: k.S(n, sh, dt, ph)
        wo = S("wo", [128, KD, D], BF16)
        b_wo = P.buf("wo")
        ot = [S("ot%d" % i, [128, D], BF16) for i in range(2)]
        b_ot = P.bufs_n("ot", 2)
        oT = [S("oT%d" % i, [128, KD, 128], BF16) for i in range(2)]
        b_oT = P.bufs_n("oT", 2)
        ysb = [S("ysbo%d" % i, [128, D], F32) for i in range(2)]
        b_ysb = P.bufs_n("ysbo", 2)
        ss2 = [S("ss2o%d" % i, [128, 2], F32) for i in range(2)]
        b_ss2 = P.bufs_n("ss2o", 2)
        junk = S("junko", [128, 512], BF16)
        b_junk = P.buf("junko")
        P.dma("pool", wo[:], k.w_out[l].rearrange("(k p) n -> p k n", p=128), w=[b_wo])
        tiles = list(range(NCT, NT)) if last else list(range(NT))
        for n, t in enumerate(tiles):
            s = n % 2
            P.dma("sp", ot[s][:], k.oscr[t * 128:(t + 1) * 128, :], r=[k.b_oscr], w=[b_ot[s]])
            for kk in range(KD):
                _tp(P, k.ps_tp[:, kk * 128:(kk + 1) * 128], ot[s][:, kk * 128:(kk + 1) * 128], k.ident_b[:], [b_ot[s], k.b_cst], [k.b_tp])
            _cp(P, oT[s][:].rearrange("p a b -> p (a b)"), k.ps_tp[:, :], [k.b_tp], [b_oT[s]], eng="act")
            for half in range(2):
                py, bpy = k.ps[4 + half], k.b_ps[4 + half]
                for kk in range(KD):
                    _mm(P, py[:], oT[s][:, kk, :], wo[:, kk, half * 512:(half + 1) * 512], kk == 0, kk == KD - 1, [b_oT[s], b_wo], [bpy])
                _cp(P, ysb[s][:, half * 512:(half + 1) * 512], py[:], [bpy], [b_ysb[s]])
                _act(P, junk[:], ysb[s][:, half * 512:(half + 1) * 512], AF.Square, [b_ysb[s]], [b_junk, b_ss2[s]],
                     accum_out=ss2[s][:, half:half + 1])
            post_tile(k, t, 1, ysb[s][:], b_ysb[s], ss2[s], b_ss2[s])
        P.flush()


def _fm(v):
    v = np.asarray(v, np.float32)
    n = v.shape[-1] // 128
    lead = v.shape[:-1]
    a = v.reshape(lead + (n, 128))
    a = np.moveaxis(a, -1, 0)
    return np.ascontiguousarray(a.reshape(128, -1))


def _consts():
    c = np.zeros((128, 13, 128), np.float32)
    r = np.arange(128)[:, None]
    q = np.arange(128)[None, :]
    c[:, 0, :] = (r == q)
    c[:, 1, :] = 1.0
    c[:, 2, :] = (r <= q)
    c[:, 3, :] = (r >= q)
    c[:, 4, :] = (r < q)
    c[:, 5, :] = (r > q)
    for lv in range(7):
        sz = 1 << lv
        c[:, 6 + lv, :] = ((r // (2 * sz)) == (q // (2 * sz))) & ((r // sz) != (q // sz))
    return c.reshape(128, 13 * 128)


def _rope_tables():
    S_, GW = 2048, 64
    row = np.repeat(np.arange(S_ // GW), GW).astype(np.float32)
    col = np.tile(np.arange(GW), S_ // GW).astype(np.float32)
    inv = np.power(np.float32(10000.0), -np.arange(0, 16, 2, dtype=np.float32) / np.float32(16)).astype(np.float32)
    ang = np.concatenate([row[:, None] * inv, col[:, None] * inv], -1).astype(np.float32)
    cos, sin = np.cos(ang).T, np.sin(ang).T
    C = np.zeros((128, T), np.float32)
    Sn = np.zeros((128, T), np.float32)
    C[0:96, :] = 1.0
    C[64:80, 256:] = cos
    C[80:96, 256:] = cos
    Sn[64:80, 256:] = -sin
    Sn[80:96, 256:] = sin
    return C, Sn


_CACHE = {}


def kernel(**inp):
    B = inp["x"].shape[0]
    if "nc" not in _CACHE:
        _CACHE["nc"] = build()
    nc, k = _CACHE["nc"]
    shared = {
        "w_ada": np.ascontiguousarray(inp["w_ada"], np.float32),
        "b_fm": _fm(inp["b_ada"]),
        "gpre_fm": _fm(inp["norm_pre"]),
        "gpost_fm": _fm(inp["norm_post"]),
        "ffn_w_gate": np.ascontiguousarray(inp["ffn_w_gate"], np.float32),
        "ffn_w_up": np.ascontiguousarray(inp["ffn_w_up"], np.float32),
        "ffn_w_down": np.ascontiguousarray(inp["ffn_w_down"], np.float32),
        "cst": _consts(),
        "w_in": np.ascontiguousarray(inp["w_in"], np.float32),
        "w_out": np.ascontiguousarray(inp["w_out"], np.float32),
        "qn_fm": _fm(inp["mla_q_norm"]),
        "kvn_fm": _fm(inp["mla_kv_norm"]),
        "w_uq": np.ascontiguousarray(inp["mla_w_uq"], np.float32),
    }
    dtb = np.asarray(inp["gdn_dt_bias"], np.float32).reshape(DEPTH, 1, 1, 8)
    alg = np.asarray(inp["gdn_a_log"], np.float32).reshape(DEPTH, 1, 1, 8)
    g2 = np.concatenate([np.broadcast_to(dtb, (DEPTH, 1, NT, 8)), np.broadcast_to(alg, (DEPTH, 1, NT, 8))], 1)
    shared["gdnc"] = np.ascontiguousarray(np.broadcast_to(g2.reshape(1, -1), (128, DEPTH * 2 * NT * 8)), np.float32)
    cw = np.asarray(inp["gdn_conv"], np.float32)
    cw = cw.transpose(0, 2, 1).reshape(DEPTH, 12, 128, 5).transpose(2, 0, 1, 3)
    shared["convw"] = np.ascontiguousarray(cw.reshape(128, -1))
    gn = np.asarray(inp["gdn_out_norm"], np.float32).reshape(1, -1)
    shared["gnb"] = np.ascontiguousarray(np.broadcast_to(gn, (128, DEPTH * 128)), np.float32)
    perm = np.arange(768).reshape(8, 96)
    perm = np.concatenate([perm[:, 0:64], perm[:, 80:96], perm[:, 64:80]], 1).reshape(-1)
    shared["w_uq_sw"] = np.ascontiguousarray(shared["w_uq"][:, :, perm])
    wkv = np.asarray(inp["mla_w_ukv"], np.float32).reshape(DEPTH, 256, 8, 128)
    shared["w_ukv_k"] = np.ascontiguousarray(wkv[:, :, :, 0:64].reshape(DEPTH, 256, 512))
    shared["w_ukv_v"] = np.ascontiguousarray(wkv[:, :, :, 64:128].reshape(DEPTH, 256, 512))
    shared["cosq"], shared["sinq"] = _rope_tables()
    in_maps = []
    for b in range(B):
        m = dict(shared)
        m["xcat"] = np.ascontiguousarray(np.concatenate([inp["ctx"][b], inp["x"][b]], 0), np.float32)
        cc = np.stack([_fm(inp["c"][b]), _fm(inp["c_ctx"])], -1)
        m["cc_fm"] = np.ascontiguousarray(cc.reshape(128, 16))
        in_maps.append(m)
    res = run_bass_kernel_spmd(nc, in_maps, core_ids=list(range(B)))
    return np.stack([r["out"] for r in res.results], 0)
```

```python
from contextlib import ExitStack
import numpy as np
import concourse.bass as bass
import concourse.mybir as mybir
from concourse.bass_utils import run_bass_kernel_spmd

F32 = mybir.dt.float32
BF16 = mybir.dt.bfloat16
AF = mybir.ActivationFunctionType
ALU = mybir.AluOpType
AX = mybir.AxisListType

ENGS = ("pe", "act", "dve", "pool", "sp")
DMA_K = 8


class Buf:
    __slots__ = ("name", "w", "r", "excl")

    def __init__(self, name="", excl=False):
        self.name = name
        self.w = None
        self.r = {}
        self.excl = excl


class Op:
    __slots__ = ("eng", "fn", "deps", "inc", "sem", "val", "dma")

    def __init__(self, eng, fn, dma):
        self.eng = eng
        self.fn = fn
        self.deps = []
        self.inc = False
        self.sem = None
        self.val = 0
        self.dma = dma


class Prog:
    def __init__(self, nc, stack):
        self.nc = nc
        self.ops = {e: [] for e in ENGS}
        self.dma_hist = {e: [] for e in ENGS}
        self.esem = {e: stack.enter_context(nc.semaphore("es_" + e)) for e in ENGS}
        self.dsem = {
            e: [stack.enter_context(nc.semaphore("ds_%s%d" % (e, i))) for i in range(DMA_K)]
            for e in ("sp", "pool", "act")
        }
        self.cnt = {e: 0 for e in ENGS}
        self.nd = {e: 0 for e in ENGS}
        self.known = {e: {} for e in ENGS}
        self.carry = {}
        self.bufs = []
        self.stats = {e: [0, 0] for e in ENGS}

    def buf(self, name="", excl=False):
        b = Buf(name, excl)
        self.bufs.append(b)
        return b

    def bufs_n(self, name, n):
        return [self.buf("%s%d" % (name, i)) for i in range(n)]

    def _add(self, eng, fn, r, w, dma=False):
        op = Op(eng, fn, dma)
        deps = []
        xr = [b for b in r if b.excl]
        if xr:
            r = [b for b in r if not b.excl]
            w = list(w) + [b for b in xr if b not in w]
        for b in r:
            if b.w is not None:
                deps.append(b.w)
        for b in w:
            if b.w is not None and (dma or b.w.dma or b.w.eng != eng):
                deps.append(b.w)
            for o in b.r.values():
                if dma or o.dma or o.eng != eng:
                    deps.append(o)
        for b in r:
            b.r[eng] = op
        for b in w:
            b.w = op
            b.r = {}
        if dma:
            h = self.dma_hist[eng]
            if len(h) >= DMA_K:
                deps.append(h[-DMA_K])
            h.append(op)
            op.inc = True
        seen = set()
        for d in deps:
            if d is op or id(d) in seen:
                continue
            if (not d.dma) and d.eng == "pe" and eng == "pe" and not dma:
                continue
            seen.add(id(d))
            d.inc = True
            op.deps.append(d)
        self.ops[eng].append(op)
        return op

    def pe(self, fn, r=(), w=()):
        return self._add("pe", fn, r, w)

    def act(self, fn, r=(), w=()):
        return self._add("act", fn, r, w)

    def dve(self, fn, r=(), w=()):
        return self._add("dve", fn, r, w)

    def pool(self, fn, r=(), w=()):
        return self._add("pool", fn, r, w)

    def dma(self, q, out, in_, r=(), w=(), **kw):
        return self._add(q, lambda e: e.dma_start(out=out, in_=in_, **kw), r, w, dma=True)

    def flush(self):
        nc = self.nc
        newcarry = {}
        for e in ENGS:
            lastc = None
            for op in self.ops[e]:
                if not op.dma:
                    lastc = op
            if lastc is not None:
                lastc.inc = True
            for op in self.ops[e]:
                if op.dma:
                    n = self.nd[e]
                    op.sem = self.dsem[e][n % DMA_K]
                    op.val = 16 * (n // DMA_K + 1)
                    self.nd[e] = n + 1
                    newcarry[op.sem.num] = (op.sem, op.val)
                elif op.inc:
                    self.cnt[e] += 1
                    op.sem = self.esem[e]
                    op.val = self.cnt[e]
                    newcarry[op.sem.num] = (op.sem, op.val)
        carry = dict(self.carry)
        with nc.Block() as block:
            def emit(e):
                def body(eng):
                    known = self.known[e]
                    nwait = 0
                    for k, (sm, v) in carry.items():
                        if known.get(k, 0) < v:
                            eng.wait_ge(sm, v)
                            known[k] = v
                            nwait += 1
                    for op in self.ops[e]:
                        need = {}
                        for d in op.deps:
                            k = d.sem.num
                            if need.get(k, (None, 0))[1] < d.val:
                                need[k] = (d.sem, d.val)
                        for k, (sm, v) in need.items():
                            if known.get(k, 0) < v:
                                eng.wait_ge(sm, v)
                                known[k] = v
                                nwait += 1
                        ins = op.fn(eng)
                        if op.inc:
                            ins.then_inc(op.sem, 16 if op.dma else 1)
                    self.stats[e][0] += len(self.ops[e])
                    self.stats[e][1] += nwait
                return body

            block.tensor(emit("pe"))
            block.scalar(emit("act"))
            block.vector(emit("dve"))
            block.gpsimd(emit("pool"))
            block.sync(emit("sp"))
        for k, v in newcarry.items():
            self.carry[k] = v
        self.ops = {e: [] for e in ENGS}
        self.dma_hist = {e: [] for e in ENGS}
        for b in self.bufs:
            b.w = None
            b.r = {}

    def finish(self):
        self.flush()
        nc = self.nc
        carry = dict(self.carry)
        with nc.Block() as block:
            def emit(e):
                def body(eng):
                    known = self.known[e]
                    for k, (sm, v) in carry.items():
                        if known.get(k, 0) < v:
                            eng.wait_ge(sm, v)
                            known[k] = v
                return body

            block.tensor(emit("pe"))
            block.scalar(emit("act"))
            block.vector(emit("dve"))
            block.gpsimd(emit("pool"))
            block.sync(emit("sp"))

D = 1024
KD = 8
NT = 18
NCT = 2
T = NT * 128
DEPTH = 4
DFF = 2816
NF = 22
IN_COLS = 2736
EPS = 1e-6
OFF_Q, OFF_K, OFF_V, OFF_Z, OFF_A, OFF_B, OFF_CQ, OFF_CKV, OFF_KR = 0, 512, 1024, 1536, 2048, 2056, 2064, 2448, 2704


def _mm(P, out, lhsT, rhs, start, stop, r, w):
    return P.pe(lambda e: e.matmul(out, lhsT=lhsT, rhs=rhs, start=start, stop=stop), r, w)


def _tp(P, out, in_, ident, r, w):
    return P.pe(lambda e: e.transpose(out, in_, ident), r, w)


def _act(P, out, in_, func, r, w, bias=None, scale=None, accum_out=None):
    kw = {}
    if bias is not None:
        kw["bias"] = bias
    if scale is not None:
        kw["scale"] = scale
    if accum_out is not None:
        kw["accum_out"] = accum_out
    return P.act(lambda e: e.activation(out=out, in_=in_, func=func, **kw), r, w)


def _ts(P, out, in0, s1, s2, op0, op1, r, w, eng="dve", accum_out=None):
    kw = {}
    if op1 is not None:
        kw["op1"] = op1
    if accum_out is not None:
        kw["accum_out"] = accum_out
    f = lambda e: e.tensor_scalar(out=out, in0=in0, scalar1=s1, scalar2=s2, op0=op0, **kw)
    return P._add(eng, f, r, w)


def _stt(P, out, in0, scalar, in1, op0, op1, r, w):
    return P.dve(lambda e: e.scalar_tensor_tensor(out=out, in0=in0, scalar=scalar, in1=in1, op0=op0, op1=op1), r, w)


def _tt(P, out, in0, in1, op, r, w, eng="dve"):
    return P._add(eng, lambda e: e.tensor_tensor(out=out, in0=in0, in1=in1, op=op), r, w)


def _cp(P, out, in_, r, w, eng="dve"):
    if eng == "act":
        return P.act(lambda e: e.activation(out=out, in_=in_, func=AF.Copy), r, w)
    return P._add(eng, lambda e: e.tensor_copy(out=out, in_=in_), r, w)


def _rstd(P, ss, n, inv_n, eps, r, w):
    _ts(P, ss, ss, inv_n, eps, ALU.mult, ALU.add, r, w)
    _act(P, ss, ss, AF.Sqrt, r, w)
    P.dve(lambda e: e.reciprocal(out=ss, in_=ss), r, w)


class K:
    pass


def build(nlayers=DEPTH, dbg=None, stages=("ada", "ffn0", "mix", "ffn1")):
    nc = bass.Bass("TRN2", target_bir_lowering=False)
    k = K()
    k.nc = nc
    k.dbgmode = dbg

    def din(name, shape):
        return nc.dram_tensor(name, list(shape), F32, kind="ExternalInput").ap()

    k.xcat = din("xcat", [T, D])
    k.cc_fm = din("cc_fm", [128, KD * 2])
    k.w_ada = din("w_ada", [DEPTH, D, 9 * D])
    k.b_fm = din("b_fm", [128, DEPTH * 72])
    k.gpre_fm = din("gpre_fm", [128, DEPTH * 3 * KD])
    k.gpost_fm = din("gpost_fm", [128, DEPTH * 3 * KD])
    k.wg = din("ffn_w_gate", [DEPTH, 2, D, DFF])
    k.wu = din("ffn_w_up", [DEPTH, 2, D, DFF])
    k.wd = din("ffn_w_down", [DEPTH, 2, DFF, D])
    k.cst = din("cst", [128, 128 * 13])
    k.w_in = din("w_in", [DEPTH, D, IN_COLS])
    k.w_out = din("w_out", [DEPTH, D, D])
    k.gdnc_d = din("gdnc", [128, DEPTH * 2 * NT * 8])
    k.convw_d = din("convw", [128, DEPTH * 60])
    k.gnb_d = din("gnb", [128, DEPTH * 128])
    k.qn_d = din("qn_fm", [128, DEPTH * 3])
    k.kvn_d = din("kvn_fm", [128, DEPTH * 2])
    k.w_uq = din("w_uq", [DEPTH, 384, 768])
    k.w_uq_sw = din("w_uq_sw", [DEPTH, 384, 768])
    k.w_ukv_k = din("w_ukv_k", [DEPTH, 256, 512])
    k.w_ukv_v = din("w_ukv_v", [DEPTH, 256, 512])
    k.cosq = din("cosq", [128, T])
    k.sinq = din("sinq", [128, T])
    k.oscr = nc.dram_tensor("oscr", [T, D], BF16, kind="Internal").ap()
    k.hscr = nc.dram_tensor("hscr", [128, KD, T], BF16, kind="Internal").ap()
    k.wscr = [nc.dram_tensor("wscr%d" % i, [NF // 2, 128, KD * 256], BF16, kind="Internal").ap() for i in range(2)]
    k.out = nc.dram_tensor("out", [2048, D], F32, kind="ExternalOutput").ap()
    if dbg:
        k.dbg = nc.dram_tensor("dbg", [T, D], F32, kind="ExternalOutput").ap()
        k.dbg2 = nc.dram_tensor("dbg2", [T, D], BF16, kind="ExternalOutput").ap()
        k.dbg3 = nc.dram_tensor("dbg3", [3, 128, T], BF16, kind="ExternalOutput").ap()
        k.dbg4 = nc.dram_tensor("dbg4", [128, NT * 128], F32, kind="ExternalOutput").ap()
        k.dbg5 = nc.dram_tensor("dbg5", [128, NT * 48], F32, kind="ExternalOutput").ap()

    with ExitStack() as st:
        P = Prog(nc, st)
        k.P = P
        k.st = st

        k.uid = 0

        def S(name, shape, dt, stack=st):
            k.uid += 1
            return stack.enter_context(nc.sbuf_tensor("%s_%d" % (name, k.uid), list(shape), dt))

        k.S = S
        k.ps_tp = st.enter_context(nc.psum_tensor("ps_tp", [128, 1024], BF16))
        k.b_tp = P.buf("ps_tp", excl=True)
        k.ps = [st.enter_context(nc.psum_tensor("ps%d" % i, [128, 512], F32)) for i in range(7)]
        k.b_ps = [P.buf("ps%d" % i, excl=True) for i in range(7)]

        k.x = S("x_res", [128, NT, D], F32)
        k.b_x = P.bufs_n("x", NT)
        k.cstt = S("cstt", [128, 13, 128], F32)
        k.ones_b = S("ones_b", [128, 128], BF16)
        k.convw = S("convw_s", [128, DEPTH * 60], F32)
        k.gnb = S("gnb_s", [128, DEPTH * 128], F32)
        k.qn = S("qn", [128, DEPTH * 3], F32)
        k.kvn = S("kvn", [128, DEPTH * 2], F32)
        k.b_oscr = P.buf("oscr")
        k.ident_b = S("ident_b", [128, 128], BF16)
        k.ones_f = k.cstt[:, 1, :]
        k.ident_f = k.cstt[:, 0, :]
        k.b_cst = P.buf("cst")
        k.s_fm = S("s_fm", [128, KD, 2], BF16)
        k.ccf = S("ccf", [128, KD * 2], F32)
        k.bfm = S("bfm", [128, DEPTH * 72], F32)
        k.gpre = S("gpre", [128, DEPTH * 3 * KD], F32)
        k.gpost = S("gpost", [128, DEPTH * 3 * KD], F32)
        k.b_small = P.buf("small")
        k.m_fm = S("m_fm", [128, 72, 2], F32)
        k.sc1p = S("sc1p", [128, 3, 2, KD], F32)
        k.gatev = S("gatev", [128, 3, 2, KD], F32)
        k.G = S("Gb", [128, 6, D], BF16)
        k.b_mod = P.buf("mod")
        k.b_G = P.buf("G")

        for t in range(NT):
            P.dma("sp", k.x[:, t, :], k.xcat[t * 128:(t + 1) * 128, :], w=[k.b_x[t]])
        P.dma("sp", k.cstt[:], k.cst.rearrange("p (a b) -> p a b", a=13), w=[k.b_cst])
        P.dma("sp", k.convw[:], k.convw_d, w=[k.b_small])
        P.dma("sp", k.gnb[:], k.gnb_d, w=[k.b_small])
        P.dma("sp", k.qn[:], k.qn_d, w=[k.b_small])
        P.dma("sp", k.kvn[:], k.kvn_d, w=[k.b_small])
        P.dma("sp", k.ccf[:], k.cc_fm, w=[k.b_small])
        P.dma("sp", k.bfm[:], k.b_fm, w=[k.b_small])
        P.dma("sp", k.gpre[:], k.gpre_fm, w=[k.b_small])
        P.dma("sp", k.gpost[:], k.gpost_fm, w=[k.b_small])
        _cp(P, k.ident_b[:], k.ident_f, [k.b_cst], [k.b_cst])
        _cp(P, k.ones_b[:], k.ones_f, [k.b_cst], [k.b_cst])
        _act(P, k.s_fm[:].rearrange("p a b -> p (a b)"), k.ccf[:], AF.Silu, [k.b_small], [k.b_small])
        P.flush()

        for l in range(nlayers):
            last = l == DEPTH - 1
            if "ada" in stages:
                ada_phase(k, l)
            if "ffn0" in stages:
                ffn_phase(k, l, 0, 0, list(range(NT)))
            if "mix" in stages:
                mix_phase(k, l)
            if "ffn1" in stages:
                ffn_phase(k, l, 1, 2, list(range(NCT, NT)) if last else list(range(NT)))

        for t in range(NCT, NT):
            P.dma("sp", k.out[(t - NCT) * 128:(t - NCT + 1) * 128, :], k.x[:, t, :], r=[k.b_x[t]])
        if dbg:
            for t in range(NT):
                P.dma("sp", k.dbg[t * 128:(t + 1) * 128, :], k.x[:, t, :], r=[k.b_x[t]])
            P.dma("sp", k.dbg2, k.oscr, r=[k.b_oscr])
        P.finish()
        k.stats = P.stats
    return nc, k


def ada_phase(k, l):
    nc, P = k.nc, k.P
    with ExitStack() as ph:
        wa = [k.S("wa%d" % i, [128, 9 * D], BF16, ph) for i in range(2)]
        b_wa = P.bufs_n("wa", 2)
        diag = [k.S("diag%d" % i, [128, 128], F32, ph) for i in range(2)]
        b_diag = P.bufs_n("diag", 2)
        psm = k.ps[6]
        b_psm = k.b_ps[6]
        for kk in range(KD):
            s = kk % 2
            P.dma("pool", wa[s][:], k.w_ada[l, kk * 128:(kk + 1) * 128, :], w=[b_wa[s]])
            for j in range(72):
                P.pe(lambda e, o=psm[:, j * 2:(j + 1) * 2], a=wa[s][:, j * 128:(j + 1) * 128], r_=k.s_fm[:, kk, :],
                     st=(kk == 0 and j == 0), sp=(kk == KD - 1):
                     e.matmul(o, lhsT=a, rhs=r_, start=st, stop=sp, skip_group_check=True), [b_wa[s], k.b_small], [b_psm])
        psm3 = psm[:, 0:144].rearrange("p (j w) -> p j w", w=2)
        for w in range(2):
            _tt(P, k.m_fm[:, :, w], psm3[:, :, w], k.bfm[:, l * 72:(l + 1) * 72], ALU.add, [b_psm, k.b_small], [k.b_mod])
        for s in range(3):
            wt = 1.0 if s == 1 else 0.5
            for w in range(2):
                gp = k.gpre[:, (l * 3 + s) * KD:(l * 3 + s + 1) * KD]
                go = k.gpost[:, (l * 3 + s) * KD:(l * 3 + s + 1) * KD]
                _stt(P, k.sc1p[:, s, w, :], k.m_fm[:, (3 * s + 1) * KD:(3 * s + 2) * KD, w], 1.0, gp, ALU.add, ALU.mult,
                     [k.b_mod, k.b_small], [k.b_mod])
                _stt(P, k.gatev[:, s, w, :], k.m_fm[:, (3 * s + 2) * KD:(3 * s + 3) * KD, w], wt, go, ALU.mult, ALU.mult,
                     [k.b_mod, k.b_small], [k.b_mod])
        n = 0
        for s in range(3):
            for w in range(2):
                for half in range(2):
                    pb = k.ps[4 + half]
                    bpb = k.b_ps[4 + half]
                    for q in range(4):
                        kk = half * 4 + q
                        dg = diag[n % 2]
                        _ts(P, dg[:], k.ident_f, k.gatev[:, s, w, kk:kk + 1], None, ALU.mult, None,
                            [k.b_mod, k.b_cst], [b_diag[n % 2]])
                        _mm(P, pb[:, q * 128:(q + 1) * 128], k.ones_f, dg[:], True, True, [b_diag[n % 2], k.b_cst], [bpb])
                        n += 1
                    _act(P, k.G[:, s * 2 + w, half * 512:(half + 1) * 512], pb[:], AF.Copy, [bpb], [k.b_G])
        P.flush()


def premod_tile(k, t, slot, hT_dst, b_hT, xn, b_xn, ss, b_ss, junk, b_junk):
    P = k.P
    w = 1 if t < NCT else 0
    xt = k.x[:, t, :]
    _act(P, junk, xt, AF.Square, [k.b_x[t]], [b_junk, b_ss], accum_out=ss)
    _rstd(P, ss, 1, 1.0 / D, EPS, [b_ss], [b_ss])
    _ts(P, xn, xt, ss, None, ALU.mult, None, [k.b_x[t], b_ss], [b_xn])
    for kk in range(KD):
        _tp(P, k.ps_tp[:, kk * 128:(kk + 1) * 128], xn[:, kk * 128:(kk + 1) * 128], k.ident_b[:], [b_xn, k.b_cst], [k.b_tp])
    for kk in range(KD):
        _act(P, hT_dst[:, kk, :], k.ps_tp[:, kk * 128:(kk + 1) * 128], AF.Identity, [k.b_tp, k.b_mod], [b_hT],
             bias=k.m_fm[:, 3 * slot * KD + kk, w:w + 1], scale=k.sc1p[:, slot, w, kk:kk + 1])


def post_tile(k, t, slot, ysb, b_ysb, ss2, b_ss2):
    P = k.P
    w = 1 if t < NCT else 0
    _tt(P, ss2[:, 0:1], ss2[:, 0:1], ss2[:, 1:2], ALU.add, [b_ss2], [b_ss2])
    _rstd(P, ss2[:, 0:1], 1, 1.0 / D, EPS, [b_ss2], [b_ss2])
    _stt(P, ysb, ysb, ss2[:, 0:1], k.G[:, slot * 2 + w, :], ALU.mult, ALU.mult, [b_ysb, b_ss2, k.b_G], [b_ysb])
    _tt(P, k.x[:, t, :], k.x[:, t, :], ysb, ALU.add, [k.b_x[t], b_ysb], [k.b_x[t]])


def ffn_phase(k, l, f, slot, tiles):
    nc, P = k.nc, k.P
    FB = 2
    nfb = NF // FB
    with ExitStack() as ph:
        S = lambda n, sh, dt: k.S(n, sh, dt, ph)
        wd = S("wd", [128, NF, D], BF16)
        b_wd = P.bufs_n("wd", NF // 2)
        wgs = [S("wg%d" % i, [128, KD, FB * 128], BF16) for i in range(2)]
        wus = [S("wu%d" % i, [128, KD, FB * 128], BF16) for i in range(2)]
        b_wg = P.bufs_n("wg", 2)
        b_wu = P.bufs_n("wu", 2)
        hTs = [S("hT%d" % i, [128, KD, 512], BF16) for i in range(2)]
        b_hTs = [P.bufs_n("hT%d_" % i, 4) for i in range(2)]
        aT = S("aT", [128, NF, 512], BF16)
        b_aT = P.bufs_n("aT", NF)
        xn = [S("xn%d" % i, [128, D], BF16) for i in range(2)]
        b_xn = P.bufs_n("xn", 2)
        junk = S("junk", [128, D], BF16)
        b_junk = P.buf("junk")
        ss = [S("ss%d" % i, [128, 1], F32) for i in range(2)]
        b_ss = P.bufs_n("ss", 2)
        sg = [S("sg%d" % i, [128, 512], BF16) for i in range(2)]
        b_sg = P.bufs_n("sg", 2)
        ysb = [S("ysb%d" % i, [128, D], F32) for i in range(1)] * 2
        b_ysb = P.bufs_n("ysb", 1) * 2
        ss2 = [S("ss2%d" % i, [128, 2], F32) for i in range(2)]
        b_ss2 = P.bufs_n("ss2", 2)

        wd_src = k.wd[l, f]

        def load_wd(c):
            P.dma("pool", wd[:, 2 * c:2 * c + 2, :], wd_src[c * 256:(c + 1) * 256, :].rearrange("(c p) n -> p c n", p=128),
                  w=[b_wd[c]])

        blocks = [tiles[i:i + 4] for i in range(0, len(tiles), 4)]
        nblk = len(blocks)

        b_wscr = [P.bufs_n("wscr%d_" % i, nfb) for i in range(2)]

        def load_w(idx):
            fb = idx % nfb
            s = idx % 2
            if idx < nfb:
                P.dma("pool", wgs[s][:], k.wg[l, f, :, fb * 256:(fb + 1) * 256].rearrange("(k p) n -> p k n", p=128), w=[b_wg[s]])
                P.dma("pool", wus[s][:], k.wu[l, f, :, fb * 256:(fb + 1) * 256].rearrange("(k p) n -> p k n", p=128), w=[b_wu[s]])
                P.dma("sp", k.wscr[0][fb], wgs[s][:].rearrange("p a b -> p (a b)"), r=[b_wg[s]], w=[b_wscr[0][fb]])
                P.dma("sp", k.wscr[1][fb], wus[s][:].rearrange("p a b -> p (a b)"), r=[b_wu[s]], w=[b_wscr[1][fb]])
            else:
                P.dma("sp", wgs[s][:].rearrange("p a b -> p (a b)"), k.wscr[0][fb], r=[b_wscr[0][fb]], w=[b_wg[s]])
                P.dma("sp", wus[s][:].rearrange("p a b -> p (a b)"), k.wscr[1][fb], r=[b_wscr[1][fb]], w=[b_wu[s]])

        total = nblk * nfb
        load_w(0)
        it = 0
        nt_state = [0]

        def premod_blk_tile(bi, j):
            t = blocks[bi][j]
            n_ = nt_state[0]
            premod_tile(k, t, slot, hTs[bi % 2][:, :, j * 128:(j + 1) * 128], b_hTs[bi % 2][j], xn[n_ % 2][:], b_xn[n_ % 2],
                        ss[n_ % 2][:], b_ss[n_ % 2], junk[:], b_junk)
            nt_state[0] += 1

        for j in range(len(blocks[0])):
            premod_blk_tile(0, j)
        for bi, blk in enumerate(blocks):
            ntok = len(blk) * 128
            hT = hTs[bi % 2]
            hbufs = [b_hTs[bi % 2][j] for j in range(len(blk))]
            for fb in range(nfb):
                if it + 1 < total:
                    load_w(it + 1)
                if bi == 0:
                    load_wd(fb)
                s = it % 2
                for ci in range(FB):
                    c = fb * FB + ci
                    pset = (it * FB + ci) % 2
                    pg, pu = k.ps[pset * 2], k.ps[pset * 2 + 1]
                    bpg, bpu = k.b_ps[pset * 2], k.b_ps[pset * 2 + 1]
                    for kk in range(KD):
                        _mm(P, pg[:, 0:ntok], wgs[s][:, kk, ci * 128:(ci + 1) * 128], hT[:, kk, 0:ntok], kk == 0, kk == KD - 1,
                            [b_wg[s]] + hbufs, [bpg])
                    for kk in range(KD):
                        _mm(P, pu[:, 0:ntok], wus[s][:, kk, ci * 128:(ci + 1) * 128], hT[:, kk, 0:ntok], kk == 0, kk == KD - 1,
                            [b_wu[s]] + hbufs, [bpu])
                    _act(P, sg[pset][:, 0:ntok], pg[:, 0:ntok], AF.Silu, [bpg], [b_sg[pset]])
                    _tt(P, aT[:, c, 0:ntok], sg[pset][:, 0:ntok], pu[:, 0:ntok], ALU.mult, [b_sg[pset], bpu], [b_aT[c]])
                it += 1
                if bi + 1 < nblk and fb % 2 == 1 and fb // 2 < len(blocks[bi + 1]):
                    premod_blk_tile(bi + 1, fb // 2)
            for j, t in enumerate(blk):
                yi = t % 2
                for half in range(2):
                    py = k.ps[4 + half]
                    bpy = k.b_ps[4 + half]
                    for c in range(NF):
                        _mm(P, py[:], aT[:, c, j * 128:(j + 1) * 128], wd[:, c, half * 512:(half + 1) * 512], c == 0, c == NF - 1,
                            [b_aT[c], b_wd[c // 2]], [bpy])
                    _cp(P, ysb[yi][:, half * 512:(half + 1) * 512], py[:], [bpy], [b_ysb[yi]])
                    _act(P, junk[:, 0:512], ysb[yi][:, half * 512:(half + 1) * 512], AF.Square, [b_ysb[yi]], [b_junk, b_ss2[yi]],
                         accum_out=ss2[yi][:, half:half + 1])
                post_tile(k, t, slot, ysb[yi][:], b_ysb[yi], ss2[yi], b_ss2[yi])
        P.flush()


TBLK = [(0, 256), (256, 512), (768, 512), (1280, 512), (1792, 512)]
C_ID, C_ONE, C_LE, C_GE, C_LT, C_GT, C_MK, C_N = 0, 1, 2, 3, 4, 5, 6, 13


def _rawcol(tok):
    return tok + 2 if tok < 256 else tok + 6


def _memset(P, ap, val, w, eng="dve"):
    return P._add(eng, lambda e: e.memset(ap, val), (), w)


def mix_phase(k, l):
    nc, P = k.nc, k.P
    last = l == DEPTH - 1
    with ExitStack() as mx:
        S = lambda n, sh, dt, stack=mx: k.S(n, sh, dt, stack)
        m = K()
        m.gb = S("gb", [128, NT, 16], F32)
        m.nb = S("nbeta", [128, NT, 8], F32)
        m.EX = S("EX", [128, NT, 24], F32)
        m.b_gb = P.buf("gb")
        m.b_hscr = P.bufs_n("hscr", len(TBLK))
        with ExitStack() as ph:
            Sp = lambda n, sh, dt: k.S(n, sh, dt, ph)
            xn = [Sp("xn%d" % i, [128, D], BF16) for i in range(2)]
            b_xn = P.bufs_n("xn", 2)
            junk = Sp("junk", [128, D], BF16)
            b_junk = P.buf("junk")
            ss = [Sp("ss%d" % i, [128, 1], F32) for i in range(2)]
            b_ss = P.bufs_n("ss", 2)
            hblk = [Sp("hblkA%d" % i, [128, KD, 512], BF16) for i in range(2)]
            b_hblk = [P.bufs_n("hblkA%d_" % i, 4) for i in range(2)]
            wab = Sp("wab", [128, KD, 16], BF16)
            b_wab = P.buf("wab")
            wk1 = Sp("wk1", [128, NT, 8], F32)
            wk2 = Sp("wk2", [128, NT, 8], F32)
            b_wk = P.buf("wk")
            P.dma("pool", wab[:], k.w_in[l, :, OFF_A:OFF_A + 16].rearrange("(k p) n -> p k n", p=128), w=[b_wab])
            pab = k.ps[6]
            bpab = k.b_ps[6]
            for bi, (t0, nt) in enumerate(TBLK):
                s = bi % 2
                for j in range(nt // 128):
                    t = t0 // 128 + j
                    premod_tile(k, t, 1, hblk[s][:, :, j * 128:(j + 1) * 128], b_hblk[s][j], xn[t % 2][:], b_xn[t % 2],
                                ss[t % 2][:], b_ss[t % 2], junk[:], b_junk)
                    for kk in range(KD):
                        _mm(P, pab[:, t * 16:(t + 1) * 16], hblk[s][:, kk, j * 128:(j + 1) * 128], wab[:, kk, :], kk == 0, kk == KD - 1,
                            [b_hblk[s][j], b_wab], [bpab])
                P.dma("sp", k.hscr[:, :, t0:t0 + nt], hblk[s][:, :, 0:nt], r=b_hblk[s][0:nt // 128], w=[m.b_hscr[bi]])
            gbf = m.gb[:].rearrange("p a b -> p (a b)")
            _cp(P, gbf, pab[:, 0:NT * 16], [bpab], [m.b_gb])
            a3 = m.gb[:, :, 0:8]
            b3 = m.gb[:, :, 8:16]
            gd = Sp("gd", [128, 2, NT * 8], F32)
            P.dma("sp", gd[:].rearrange("p a b -> p (a b)"), k.gdnc_d[:, l * 2 * NT * 8:(l + 1) * 2 * NT * 8], w=[b_wk])
            dtb3 = gd[:, 0, :].rearrange("p (a b) -> p a b", b=8)
            alg3 = gd[:, 1, :].rearrange("p (a b) -> p a b", b=8)
            rw = [m.b_gb, b_wk, k.b_small]
            _tt(P, wk1[:], a3, dtb3, ALU.add, rw, [b_wk])
            _stt(P, wk2[:], wk1[:], -1.0, wk1[:], ALU.mult, ALU.max, rw, [b_wk])
            _act(P, wk2[:], wk2[:], AF.Exp, rw, [b_wk], scale=-1.0)
            _act(P, wk2[:], wk2[:], AF.Ln, rw, [b_wk], bias=1.0)
            _ts(P, wk1[:], wk1[:], 0.0, None, ALU.max, None, rw, [b_wk])
            _tt(P, wk1[:], wk1[:], wk2[:], ALU.add, rw, [b_wk])
            _act(P, wk2[:], alg3, AF.Exp, rw, [b_wk])
            _stt(P, a3, wk1[:], -1.0, wk2[:], ALU.mult, ALU.mult, rw, [m.b_gb])
            _act(P, b3, b3, AF.Sigmoid, rw, [m.b_gb])
            _ts(P, m.nb[:], b3, -1.0, None, ALU.mult, None, rw, [m.b_gb])
            pex = k.ps[5]
            bpex = k.b_ps[5]
            cs = k.cstt
            for t in range(NT):
                base = t * 24
                gf = m.gb[:, t, 0:4]
                gbw = m.gb[:, t, 4:8]
                rr = [m.b_gb, k.b_cst]
                _mm(P, pex[:, base + 0:base + 4], cs[:, C_LE, :], gf, True, True, rr, [bpex])
                _mm(P, pex[:, base + 4:base + 8], cs[:, C_GE, :], gbw, True, True, rr, [bpex])
                _mm(P, pex[:, base + 8:base + 12], cs[:, C_GT, :], gf, True, True, rr, [bpex])
                _mm(P, pex[:, base + 12:base + 16], cs[:, C_LT, :], gbw, True, True, rr, [bpex])
                _mm(P, pex[:, base + 16:base + 24], cs[:, C_ONE, :], m.gb[:, t, 0:8], True, True, rr, [bpex])
            _act(P, m.EX[:].rearrange("p a b -> p (a b)"), pex[:, 0:NT * 24], AF.Exp, [bpex], [m.b_gb])
            P.flush()
        for h in range(4):
            gdn_head(k, m, l, h)
        mla_group(k, m, l)
    out_proj(k, l)


def gdn_head(k, m, l, h):
    nc, P = k.nc, k.P
    cs = k.cstt
    with ExitStack() as hd:
        S = lambda n, sh, dt, stack=hd: k.S(n, sh, dt, stack)
        cT = [S("cT%d" % i, [128, T], BF16) for i in range(3)]
        b_cT = [P.bufs_n("cT%d_" % i, len(TBLK)) for i in range(3)]
        kn_tok = S("kn_tok", [128, NT, 128], BF16)
        v_tok = S("v_tok", [128, NT, 128], BF16)
        b_tok = P.buf("tok")
        zs = S("zs", [128, NT, 128], BF16)
        b_zs = P.buf("zs")
        o_acc = S("o_acc", [128, NT, 128], F32)
        b_o = P.bufs_n("oacc", NT)
        with ExitStack() as g1:
            Sp = lambda n, sh, dt: k.S(n, sh, dt, g1)
            raw = [Sp("raw%d" % i, [128, T + 8], BF16) for i in range(3)]
            b_raw = P.bufs_n("raw", 3)
            acc = Sp("acc", [128, T], F32)
            b_acc = P.buf("acc")
            wq = [Sp("wq%d" % i, [128, KD, 128], BF16) for i in range(4)]
            b_wq = P.bufs_n("wq", 4)
            sqall = Sp("sqall", [128, T], BF16)
            b_sqall = P.buf("sqall")
            rsall = Sp("rsall", [128, T], F32)
            b_rsall = P.buf("rsall")
            for i, c in enumerate((h, 4 + h, 8 + h, 12 + h)):
                P.dma("pool", wq[i][:], k.w_in[l, :, c * 128:(c + 1) * 128].rearrange("(k p) n -> p k n", p=128), w=[b_wq[i]])
            for i in range(3):
                _memset(P, raw[i][:], 0.0, [b_raw[i]])
            hblk = [Sp("hblkG%d" % i, [128, KD, 512], BF16) for i in range(2)]
            b_hblk = P.bufs_n("hblkG", 2)
            for bi, (t0, nt) in enumerate(TBLK):
                s = bi % 2
                P.dma("sp", hblk[s][:, :, 0:nt], k.hscr[:, :, t0:t0 + nt], r=[m.b_hscr[bi]], w=[b_hblk[s]])
                for i in range(3):
                    pp = k.ps[i]
                    bpp = k.b_ps[i]
                    for kk in range(KD):
                        _mm(P, pp[:, 0:nt], wq[i][:, kk, :], hblk[s][:, kk, 0:nt], kk == 0, kk == KD - 1, [b_wq[i], b_hblk[s]], [bpp])
                    rc = _rawcol(t0)
                    _act(P, raw[i][:, rc:rc + nt], pp[:, 0:nt], AF.Copy, [bpp], [b_raw[i]])
                pp = k.ps[3]
                bpp = k.b_ps[3]
                ntl = nt // 128
                for q in range(ntl):
                    for kk in range(KD):
                        _mm(P, pp[:, q * 128:(q + 1) * 128], hblk[s][:, kk, q * 128:(q + 1) * 128], wq[3][:, kk, :], kk == 0, kk == KD - 1,
                            [b_hblk[s], b_wq[3]], [bpp])
                tt0 = t0 // 128
                _act(P, zs[:, tt0:tt0 + ntl, :].rearrange("p a b -> p (a b)"), pp[:, 0:ntl * 128], AF.Silu, [bpp], [b_zs])
            n = 0
            for i in range(3):
                c = (h, 4 + h, 8 + h)[i]
                for (s0, s1, base) in ((0, 256, 0), (256, T, 260)):
                    ln = s1 - s0
                    for j in range(5):
                        cw = k.convw[:, (l * 12 + c) * 5 + j:(l * 12 + c) * 5 + j + 1]
                        src = raw[i][:, base + j:base + j + ln]
                        if j == 0:
                            _ts(P, acc[:, s0:s1], src, cw, None, ALU.mult, None, [b_raw[i], k.b_small], [b_acc])
                        else:
                            _stt(P, acc[:, s0:s1], src, cw, acc[:, s0:s1], ALU.mult, ALU.add, [b_raw[i], k.b_small, b_acc], [b_acc])
                if i == 2:
                    _act(P, cT[2][:], acc[:], AF.Silu, [b_acc], b_cT[2])
                else:
                    _act(P, acc[:], acc[:], AF.Silu, [b_acc], [b_acc])
                    scale = (128.0 ** -0.5) if i == 0 else 1.0
                    _act(P, sqall[:], acc[:], AF.Square, [b_acc], [b_sqall])
                    for bi, (t0, nt) in enumerate(TBLK):
                        pp = k.ps[2 + bi]
                        bpp = k.b_ps[2 + bi]
                        _mm(P, pp[:, 0:nt], k.ones_b[:], sqall[:, t0:t0 + nt], True, True, [b_sqall, k.b_cst], [bpp])
                        _ts(P, rsall[:, t0:t0 + nt], pp[:, 0:nt], EPS, None, ALU.add, None, [bpp], [b_rsall])
                    _act(P, rsall[:], rsall[:], AF.Sqrt, [b_rsall], [b_rsall])
                    P.dve(lambda e: e.reciprocal(out=rsall[:], in_=rsall[:]), [b_rsall], [b_rsall])
                    _stt(P, cT[i][:], acc[:], scale, rsall[:], ALU.mult, ALU.mult, [b_acc, b_rsall], b_cT[i])
            for (src_i, dst) in ((1, kn_tok), (2, v_tok)):
                for t0 in range(0, NT, 8):
                    ntl = min(8, NT - t0)
                    for q in range(ntl):
                        t = t0 + q
                        _tp(P, k.ps_tp[:, q * 128:(q + 1) * 128], cT[src_i][:, t * 128:(t + 1) * 128], k.ident_b[:],
                            b_cT[src_i] + [k.b_cst], [k.b_tp])
                    _cp(P, dst[:, t0:t0 + ntl, :].rearrange("p a b -> p (a b)"), k.ps_tp[:, 0:ntl * 128], [k.b_tp], [b_tok])
            for t in range(NT):
                _memset(P, o_acc[:, t, :], 0.0, [b_o[t]])
            import os
            if k.dbgmode and h == int(os.environ.get("DBG_HEAD", "0")):
                for i in range(3):
                    P.dma("sp", k.dbg3[i], cT[i][:], r=b_cT[i])
                P.dma("sp", k.dbg5[:, 0:NT * 16], m.gb[:].rearrange("p a b -> p (a b)"), r=[m.b_gb])
                P.dma("sp", k.dbg5[:, NT * 16:NT * 40], m.EX[:].rearrange("p a b -> p (a b)"), r=[m.b_gb])
            P.flush()
        with ExitStack() as g2:
            Sp = lambda n, sh, dt: k.S(n, sh, dt, g2)
            NJ = 3
            NS = 4
            lhsE = [[Sp("lhsE%d_%d" % (d, j), [128, 128], F32) for j in range(NJ)] for d in range(2)]
            DMi = [Sp("DMi%d" % j, [128, 256], F32) for j in range(NJ)]
            DMs = [Sp("DMs%d" % j, [128, 256], F32) for j in range(NJ)]
            MM_ = [Sp("MM%d" % j, [128, 512], F32) for j in range(NJ)]
            BB_ = [[Sp("BB%d_%d" % (j, i), [128, 512], F32) for i in range(2)] for j in range(NJ)]
            XX_ = [Sp("XX%d" % j, [128, 512], F32) for j in range(NJ)]
            TT_ = [Sp("TT%d" % j, [128, 512], F32) for j in range(NJ)]
            b_MM_ = [P.buf("MM%d" % j) for j in range(NJ)]
            b_BB_ = [[P.buf("BB%d_%d" % (j, i)) for i in range(2)] for j in range(NJ)]
            tp32 = k.ps_tp[:].bitcast(F32)
            JB = [((k.ps[0], k.b_ps[0]), (k.ps[2], k.b_ps[2])), ((k.ps[3], k.b_ps[3]), (k.ps[4], k.b_ps[4])),
                  ((k.ps[5], k.b_ps[5]), (tp32, k.b_tp))]
            R = [[Sp("R%d_%d" % (d, j), [128, 256], F32) for j in range(NJ)] for d in range(2)]
            wtok = [[Sp("wtok%d_%d" % (d, j), [128, 128], F32) for j in range(NJ)] for d in range(2)]
            bj = lambda nm: [P.buf("%s%d" % (nm, j)) for j in range(NJ)]
            b_XX_, b_TT_, b_DM = bj("XX"), bj("TT"), bj("DM")
            Bj = lambda nm: [[P.buf("%s%d_%d" % (nm, d, j)) for j in range(NJ)] for d in range(2)]
            b_lhsE, b_R, b_wtok = Bj("lhsE"), Bj("R"), Bj("wtok")
            QK = [[Sp("QK%d_%d" % (d, s), [128, 128], BF16) for s in range(NS)] for d in range(2)]
            kdec = [[Sp("kdec%d_%d" % (d, s), [128, 128], F32) for s in range(NS)] for d in range(2)]
            u0b = [[Sp("u0b%d_%d" % (d, s), [128, 128], F32) for s in range(NS)] for d in range(2)]
            nwT = [[Sp("nwT%d_%d" % (d, s), [128, 128], F32) for s in range(NS)] for d in range(2)]
            B = lambda nm: [[P.buf("%s%d_%d" % (nm, d, s)) for s in range(NS)] for d in range(2)]
            b_QK, b_kdec, b_u0b, b_nwT = B("QK"), B("kdec"), B("u0b"), B("nwT")
            Sst = [Sp("Sst%d" % d, [128, 128], F32) for d in range(2)]
            Sbf = [Sp("Sbf%d" % d, [128, 128], BF16) for d in range(2)]
            Usb = [Sp("Usb%d" % d, [128, 128], BF16) for d in range(2)]
            Usf = [Sp("Usf%d" % d, [128, 128], F32) for d in range(2)]
            b_Uf = P.bufs_n("Usf", 2)
            b_S = P.bufs_n("Sst", 2)
            b_Sbf = P.bufs_n("Sbf", 2)
            b_U = P.bufs_n("Usb", 2)
            for d in range(2):
                _memset(P, Sst[d][:], 0.0, [b_S[d]])
                _memset(P, Sbf[d][:], 0.0, [b_Sbf[d]])
            order = [list(range(NT)), [1, 0] + list(range(NT - 1, 1, -1))]
            cb = lambda c: [b_cT[0][_blk_of(c)], b_cT[1][_blk_of(c)]]

            def pre(step, j):
                s = step % NS
                cc = [order[0][step], order[1][step]]
                MM, XX, TT = MM_[j], XX_[j], TT_[j]
                b_MM, b_XX, b_TT = b_MM_[j], b_XX_[j], b_TT_[j]
                BB, b_BB = BB_[j], b_BB_[j]
                pE, bpE = JB[j][0]
                pA, bpA = JB[j][1]
                pB, bpB = pE, bpE
                MM3 = MM[:].rearrange("p (a b) -> p a b", a=4)

                def mask_level(lv):
                    mk = cs[:, C_MK + lv, :].unsqueeze(1).to_broadcast([128, 4, 128])
                    _tt(P, BB[lv % 2][:].rearrange("p (a b) -> p a b", a=4), MM3, mk, ALU.mult, [b_MM, k.b_cst], [b_BB[lv % 2]], eng="pool")

                for d in range(2):
                    c = cc[d]
                    gcol = m.gb[:, c, d * 4 + h:d * 4 + h + 1]
                    msk = cs[:, C_GT, :] if d == 0 else cs[:, C_LT, :]
                    _ts(P, lhsE[d][j][:], msk, gcol, None, ALU.mult, None, [m.b_gb, k.b_cst], [b_lhsE[d][j]])
                yield
                for d in range(2):
                    c = cc[d]
                    rhs = cs[:, C_LE, :] if d == 0 else cs[:, C_GE, :]
                    _mm(P, pE[:, d * 128:(d + 1) * 128], lhsE[d][j][:], rhs, True, True, [b_lhsE[d][j], k.b_cst], [bpE])
                    knc = cT[1][:, c * 128:(c + 1) * 128]
                    _mm(P, pE[:, 256 + d * 128:256 + (d + 1) * 128], knc, knc, True, True, cb(c), [bpE])
                yield
                _act(P, DMi[j][:], pE[:, 0:256], AF.Exp, [bpE], [b_DM[j]])
                yield
                inc2 = cs[:, C_LE:C_GE + 1, :].rearrange("p a b -> p (a b)")
                str2 = cs[:, C_LT:C_GT + 1, :].rearrange("p a b -> p (a b)")
                _tt(P, DMs[j][:], DMi[j][:], str2, ALU.mult, [b_DM[j], k.b_cst], [b_DM[j]])
                _tt(P, DMi[j][:], DMi[j][:], inc2, ALU.mult, [b_DM[j], k.b_cst], [b_DM[j]])
                for d in range(2):
                    c = cc[d]
                    bcol = m.gb[:, c, 8 + d * 4 + h:8 + d * 4 + h + 1]
                    _stt(P, MM[:, d * 128:(d + 1) * 128], pE[:, 256 + d * 128:256 + (d + 1) * 128], bcol, DMs[j][:, d * 128:(d + 1) * 128],
                         ALU.mult, ALU.mult, [bpE, m.b_gb, b_DM[j]], [b_MM])
                yield
                for d in range(2):
                    c = cc[d]
                    knc = cT[1][:, c * 128:(c + 1) * 128]
                    qnc = cT[0][:, c * 128:(c + 1) * 128]
                    _mm(P, pE[:, d * 128:(d + 1) * 128], knc, qnc, True, True, cb(c), [bpE])
                for d in range(2):
                    _tp(P, pA[:, d * 128:(d + 1) * 128], MM[:, d * 128:(d + 1) * 128], k.ident_f, [b_MM, k.b_cst], [bpA])
                yield
                for d in range(2):
                    _tt(P, QK[d][s][:], pE[:, d * 128:(d + 1) * 128], DMi[j][:, d * 128:(d + 1) * 128], ALU.mult, [bpE, b_DM[j]],
                        [b_QK[d][s]])
                _cp(P, MM[:, 256:512], pA[:, 0:256], [bpA], [b_MM], eng="act")
                yield
                mask_level(0)
                for d in range(2):
                    c = cc[d]
                    eG = m.EX[:, c, d * 4 + h:d * 4 + h + 1]
                    ekd = m.EX[:, c, 8 + d * 4 + h:8 + d * 4 + h + 1]
                    _act(P, R[d][j][:, 128:256], kn_tok[:, c, :], AF.Copy, [b_tok, m.b_gb], [b_R[d][j]], scale=eG)
                    _act(P, kdec[d][s][:], kn_tok[:, c, :], AF.Copy, [b_tok, m.b_gb], [b_kdec[d][s]], scale=ekd)
                yield
                id4 = cs[:, C_ID, :].unsqueeze(1).to_broadcast([128, 4, 128])
                _tt(P, XX[:].rearrange("p (a b) -> p a b", a=4), id4, BB[0][:].rearrange("p (a b) -> p a b", a=4), ALU.subtract,
                    [b_BB[0], k.b_cst], [b_XX])
                mask_level(1)
                for d in range(2):
                    c = cc[d]
                    _cp(P, R[d][j][:, 0:128], v_tok[:, c, :], [b_tok], [b_R[d][j]], eng="pool")
                yield
                for lv in range(1, 7):
                    lastlv = lv == 6
                    Bl = BB[lv % 2]
                    bBl = b_BB[lv % 2]
                    for d in range(2):
                        _mm(P, pA[:, d * 128:(d + 1) * 128], Bl[:, (2 + d) * 128:(3 + d) * 128], XX[:, d * 128:(d + 1) * 128], True, True,
                            [bBl, b_XX], [bpA])
                    for d in range(2):
                        _tp(P, pA[:, (2 + d) * 128:(3 + d) * 128], XX[:, d * 128:(d + 1) * 128], k.ident_f, [b_XX, k.b_cst], [bpA])
                    yield
                    _cp(P, TT[:, 0:512], pA[:, 0:512], [bpA], [b_TT], eng="act")
                    if not lastlv:
                        mask_level(lv + 1)
                    yield
                    for d in range(2):
                        _mm(P, pB[:, d * 128:(d + 1) * 128], TT[:, (2 + d) * 128:(3 + d) * 128], TT[:, d * 128:(d + 1) * 128], True, True,
                            [b_TT], [bpB])
                    yield
                    _tt(P, XX[:, 0:256], XX[:, 0:256], pB[:, 0:256], ALU.subtract, [b_XX, bpB], [b_XX])
                    yield
                for d in range(2):
                    _mm(P, pA[:, d * 256:(d + 1) * 256], XX[:, d * 128:(d + 1) * 128], R[d][j][:], True, True, [b_XX, b_R[d][j]], [bpA])
                yield
                for d in range(2):
                    c = cc[d]
                    bcol = m.gb[:, c, 8 + d * 4 + h:8 + d * 4 + h + 1]
                    _act(P, wtok[d][j][:], pA[:, d * 256 + 128:d * 256 + 256], AF.Copy, [bpA], [b_wtok[d][j]], scale=-1.0)
                    _ts(P, u0b[d][s][:], pA[:, d * 256:d * 256 + 128], bcol, None, ALU.mult, None, [bpA, m.b_gb], [b_u0b[d][s]])
                yield
                for d in range(2):
                    _tp(P, pB[:, d * 128:(d + 1) * 128], wtok[d][j][:], k.ident_f, [b_wtok[d][j], k.b_cst], [bpB])
                yield
                for d in range(2):
                    _cp(P, nwT[d][s][:], pB[:, d * 128:(d + 1) * 128], [bpB], [b_nwT[d][s]], eng=("act" if d == 0 else "dve"))
                yield

            def scan(step):
                s = step % NS
                pSs = [k.ps[6], k.ps[1]]
                bpSs = [k.b_ps[6], k.b_ps[1]]
                for d in range(2):
                    c = order[d][step]
                    pS, bpS = pSs[d], bpSs[d]
                    qnc = cT[0][:, c * 128:(c + 1) * 128]
                    _mm(P, pS[:, 0:128], nwT[d][s][:], Sst[d][:], True, True, [b_nwT[d][s], b_S[d]], [bpS])
                    _mm(P, pS[:, 128:256], qnc, Sbf[d][:], True, True, [b_cT[0][_blk_of(c)], b_Sbf[d]], [bpS])
                yield
                for d in range(2):
                    c = order[d][step]
                    pS, bpS = pSs[d], bpSs[d]
                    bcol = m.gb[:, c, 8 + d * 4 + h:8 + d * 4 + h + 1]
                    _stt(P, Usf[d][:], pS[:, 0:128], bcol, u0b[d][s][:], ALU.mult, ALU.add, [bpS, m.b_gb, b_u0b[d][s]], [b_Uf[d]])
                yield
                for d in range(2):
                    _cp(P, Usb[d][:], Usf[d][:], [b_Uf[d]], [b_U[d]], eng="act")
                    pS, bpS = pSs[d], bpSs[d]
                    _mm(P, pS[:, 384:512], kdec[d][s][:], Usf[d][:], True, True, [b_kdec[d][s], b_Uf[d]], [bpS])
                yield
                for d in range(2):
                    pS, bpS = pSs[d], bpSs[d]
                    _mm(P, pS[:, 256:384], QK[d][s][:], Usb[d][:], True, True, [b_QK[d][s], b_U[d]], [bpS])
                yield
                for d in range(2):
                    c = order[d][step]
                    pS, bpS = pSs[d], bpSs[d]
                    eG = m.EX[:, c, d * 4 + h:d * 4 + h + 1]
                    cdec = m.EX[:, c, 16 + d * 4 + h:16 + d * 4 + h + 1]
                    oc = o_acc[:, c, :]
                    _stt(P, Sst[d][:], Sst[d][:], cdec, pS[:, 384:512], ALU.mult, ALU.add, [bpS, m.b_gb, b_S[d]], [b_S[d]])
                    _stt(P, oc, pS[:, 128:256], eG, oc, ALU.mult, ALU.add, [bpS, m.b_gb, b_o[c]], [b_o[c]])
                    _tt(P, oc, pS[:, 256:384], oc, ALU.add, [bpS, b_o[c]], [b_o[c]])
                yield
                for d in range(2):
                    _cp(P, Sbf[d][:], Sst[d][:], [b_S[d]], [b_Sbf[d]], eng="act")
                yield

            pres = {}
            pre_step = {}
            done_pre = set()
            nxt = 0
            scan_pos = 0
            scan_gen = None
            while scan_pos < NT:
                for j in range(NJ):
                    if j not in pres and nxt < NT and nxt < scan_pos + NS:
                        pres[j] = pre(nxt, j)
                        pre_step[j] = nxt
                        nxt += 1
                for j in list(pres):
                    try:
                        next(pres[j])
                    except StopIteration:
                        done_pre.add(pre_step[j])
                        del pres[j]
                if scan_gen is None and scan_pos in done_pre:
                    scan_gen = scan(scan_pos)
                if scan_gen is not None:
                    try:
                        next(scan_gen)
                    except StopIteration:
                        scan_gen = None
                        scan_pos += 1
            P.flush()
        with ExitStack() as g3:
            Sp = lambda n, sh, dt: k.S(n, sh, dt, g3)
            ssn = Sp("ssn", [128, NT], F32)
            b_ssn = P.buf("ssn")
            junk = Sp("junkg", [128, 128], BF16)
            b_junk = P.buf("junkg")
            tmpo = [Sp("tmpo%d" % i, [128, 128], F32) for i in range(2)]
            b_tmpo = P.bufs_n("tmpo", 2)
            obuf = Sp("obuf", [128, NT, 128], BF16)
            b_obuf = P.buf("obuf")
            import os
            if k.dbgmode and h == int(os.environ.get("DBG_HEAD", "0")):
                P.dma("sp", k.dbg4, o_acc[:].rearrange("p a b -> p (a b)"), r=b_o)
            sq3 = Sp("sq3", [128, NT, 128], F32)
            b_sq3 = P.buf("sq3")
            _act(P, sq3[:].rearrange("p a b -> p (a b)"), o_acc[:].rearrange("p a b -> p (a b)"), AF.Square, b_o, [b_sq3])
            P.dve(lambda e: e.reduce_sum(out=ssn[:], in_=sq3[:], axis=AX.X), [b_sq3], [b_ssn])
            _rstd(P, ssn[:], NT, 1.0 / 128, EPS, [b_ssn], [b_ssn])
            gnb = k.gnb[:, l * 128:(l + 1) * 128]
            _tt(P, sq3[:], o_acc[:], gnb.unsqueeze(1).to_broadcast([128, NT, 128]), ALU.mult, b_o + [k.b_small], [b_sq3])
            _tt(P, sq3[:], sq3[:], zs[:], ALU.mult, [b_sq3, b_zs], [b_sq3])
            _tt(P, obuf[:], sq3[:], ssn[:].unsqueeze(2).to_broadcast([128, NT, 128]), ALU.mult, [b_sq3, b_ssn], [b_obuf])
            P.dma("sp", k.oscr[:, h * 128:(h + 1) * 128].rearrange("(t p) c -> p t c", p=128), obuf[:], r=[b_obuf], w=[k.b_oscr])
            P.flush()


def _blk_of(c):
    tok = c * 128
    for bi, (t0, nt) in enumerate(TBLK):
        if t0 <= tok < t0 + nt:
            return bi
    raise ValueError


def mla_group(k, m, l):
    nc, P = k.nc, k.P
    SC = float(96.0 ** -0.5)
    with ExitStack() as ml:
        S = lambda n, sh, dt, stack=ml: k.S(n, sh, dt, stack)
        cn = S("cn", [128, 5, T], BF16)
        b_cn = P.bufs_n("cn", len(TBLK))
        Vall = S("Vall", [128, NT, 8, 65], BF16)
        b_V = P.buf("Vall")
        krr = S("krr", [96, T], BF16)
        b_krr = P.buf("krr")
        cos = S("cosq", [96, T], BF16)
        sin = S("sinq", [96, T], BF16)
        b_rope = P.buf("rope")
        P.dma("pool", cos[:], k.cosq[0:96, :], w=[b_rope])
        P.dma("pool", sin[:], k.sinq[0:96, :], w=[b_rope])
        with ExitStack() as p1:
            Sp = lambda n, sh, dt: k.S(n, sh, dt, p1)
            wc = Sp("wc", [128, KD, 640], BF16)
            b_wc = P.buf("wc")
            wkr = Sp("wkr", [128, KD, 2, 96], BF16)
            b_wkr = P.buf("wkr")
            wv = Sp("wv", [128, 2, 512], BF16)
            b_wv = P.buf("wv")
            rawc = [Sp("rawc%d" % i, [128, 5, 512], F32) for i in range(1)] * 2
            b_rawc = P.bufs_n("rawc", 1) * 2
            sqb = [Sp("sqc%d" % i, [128, 5, 512], BF16) for i in range(1)] * 2
            b_sqb = P.bufs_n("sqc", 1) * 2
            rsb = [Sp("rsc%d" % i, [128, 2, 512], F32) for i in range(1)] * 2
            b_rsb = P.bufs_n("rsc", 1) * 2
            hblk = [Sp("hblkM%d" % i, [128, KD, 512], BF16) for i in range(1)] * 2
            b_hblk = P.bufs_n("hblkM", 1) * 2
            t1 = Sp("t1", [96, 512], F32)
            t2 = Sp("t2", [96, 512], F32)
            b_t = P.buf("t12")
            P.dma("pool", wc[:], k.w_in[l, :, OFF_CQ:OFF_CQ + 640].rearrange("(k p) n -> p k n", p=128), w=[b_wc])
            _memset(P, wkr[:].rearrange("p a b c -> p (a b c)"), 0.0, [b_wkr])
            wsrc = k.w_in[l, :, OFF_KR:OFF_KR + 32].rearrange("(k p) n -> p k n", p=128)
            P.dma("pool", wkr[:, :, 0, 64:96], wsrc, w=[b_wkr])
            P.dma("pool", wkr[:, :, 1, 64:80], wsrc[:, :, 16:32], w=[b_wkr])
            P.dma("pool", wkr[:, :, 1, 80:96], wsrc[:, :, 0:16], w=[b_wkr])
            P.dma("pool", wv[:], k.w_ukv_v[l].rearrange("(k p) n -> p k n", p=128), w=[b_wv])
            _memset(P, Vall[:].rearrange("p a b c -> p (a b c)"), 1.0, [b_V])
            for bi, (t0, nt) in enumerate(TBLK):
                s = bi % 2
                hb = [b_hblk[s]]
                hT_ = hblk[s]
                P.dma("sp", hblk[s][:, :, 0:nt], k.hscr[:, :, t0:t0 + nt], r=[m.b_hscr[bi]], w=[b_hblk[s]])
                for c in range(5):
                    pp = k.ps[c % 2]
                    bpp = k.b_ps[c % 2]
                    for kk in range(KD):
                        _mm(P, pp[:, 0:nt], wc[:, kk, c * 128:(c + 1) * 128], hT_[:, kk, 0:nt], kk == 0, kk == KD - 1, [b_wc] + hb, [bpp])
                    _cp(P, rawc[s][:, c, 0:nt], pp[:, 0:nt], [bpp], [b_rawc[s]])
                    _act(P, sqb[s][:, c, 0:nt], rawc[s][:, c, 0:nt], AF.Square, [b_rawc[s]], [b_sqb[s]])
                for gi, (c0, c1, nfeat) in enumerate(((0, 3, 384.0), (3, 5, 256.0))):
                    pp = k.ps[2 + gi]
                    bpp = k.b_ps[2 + gi]
                    for c in range(c0, c1):
                        _mm(P, pp[:, 0:nt], k.ones_b[:], sqb[s][:, c, 0:nt], c == c0, c == c1 - 1, [b_sqb[s], k.b_cst], [bpp])
                    rs = rsb[s][:, gi, 0:nt]
                    _ts(P, rs, pp[:, 0:nt], 1.0 / nfeat, EPS, ALU.mult, ALU.add, [bpp], [b_rsb[s]])
                    _act(P, rs, rs, AF.Sqrt, [b_rsb[s]], [b_rsb[s]])
                    P.dve(lambda e, a=rs: e.reciprocal(out=a, in_=a), [b_rsb[s]], [b_rsb[s]])
                    for c in range(c0, c1):
                        gcol = (k.qn[:, l * 3 + c:l * 3 + c + 1] if gi == 0 else k.kvn[:, l * 2 + (c - 3):l * 2 + (c - 3) + 1])
                        _stt(P, cn[:, c, t0:t0 + nt], rawc[s][:, c, 0:nt], gcol, rs, ALU.mult, ALU.mult, [b_rawc[s], b_rsb[s], k.b_small],
                             [b_cn[bi]])
                pk, bpk = k.ps[4], k.b_ps[4]
                pks, bpks = k.ps[5], k.b_ps[5]
                for kk in range(KD):
                    _mm(P, pk[0:96, 0:nt], wkr[:, kk, 0, :], hT_[:, kk, 0:nt], kk == 0, kk == KD - 1, [b_wkr] + hb, [bpk])
                for kk in range(KD):
                    _mm(P, pks[0:96, 0:nt], wkr[:, kk, 1, :], hT_[:, kk, 0:nt], kk == 0, kk == KD - 1, [b_wkr] + hb, [bpks])
                _tt(P, t1[64:96, 0:nt], pk[64:96, 0:nt], cos[64:96, t0:t0 + nt], ALU.mult, [bpk, b_rope], [b_t])
                _tt(P, t2[64:96, 0:nt], pks[64:96, 0:nt], sin[64:96, t0:t0 + nt], ALU.mult, [bpks, b_rope], [b_t])
                _tt(P, krr[64:96, t0:t0 + nt], t1[64:96, 0:nt], t2[64:96, 0:nt], ALU.add, [b_t], [b_krr])
                for t in range(t0 // 128, (t0 + nt) // 128):
                    pv, bpv = k.ps[6], k.b_ps[6]
                    for c in range(2):
                        _mm(P, pv[:, 0:512], cn[:, 3 + c, t * 128:(t + 1) * 128], wv[:, c, :], c == 0, c == 1, [b_cn[bi], b_wv], [bpv])
                    _act(P, Vall[:, t, :, 0:64], pv[:, 0:512].rearrange("p (a b) -> p a b", b=64), AF.Copy, [bpv], [b_V])
            P.flush()
        with ExitStack() as p2:
            Sp = lambda n, sh, dt: k.S(n, sh, dt, p2)
            wuq = [Sp("wuq%d" % i, [128, 3, 2, 96], BF16) for i in range(2)]
            b_wuq = P.bufs_n("wuq", 2)
            wuk = [Sp("wuk%d" % i, [128, 2, 64], BF16) for i in range(2)]
            b_wuk = P.bufs_n("wuk", 2)
            Qf = [Sp("Qf%d" % i, [96, T], BF16) for i in range(2)]
            b_Qf = P.bufs_n("Qf", 2)
            Kf = [Sp("Kf%d" % i, [96, T], BF16) for i in range(2)]
            b_Kf = P.bufs_n("Kf", 2)
            t1 = Sp("t1b", [96, 512], F32)
            t2 = Sp("t2b", [96, 512], F32)
            b_t = P.buf("t12b")
            PT = [Sp("PT%d" % i, [128, 512], BF16) for i in range(3)]
            b_PT = P.bufs_n("PT", 3)
            rec = Sp("rec", [128, 4], F32)
            b_rec = P.buf("rec")
            omla = Sp("omla", [128, NT, 512], BF16)
            b_om = P.buf("omla")
            npt = 0
            for hh in range(8):
                s = hh % 2
                P.dma("pool", wuq[s][:, :, 0, :], k.w_uq[l, :, hh * 96:(hh + 1) * 96].rearrange("(k p) n -> p k n", p=128), w=[b_wuq[s]])
                P.dma("pool", wuq[s][:, :, 1, :], k.w_uq_sw[l, :, hh * 96:(hh + 1) * 96].rearrange("(k p) n -> p k n", p=128), w=[b_wuq[s]])
                P.dma("pool", wuk[s][:], k.w_ukv_k[l, :, hh * 64:(hh + 1) * 64].rearrange("(k p) n -> p k n", p=128), w=[b_wuk[s]])
                for bi, (t0, nt) in enumerate(TBLK):
                    pq, bpq = k.ps[0], k.b_ps[0]
                    pqs, bpqs = k.ps[1], k.b_ps[1]
                    pkn, bpkn = k.ps[2], k.b_ps[2]
                    for c in range(3):
                        _mm(P, pq[0:96, 0:nt], wuq[s][:, c, 0, :], cn[:, c, t0:t0 + nt], c == 0, c == 2, [b_wuq[s], b_cn[bi]], [bpq])
                    for c in range(3):
                        _mm(P, pqs[0:96, 0:nt], wuq[s][:, c, 1, :], cn[:, c, t0:t0 + nt], c == 0, c == 2, [b_wuq[s], b_cn[bi]], [bpqs])
                    for c in range(2):
                        _mm(P, pkn[0:64, 0:nt], wuk[s][:, c, :], cn[:, 3 + c, t0:t0 + nt], c == 0, c == 1, [b_wuk[s], b_cn[bi]], [bpkn])
                    _tt(P, t1[:, 0:nt], pq[0:96, 0:nt], cos[:, t0:t0 + nt], ALU.mult, [bpq, b_rope], [b_t])
                    _tt(P, t2[:, 0:nt], pqs[0:96, 0:nt], sin[:, t0:t0 + nt], ALU.mult, [bpqs, b_rope], [b_t])
                    _tt(P, Qf[s][:, t0:t0 + nt], t1[:, 0:nt], t2[:, 0:nt], ALU.add, [b_t], [b_Qf[s]])
                    _act(P, Kf[s][0:64, t0:t0 + nt], pkn[0:64, 0:nt], AF.Copy, [bpkn], [b_Kf[s]])
                _cp(P, Kf[s][64:96, :], krr[64:96, :], [b_krr], [b_Kf[s]])
                for bi, (t0, nt) in enumerate(TBLK):
                    ktiles = [0, 1] if bi == 0 else list(range(NT))
                    nq = nt // 128
                    po, bpo = k.ps[6], k.b_ps[6]
                    po3 = po[:, 0:4 * 65].rearrange("p (a b) -> p a b", b=65)
                    def st_exp(kt):
                        nonlocal npt
                        pst, bpst = k.ps[3 + (npt % 3)], k.b_ps[3 + (npt % 3)]
                        pt, bpt = PT[npt % 3], b_PT[npt % 3]
                        npt += 1
                        _mm(P, pst[:, 0:nt], Kf[s][:, kt * 128:(kt + 1) * 128], Qf[s][:, t0:t0 + nt], True, True, [b_Kf[s], b_Qf[s]], [bpst])
                        _act(P, pt[:, 0:nt], pst[:, 0:nt], AF.Exp, [bpst], [bpt], scale=SC)
                        return pt, bpt

                    nxt_pt = st_exp(ktiles[0])
                    for ki, kt in enumerate(ktiles):
                        pt, bpt = nxt_pt
                        if ki + 1 < len(ktiles):
                            nxt_pt = st_exp(ktiles[ki + 1])
                        for qi in range(nq):
                            P.pe(lambda e, o=po3[:, qi, :], a=pt[:, qi * 128:(qi + 1) * 128], b=Vall[:, kt, hh, :],
                                 st=(ki == 0 and qi == 0), sp=(ki == len(ktiles) - 1):
                                 e.matmul(o, lhsT=a, rhs=b, start=st, stop=sp, skip_group_check=True), [bpt, b_V], [bpo])
                    P.dve(lambda e, o=rec[:, 0:nq], a=po3[:, 0:nq, 64]: e.reciprocal(out=o, in_=a), [bpo], [b_rec])
                    for qi in range(nq):
                        t = t0 // 128 + qi
                        _ts(P, omla[:, t, hh * 64:(hh + 1) * 64], po3[:, qi, 0:64], rec[:, qi:qi + 1], None, ALU.mult, None, [bpo, b_rec], [b_om])
            P.dma("sp", k.oscr[:, 512:1024].rearrange("(t p) c -> p t c", p=128), omla[:], r=[b_om], w=[k.b_oscr])
            P.flush()


def out_proj(k, l):
    nc, P = k.nc, k.P
    last = l == DEPTH - 1
    with ExitStack() as ph:
        S = lambda n, sh, dt: k.S(n, sh, dt, ph)
        wo = S("wo", [128, KD, D], BF16)
        b_wo = P.buf("wo")
        ot = [S("ot%d" % i, [128, D], BF16) for i in range(2)]
        b_ot = P.bufs_n("ot", 2)
        oT = [S("oT%d" % i, [128, KD, 128], BF16) for i in range(2)]
        b_oT = P.bufs_n("oT", 2)
        ysb = [S("ysbo%d" % i, [128, D], F32) for i in range(2)]
        b_ysb = P.bufs_n("ysbo", 2)
        ss2 = [S("ss2o%d" % i, [128, 2], F32) for i in range(2)]
        b_ss2 = P.bufs_n("ss2o", 2)
        junk = S("junko", [128, 512], BF16)
        b_junk = P.buf("junko")
        P.dma("pool", wo[:], k.w_out[l].rearrange("(k p) n -> p k n", p=128), w=[b_wo])
        tiles = list(range(NCT, NT)) if last else list(range(NT))
        for n, t in enumerate(tiles):
            s = n % 2
            P.dma("sp", ot[s][:], k.oscr[t * 128:(t + 1) * 128, :], r=[k.b_oscr], w=[b_ot[s]])
            for kk in range(KD):
                _tp(P, k.ps_tp[:, kk * 128:(kk + 1) * 128], ot[s][:, kk * 128:(kk + 1) * 128], k.ident_b[:], [b_ot[s], k.b_cst], [k.b_tp])
            _cp(P, oT[s][:].rearrange("p a b -> p (a b)"), k.ps_tp[:, :], [k.b_tp], [b_oT[s]], eng="act")
            for half in range(2):
                py, bpy = k.ps[4 + half], k.b_ps[4 + half]
                for kk in range(KD):
                    _mm(P, py[:], oT[s][:, kk, :], wo[:, kk, half * 512:(half + 1) * 512], kk == 0, kk == KD - 1, [b_oT[s], b_wo], [bpy])
                _cp(P, ysb[s][:, half * 512:(half + 1) * 512], py[:], [bpy], [b_ysb[s]])
                _act(P, junk[:], ysb[s][:, half * 512:(half + 1) * 512], AF.Square, [b_ysb[s]], [b_junk, b_ss2[s]],
                     accum_out=ss2[s][:, half:half + 1])
            post_tile(k, t, 1, ysb[s][:], b_ysb[s], ss2[s], b_ss2[s])
        P.flush()


def _fm(v):
    v = np.asarray(v, np.float32)
    n = v.shape[-1] // 128
    lead = v.shape[:-1]
    a = v.reshape(lead + (n, 128))
    a = np.moveaxis(a, -1, 0)
    return np.ascontiguousarray(a.reshape(128, -1))


def _consts():
    c = np.zeros((128, 13, 128), np.float32)
    r = np.arange(128)[:, None]
    q = np.arange(128)[None, :]
    c[:, 0, :] = (r == q)
    c[:, 1, :] = 1.0
    c[:, 2, :] = (r <= q)
    c[:, 3, :] = (r >= q)
    c[:, 4, :] = (r < q)
    c[:, 5, :] = (r > q)
    for lv in range(7):
        sz = 1 << lv
        c[:, 6 + lv, :] = ((r // (2 * sz)) == (q // (2 * sz))) & ((r // sz) != (q // sz))
    return c.reshape(128, 13 * 128)


def _rope_tables():
    S_, GW = 2048, 64
    row = np.repeat(np.arange(S_ // GW), GW).astype(np.float32)
    col = np.tile(np.arange(GW), S_ // GW).astype(np.float32)
    inv = np.power(np.float32(10000.0), -np.arange(0, 16, 2, dtype=np.float32) / np.float32(16)).astype(np.float32)
    ang = np.concatenate([row[:, None] * inv, col[:, None] * inv], -1).astype(np.float32)
    cos, sin = np.cos(ang).T, np.sin(ang).T
    C = np.zeros((128, T), np.float32)
    Sn = np.zeros((128, T), np.float32)
    C[0:96, :] = 1.0
    C[64:80, 256:] = cos
    C[80:96, 256:] = cos
    Sn[64:80, 256:] = -sin
    Sn[80:96, 256:] = sin
    return C, Sn


_CACHE = {}


def kernel(**inp):
    B = inp["x"].shape[0]
    if "nc" not in _CACHE:
        _CACHE["nc"] = build()
    nc, k = _CACHE["nc"]
    shared = {
        "w_ada": np.ascontiguousarray(inp["w_ada"], np.float32),
        "b_fm": _fm(inp["b_ada"]),
        "gpre_fm": _fm(inp["norm_pre"]),
        "gpost_fm": _fm(inp["norm_post"]),
        "ffn_w_gate": np.ascontiguousarray(inp["ffn_w_gate"], np.float32),
        "ffn_w_up": np.ascontiguousarray(inp["ffn_w_up"], np.float32),
        "ffn_w_down": np.ascontiguousarray(inp["ffn_w_down"], np.float32),
        "cst": _consts(),
        "w_in": np.ascontiguousarray(inp["w_in"], np.float32),
        "w_out": np.ascontiguousarray(inp["w_out"], np.float32),
        "qn_fm": _fm(inp["mla_q_norm"]),
        "kvn_fm": _fm(inp["mla_kv_norm"]),
        "w_uq": np.ascontiguousarray(inp["mla_w_uq"], np.float32),
    }
    dtb = np.asarray(inp["gdn_dt_bias"], np.float32).reshape(DEPTH, 1, 1, 8)
    alg = np.asarray(inp["gdn_a_log"], np.float32).reshape(DEPTH, 1, 1, 8)
    g2 = np.concatenate([np.broadcast_to(dtb, (DEPTH, 1, NT, 8)), np.broadcast_to(alg, (DEPTH, 1, NT, 8))], 1)
    shared["gdnc"] = np.ascontiguousarray(np.broadcast_to(g2.reshape(1, -1), (128, DEPTH * 2 * NT * 8)), np.float32)
    cw = np.asarray(inp["gdn_conv"], np.float32)
    cw = cw.transpose(0, 2, 1).reshape(DEPTH, 12, 128, 5).transpose(2, 0, 1, 3)
    shared["convw"] = np.ascontiguousarray(cw.reshape(128, -1))
    gn = np.asarray(inp["gdn_out_norm"], np.float32).reshape(1, -1)
    shared["gnb"] = np.ascontiguousarray(np.broadcast_to(gn, (128, DEPTH * 128)), np.float32)
    perm = np.arange(768).reshape(8, 96)
    perm = np.concatenate([perm[:, 0:64], perm[:, 80:96], perm[:, 64:80]], 1).reshape(-1)
    shared["w_uq_sw"] = np.ascontiguousarray(shared["w_uq"][:, :, perm])
    wkv = np.asarray(inp["mla_w_ukv"], np.float32).reshape(DEPTH, 256, 8, 128)
    shared["w_ukv_k"] = np.ascontiguousarray(wkv[:, :, :, 0:64].reshape(DEPTH, 256, 512))
    shared["w_ukv_v"] = np.ascontiguousarray(wkv[:, :, :, 64:128].reshape(DEPTH, 256, 512))
    shared["cosq"], shared["sinq"] = _rope_tables()
    in_maps = []
    for b in range(B):
        m = dict(shared)
        m["xcat"] = np.ascontiguousarray(np.concatenate([inp["ctx"][b], inp["x"][b]], 0), np.float32)
        cc = np.stack([_fm(inp["c"][b]), _fm(inp["c_ctx"])], -1)
        m["cc_fm"] = np.ascontiguousarray(cc.reshape(128, 16))
        in_maps.append(m)
    res = run_bass_kernel_spmd(nc, in_maps, core_ids=list(range(B)))
    return np.stack([r["out"] for r in res.results], 0)
```

```python
from contextlib import ExitStack
import numpy as np
import concourse.bass as bass
import concourse.mybir as mybir
from concourse.bass_utils import run_bass_kernel_spmd

F32 = mybir.dt.float32
BF16 = mybir.dt.bfloat16
AF = mybir.ActivationFunctionType
ALU = mybir.AluOpType
AX = mybir.AxisListType

ENGS = ("pe", "act", "dve", "pool", "sp")
DMA_K = 8


class Buf:
    __slots__ = ("name", "w", "r", "excl")

    def __init__(self, name="", excl=False):
        self.name = name
        self.w = None
        self.r = {}
        self.excl = excl


class Op:
    __slots__ = ("eng", "fn", "deps", "inc", "sem", "val", "dma")

    def __init__(self, eng, fn, dma):
        self.eng = eng
        self.fn = fn
        self.deps = []
        self.inc = False
        self.sem = None
        self.val = 0
        self.dma = dma


class Prog:
    def __init__(self, nc, stack):
        self.nc = nc
        self.ops = {e: [] for e in ENGS}
        self.dma_hist = {e: [] for e in ENGS}
        self.esem = {e: stack.enter_context(nc.semaphore("es_" + e)) for e in ENGS}
        self.dsem = {
            e: [stack.enter_context(nc.semaphore("ds_%s%d" % (e, i))) for i in range(DMA_K)]
            for e in ("sp", "pool", "act")
        }
        self.cnt = {e: 0 for e in ENGS}
        self.nd = {e: 0 for e in ENGS}
        self.known = {e: {} for e in ENGS}
        self.carry = {}
        self.bufs = []
        self.stats = {e: [0, 0] for e in ENGS}

    def buf(self, name="", excl=False):
        b = Buf(name, excl)
        self.bufs.append(b)
        return b

    def bufs_n(self, name, n):
        return [self.buf("%s%d" % (name, i)) for i in range(n)]

    def _add(self, eng, fn, r, w, dma=False):
        op = Op(eng, fn, dma)
        deps = []
        xr = [b for b in r if b.excl]
        if xr:
            r = [b for b in r if not b.excl]
            w = list(w) + [b for b in xr if b not in w]
        for b in r:
            if b.w is not None:
                deps.append(b.w)
        for b in w:
            if b.w is not None and (dma or b.w.dma or b.w.eng != eng):
                deps.append(b.w)
            for o in b.r.values():
                if dma or o.dma or o.eng != eng:
                    deps.append(o)
        for b in r:
            b.r[eng] = op
        for b in w:
            b.w = op
            b.r = {}
        if dma:
            h = self.dma_hist[eng]
            if len(h) >= DMA_K:
                deps.append(h[-DMA_K])
            h.append(op)
            op.inc = True
        seen = set()
        for d in deps:
            if d is op or id(d) in seen:
                continue
            if (not d.dma) and d.eng == "pe" and eng == "pe" and not dma:
                continue
            seen.add(id(d))
            d.inc = True
            op.deps.append(d)
        self.ops[eng].append(op)
        return op

    def pe(self, fn, r=(), w=()):
        return self._add("pe", fn, r, w)

    def act(self, fn, r=(), w=()):
        return self._add("act", fn, r, w)

    def dve(self, fn, r=(), w=()):
        return self._add("dve", fn, r, w)

    def pool(self, fn, r=(), w=()):
        return self._add("pool", fn, r, w)

    def dma(self, q, out, in_, r=(), w=(), **kw):
        return self._add(q, lambda e: e.dma_start(out=out, in_=in_, **kw), r, w, dma=True)

    def flush(self):
        nc = self.nc
        newcarry = {}
        for e in ENGS:
            lastc = None
            for op in self.ops[e]:
                if not op.dma:
                    lastc = op
            if lastc is not None:
                lastc.inc = True
            for op in self.ops[e]:
                if op.dma:
                    n = self.nd[e]
                    op.sem = self.dsem[e][n % DMA_K]
                    op.val = 16 * (n // DMA_K + 1)
                    self.nd[e] = n + 1
                    newcarry[op.sem.num] = (op.sem, op.val)
                elif op.inc:
                    self.cnt[e] += 1
                    op.sem = self.esem[e]
                    op.val = self.cnt[e]
                    newcarry[op.sem.num] = (op.sem, op.val)
        carry = dict(self.carry)
        with nc.Block() as block:
            def emit(e):
                def body(eng):
                    known = self.known[e]
                    nwait = 0
                    for k, (sm, v) in carry.items():
                        if known.get(k, 0) < v:
                            eng.wait_ge(sm, v)
                            known[k] = v
                            nwait += 1
                    for op in self.ops[e]:
                        need = {}
                        for d in op.deps:
                            k = d.sem.num
                            if need.get(k, (None, 0))[1] < d.val:
                                need[k] = (d.sem, d.val)
                        for k, (sm, v) in need.items():
                            if known.get(k, 0) < v:
                                eng.wait_ge(sm, v)
                                known[k] = v
                                nwait += 1
                        ins = op.fn(eng)
                        if op.inc:
                            ins.then_inc(op.sem, 16 if op.dma else 1)
                    self.stats[e][0] += len(self.ops[e])
                    self.stats[e][1] += nwait
                return body

            block.tensor(emit("pe"))
            block.scalar(emit("act"))
            block.vector(emit("dve"))
            block.gpsimd(emit("pool"))
            block.sync(emit("sp"))
        for k, v in newcarry.items():
            self.carry[k] = v
        self.ops = {e: [] for e in ENGS}
        self.dma_hist = {e: [] for e in ENGS}
        for b in self.bufs:
            b.w = None
            b.r = {}

    def finish(self):
        self.flush()
        nc = self.nc
        carry = dict(self.carry)
        with nc.Block() as block:
            def emit(e):
                def body(eng):
                    known = self.known[e]
                    for k, (sm, v) in carry.items():
                        if known.get(k, 0) < v:
                            eng.wait_ge(sm, v)
                            known[k] = v
                return body

            block.tensor(emit("pe"))
            block.scalar(emit("act"))
            block.vector(emit("dve"))
            block.gpsimd(emit("pool"))
            block.sync(emit("sp"))

D = 1024
KD = 8
NT = 18
NCT = 2
T = NT * 128
DEPTH = 4
DFF = 2816
NF = 22
IN_COLS = 2736
EPS = 1e-6
OFF_Q, OFF_K, OFF_V, OFF_Z, OFF_A, OFF_B, OFF_CQ, OFF_CKV, OFF_KR = 0, 512, 1024, 1536, 2048, 2056, 2064, 2448, 2704


def _mm(P, out, lhsT, rhs, start, stop, r, w):
    return P.pe(lambda e: e.matmul(out, lhsT=lhsT, rhs=rhs, start=start, stop=stop), r, w)


def _tp(P, out, in_, ident, r, w):
    return P.pe(lambda e: e.transpose(out, in_, ident), r, w)


def _act(P, out, in_, func, r, w, bias=None, scale=None, accum_out=None):
    kw = {}
    if bias is not None:
        kw["bias"] = bias
    if scale is not None:
        kw["scale"] = scale
    if accum_out is not None:
        kw["accum_out"] = accum_out
    return P.act(lambda e: e.activation(out=out, in_=in_, func=func, **kw), r, w)


def _ts(P, out, in0, s1, s2, op0, op1, r, w, eng="dve", accum_out=None):
    kw = {}
    if op1 is not None:
        kw["op1"] = op1
    if accum_out is not None:
        kw["accum_out"] = accum_out
    f = lambda e: e.tensor_scalar(out=out, in0=in0, scalar1=s1, scalar2=s2, op0=op0, **kw)
    return P._add(eng, f, r, w)


def _stt(P, out, in0, scalar, in1, op0, op1, r, w):
    return P.dve(lambda e: e.scalar_tensor_tensor(out=out, in0=in0, scalar=scalar, in1=in1, op0=op0, op1=op1), r, w)


def _tt(P, out, in0, in1, op, r, w, eng="dve"):
    return P._add(eng, lambda e: e.tensor_tensor(out=out, in0=in0, in1=in1, op=op), r, w)


def _cp(P, out, in_, r, w, eng="dve"):
    if eng == "act":
        return P.act(lambda e: e.activation(out=out, in_=in_, func=AF.Copy), r, w)
    return P._add(eng, lambda e: e.tensor_copy(out=out, in_=in_), r, w)


def _rstd(P, ss, n, inv_n, eps, r, w):
    _ts(P, ss, ss, inv_n, eps, ALU.mult, ALU.add, r, w)
    _act(P, ss, ss, AF.Sqrt, r, w)
    P.dve(lambda e: e.reciprocal(out=ss, in_=ss), r, w)


class K:
    pass


def build(nlayers=DEPTH, dbg=None, stages=("ada", "ffn0", "mix", "ffn1")):
    nc = bass.Bass("TRN2", target_bir_lowering=False)
    k = K()
    k.nc = nc
    k.dbgmode = dbg

    def din(name, shape):
        return nc.dram_tensor(name, list(shape), F32, kind="ExternalInput").ap()

    k.xcat = din("xcat", [T, D])
    k.cc_fm = din("cc_fm", [128, KD * 2])
    k.w_ada = din("w_ada", [DEPTH, D, 9 * D])
    k.b_fm = din("b_fm", [128, DEPTH * 72])
    k.gpre_fm = din("gpre_fm", [128, DEPTH * 3 * KD])
    k.gpost_fm = din("gpost_fm", [128, DEPTH * 3 * KD])
    k.wg = din("ffn_w_gate", [DEPTH, 2, D, DFF])
    k.wu = din("ffn_w_up", [DEPTH, 2, D, DFF])
    k.wd = din("ffn_w_down", [DEPTH, 2, DFF, D])
    k.cst = din("cst", [128, 128 * 13])
    k.w_in = din("w_in", [DEPTH, D, IN_COLS])
    k.w_out = din("w_out", [DEPTH, D, D])
    k.gdnc_d = din("gdnc", [128, DEPTH * 2 * NT * 8])
    k.convw_d = din("convw", [128, DEPTH * 60])
    k.gnb_d = din("gnb", [128, DEPTH * 128])
    k.qn_d = din("qn_fm", [128, DEPTH * 3])
    k.kvn_d = din("kvn_fm", [128, DEPTH * 2])
    k.w_uq = din("w_uq", [DEPTH, 384, 768])
    k.w_uq_sw = din("w_uq_sw", [DEPTH, 384, 768])
    k.w_ukv_k = din("w_ukv_k", [DEPTH, 256, 512])
    k.w_ukv_v = din("w_ukv_v", [DEPTH, 256, 512])
    k.cosq = din("cosq", [128, T])
    k.sinq = din("sinq", [128, T])
    k.oscr = nc.dram_tensor("oscr", [T, D], BF16, kind="Internal").ap()
    k.hscr = nc.dram_tensor("hscr", [128, KD, T], BF16, kind="Internal").ap()
    k.wscr = [nc.dram_tensor("wscr%d" % i, [NF // 2, 128, KD * 256], BF16, kind="Internal").ap() for i in range(2)]
    k.out = nc.dram_tensor("out", [2048, D], F32, kind="ExternalOutput").ap()
    if dbg:
        k.dbg = nc.dram_tensor("dbg", [T, D], F32, kind="ExternalOutput").ap()
        k.dbg2 = nc.dram_tensor("dbg2", [T, D], BF16, kind="ExternalOutput").ap()
        k.dbg3 = nc.dram_tensor("dbg3", [3, 128, T], BF16, kind="ExternalOutput").ap()
        k.dbg4 = nc.dram_tensor("dbg4", [128, NT * 128], F32, kind="ExternalOutput").ap()
        k.dbg5 = nc.dram_tensor("dbg5", [128, NT * 48], F32, kind="ExternalOutput").ap()

    with ExitStack() as st:
        P = Prog(nc, st)
        k.P = P
        k.st = st

        k.uid = 0

        def S(name, shape, dt, stack=st):
            k.uid += 1
            return stack.enter_context(nc.sbuf_tensor("%s_%d" % (name, k.uid), list(shape), dt))

        k.S = S
        k.ps_tp = st.enter_context(nc.psum_tensor("ps_tp", [128, 1024], BF16))
        k.b_tp = P.buf("ps_tp", excl=True)
        k.ps = [st.enter_context(nc.psum_tensor("ps%d" % i, [128, 512], F32)) for i in range(7)]
        k.b_ps = [P.buf("ps%d" % i, excl=True) for i in range(7)]

        k.x = S("x_res", [128, NT, D], F32)
        k.b_x = P.bufs_n("x", NT)
        k.cstt = S("cstt", [128, 13, 128], F32)
        k.ones_b = S("ones_b", [128, 128], BF16)
        k.convw = S("convw_s", [128, DEPTH * 60], F32)
        k.gnb = S("gnb_s", [128, DEPTH * 128], F32)
        k.qn = S("qn", [128, DEPTH * 3], F32)
        k.kvn = S("kvn", [128, DEPTH * 2], F32)
        k.b_oscr = P.buf("oscr")
        k.ident_b = S("ident_b", [128, 128], BF16)
        k.ones_f = k.cstt[:, 1, :]
        k.ident_f = k.cstt[:, 0, :]
        k.b_cst = P.buf("cst")
        k.s_fm = S("s_fm", [128, KD, 2], BF16)
        k.ccf = S("ccf", [128, KD * 2], F32)
        k.bfm = S("bfm", [128, DEPTH * 72], F32)
        k.gpre = S("gpre", [128, DEPTH * 3 * KD], F32)
        k.gpost = S("gpost", [128, DEPTH * 3 * KD], F32)
        k.b_small = P.buf("small")
        k.m_fm = S("m_fm", [128, 72, 2], F32)
        k.sc1p = S("sc1p", [128, 3, 2, KD], F32)
        k.gatev = S("gatev", [128, 3, 2, KD], F32)
        k.G = S("Gb", [128, 6, D], BF16)
        k.b_mod = P.buf("mod")
        k.b_G = P.buf("G")

        for t in range(NT):
            P.dma("sp", k.x[:, t, :], k.xcat[t * 128:(t + 1) * 128, :], w=[k.b_x[t]])
        P.dma("sp", k.cstt[:], k.cst.rearrange("p (a b) -> p a b", a=13), w=[k.b_cst])
        P.dma("sp", k.convw[:], k.convw_d, w=[k.b_small])
        P.dma("sp", k.gnb[:], k.gnb_d, w=[k.b_small])
        P.dma("sp", k.qn[:], k.qn_d, w=[k.b_small])
        P.dma("sp", k.kvn[:], k.kvn_d, w=[k.b_small])
        P.dma("sp", k.ccf[:], k.cc_fm, w=[k.b_small])
        P.dma("sp", k.bfm[:], k.b_fm, w=[k.b_small])
        P.dma("sp", k.gpre[:], k.gpre_fm, w=[k.b_small])
        P.dma("sp", k.gpost[:], k.gpost_fm, w=[k.b_small])
        _cp(P, k.ident_b[:], k.ident_f, [k.b_cst], [k.b_cst])
        _cp(P, k.ones_b[:], k.ones_f, [k.b_cst], [k.b_cst])
        _act(P, k.s_fm[:].rearrange("p a b -> p (a b)"), k.ccf[:], AF.Silu, [k.b_small], [k.b_small])
        P.flush()

        for l in range(nlayers):
            last = l == DEPTH - 1
            if "ada" in stages:
                ada_phase(k, l)
            if "ffn0" in stages:
                ffn_phase(k, l, 0, 0, list(range(NT)))
            if "mix" in stages:
                mix_phase(k, l)
            if "ffn1" in stages:
                ffn_phase(k, l, 1, 2, list(range(NCT, NT)) if last else list(range(NT)))

        for t in range(NCT, NT):
            P.dma("sp", k.out[(t - NCT) * 128:(t - NCT + 1) * 128, :], k.x[:, t, :], r=[k.b_x[t]])
        if dbg:
            for t in range(NT):
                P.dma("sp", k.dbg[t * 128:(t + 1) * 128, :], k.x[:, t, :], r=[k.b_x[t]])
            P.dma("sp", k.dbg2, k.oscr, r=[k.b_oscr])
        P.finish()
        k.stats = P.stats
    return nc, k


def ada_phase(k, l):
    nc, P = k.nc, k.P
    with ExitStack() as ph:
        wa = [k.S("wa%d" % i, [128, 9 * D], BF16, ph) for i in range(2)]
        b_wa = P.bufs_n("wa", 2)
        diag = [k.S("diag%d" % i, [128, 128], F32, ph) for i in range(2)]
        b_diag = P.bufs_n("diag", 2)
        psm = k.ps[6]
        b_psm = k.b_ps[6]
        for kk in range(KD):
            s = kk % 2
            P.dma("pool", wa[s][:], k.w_ada[l, kk * 128:(kk + 1) * 128, :], w=[b_wa[s]])
            for j in range(72):
                P.pe(lambda e, o=psm[:, j * 2:(j + 1) * 2], a=wa[s][:, j * 128:(j + 1) * 128], r_=k.s_fm[:, kk, :],
                     st=(kk == 0 and j == 0), sp=(kk == KD - 1):
                     e.matmul(o, lhsT=a, rhs=r_, start=st, stop=sp, skip_group_check=True), [b_wa[s], k.b_small], [b_psm])
        psm3 = psm[:, 0:144].rearrange("p (j w) -> p j w", w=2)
        for w in range(2):
            _tt(P, k.m_fm[:, :, w], psm3[:, :, w], k.bfm[:, l * 72:(l + 1) * 72], ALU.add, [b_psm, k.b_small], [k.b_mod])
        for s in range(3):
            wt = 1.0 if s == 1 else 0.5
            for w in range(2):
                gp = k.gpre[:, (l * 3 + s) * KD:(l * 3 + s + 1) * KD]
                go = k.gpost[:, (l * 3 + s) * KD:(l * 3 + s + 1) * KD]
                _stt(P, k.sc1p[:, s, w, :], k.m_fm[:, (3 * s + 1) * KD:(3 * s + 2) * KD, w], 1.0, gp, ALU.add, ALU.mult,
                     [k.b_mod, k.b_small], [k.b_mod])
                _stt(P, k.gatev[:, s, w, :], k.m_fm[:, (3 * s + 2) * KD:(3 * s + 3) * KD, w], wt, go, ALU.mult, ALU.mult,
                     [k.b_mod, k.b_small], [k.b_mod])
        n = 0
        for s in range(3):
            for w in range(2):
                for half in range(2):
                    pb = k.ps[4 + half]
                    bpb = k.b_ps[4 + half]
                    for q in range(4):
                        kk = half * 4 + q
                        dg = diag[n % 2]
                        _ts(P, dg[:], k.ident_f, k.gatev[:, s, w, kk:kk + 1], None, ALU.mult, None,
                            [k.b_mod, k.b_cst], [b_diag[n % 2]])
                        _mm(P, pb[:, q * 128:(q + 1) * 128], k.ones_f, dg[:], True, True, [b_diag[n % 2], k.b_cst], [bpb])
                        n += 1
                    _act(P, k.G[:, s * 2 + w, half * 512:(half + 1) * 512], pb[:], AF.Copy, [bpb], [k.b_G])
        P.flush()


def premod_tile(k, t, slot, hT_dst, b_hT, xn, b_xn, ss, b_ss, junk, b_junk, alt_bank=None):
    P = k.P
    if alt_bank is not None and t % 2 == 1:
        tp_ap, tp_buf = k.ps[alt_bank][:].bitcast(BF16), k.b_ps[alt_bank]
    else:
        tp_ap, tp_buf = k.ps_tp[:], k.b_tp
    w = 1 if t < NCT else 0
    xt = k.x[:, t, :]
    _act(P, junk, xt, AF.Square, [k.b_x[t]], [b_junk, b_ss], accum_out=ss)
    _rstd(P, ss, 1, 1.0 / D, EPS, [b_ss], [b_ss])
    _ts(P, xn, xt, ss, None, ALU.mult, None, [k.b_x[t], b_ss], [b_xn])
    for kk in range(KD):
        _tp(P, tp_ap[:, kk * 128:(kk + 1) * 128], xn[:, kk * 128:(kk + 1) * 128], k.ident_b[:], [b_xn, k.b_cst], [tp_buf])
    for kk in range(KD):
        _act(P, hT_dst[:, kk, :], tp_ap[:, kk * 128:(kk + 1) * 128], AF.Identity, [tp_buf, k.b_mod], [b_hT],
             bias=k.m_fm[:, 3 * slot * KD + kk, w:w + 1], scale=k.sc1p[:, slot, w, kk:kk + 1])


def post_tile(k, t, slot, ysb, b_ysb, ss2, b_ss2):
    P = k.P
    w = 1 if t < NCT else 0
    _tt(P, ss2[:, 0:1], ss2[:, 0:1], ss2[:, 1:2], ALU.add, [b_ss2], [b_ss2])
    _rstd(P, ss2[:, 0:1], 1, 1.0 / D, EPS, [b_ss2], [b_ss2])
    _stt(P, ysb, ysb, ss2[:, 0:1], k.G[:, slot * 2 + w, :], ALU.mult, ALU.mult, [b_ysb, b_ss2, k.b_G], [b_ysb])
    _tt(P, k.x[:, t, :], k.x[:, t, :], ysb, ALU.add, [k.b_x[t], b_ysb], [k.b_x[t]])


def ffn_phase(k, l, f, slot, tiles):
    nc, P = k.nc, k.P
    FB = 2
    nfb = NF // FB
    with ExitStack() as ph:
        S = lambda n, sh, dt: k.S(n, sh, dt, ph)
        wd = S("wd", [128, NF, D], BF16)
        b_wd = P.bufs_n("wd", NF // 2)
        wgs = [S("wg%d" % i, [128, KD, FB * 128], BF16) for i in range(2)]
        wus = [S("wu%d" % i, [128, KD, FB * 128], BF16) for i in range(2)]
        b_wg = P.bufs_n("wg", 2)
        b_wu = P.bufs_n("wu", 2)
        hTs = [S("hT%d" % i, [128, KD, 512], BF16) for i in range(2)]
        b_hTs = [P.bufs_n("hT%d_" % i, 4) for i in range(2)]
        aT = S("aT", [128, NF, 512], BF16)
        b_aT = P.bufs_n("aT", NF)
        xn = [S("xn%d" % i, [128, D], BF16) for i in range(2)]
        b_xn = P.bufs_n("xn", 2)
        junk = S("junk", [128, D], BF16)
        b_junk = P.buf("junk")
        ss = [S("ss%d" % i, [128, 1], F32) for i in range(2)]
        b_ss = P.bufs_n("ss", 2)
        sg = [S("sg%d" % i, [128, 512], BF16) for i in range(2)]
        b_sg = P.bufs_n("sg", 2)
        ysb = [S("ysb%d" % i, [128, D], F32) for i in range(1)] * 2
        b_ysb = P.bufs_n("ysb", 1) * 2
        ss2 = [S("ss2%d" % i, [128, 2], F32) for i in range(2)]
        b_ss2 = P.bufs_n("ss2", 2)

        wd_src = k.wd[l, f]

        def load_wd(c):
            P.dma("pool", wd[:, 2 * c:2 * c + 2, :], wd_src[c * 256:(c + 1) * 256, :].rearrange("(c p) n -> p c n", p=128),
                  w=[b_wd[c]])

        blocks = [tiles[i:i + 4] for i in range(0, len(tiles), 4)]
        nblk = len(blocks)

        b_wscr = [P.bufs_n("wscr%d_" % i, nfb) for i in range(2)]

        def load_w(idx):
            fb = idx % nfb
            s = idx % 2
            if idx < nfb:
                P.dma("pool", wgs[s][:], k.wg[l, f, :, fb * 256:(fb + 1) * 256].rearrange("(k p) n -> p k n", p=128), w=[b_wg[s]])
                P.dma("pool", wus[s][:], k.wu[l, f, :, fb * 256:(fb + 1) * 256].rearrange("(k p) n -> p k n", p=128), w=[b_wu[s]])
                P.dma("sp", k.wscr[0][fb], wgs[s][:].rearrange("p a b -> p (a b)"), r=[b_wg[s]], w=[b_wscr[0][fb]])
                P.dma("sp", k.wscr[1][fb], wus[s][:].rearrange("p a b -> p (a b)"), r=[b_wu[s]], w=[b_wscr[1][fb]])
            else:
                P.dma("sp", wgs[s][:].rearrange("p a b -> p (a b)"), k.wscr[0][fb], r=[b_wscr[0][fb]], w=[b_wg[s]])
                P.dma("sp", wus[s][:].rearrange("p a b -> p (a b)"), k.wscr[1][fb], r=[b_wscr[1][fb]], w=[b_wu[s]])

        total = nblk * nfb
        load_w(0)
        it = 0
        nt_state = [0]

        def premod_blk_tile(bi, j):
            t = blocks[bi][j]
            n_ = nt_state[0]
            premod_tile(k, t, slot, hTs[bi % 2][:, :, j * 128:(j + 1) * 128], b_hTs[bi % 2][j], xn[n_ % 2][:], b_xn[n_ % 2],
                        ss[n_ % 2][:], b_ss[n_ % 2], junk[:], b_junk, alt_bank=6)
            nt_state[0] += 1

        for j in range(len(blocks[0])):
            premod_blk_tile(0, j)
        for bi, blk in enumerate(blocks):
            ntok = len(blk) * 128
            hT = hTs[bi % 2]
            hbufs = [b_hTs[bi % 2][j] for j in range(len(blk))]
            for fb in range(nfb):
                if it + 1 < total:
                    load_w(it + 1)
                if bi == 0:
                    load_wd(fb)
                s = it % 2
                for ci in range(FB):
                    c = fb * FB + ci
                    pset = (it * FB + ci) % 2
                    pg, pu = k.ps[pset * 2], k.ps[pset * 2 + 1]
                    bpg, bpu = k.b_ps[pset * 2], k.b_ps[pset * 2 + 1]
                    for kk in range(KD):
                        _mm(P, pg[:, 0:ntok], wgs[s][:, kk, ci * 128:(ci + 1) * 128], hT[:, kk, 0:ntok], kk == 0, kk == KD - 1,
                            [b_wg[s]] + hbufs, [bpg])
                    for kk in range(KD):
                        _mm(P, pu[:, 0:ntok], wus[s][:, kk, ci * 128:(ci + 1) * 128], hT[:, kk, 0:ntok], kk == 0, kk == KD - 1,
                            [b_wu[s]] + hbufs, [bpu])
                    _act(P, sg[pset][:, 0:ntok], pg[:, 0:ntok], AF.Silu, [bpg], [b_sg[pset]])
                    _tt(P, aT[:, c, 0:ntok], sg[pset][:, 0:ntok], pu[:, 0:ntok], ALU.mult, [b_sg[pset], bpu], [b_aT[c]])
                it += 1
                if bi + 1 < nblk and fb % 2 == 1 and fb // 2 < len(blocks[bi + 1]):
                    premod_blk_tile(bi + 1, fb // 2)
            for j, t in enumerate(blk):
                yi = t % 2
                for half in range(2):
                    py = k.ps[4 + half]
                    bpy = k.b_ps[4 + half]
                    for c in range(NF):
                        _mm(P, py[:], aT[:, c, j * 128:(j + 1) * 128], wd[:, c, half * 512:(half + 1) * 512], c == 0, c == NF - 1,
                            [b_aT[c], b_wd[c // 2]], [bpy])
                    _cp(P, ysb[yi][:, half * 512:(half + 1) * 512], py[:], [bpy], [b_ysb[yi]])
                    _act(P, junk[:, 0:512], ysb[yi][:, half * 512:(half + 1) * 512], AF.Square, [b_ysb[yi]], [b_junk, b_ss2[yi]],
                         accum_out=ss2[yi][:, half:half + 1])
                post_tile(k, t, slot, ysb[yi][:], b_ysb[yi], ss2[yi], b_ss2[yi])
        P.flush()


TBLK = [(0, 256), (256, 512), (768, 512), (1280, 512), (1792, 512)]
C_ID, C_ONE, C_LE, C_GE, C_LT, C_GT, C_MK, C_N = 0, 1, 2, 3, 4, 5, 6, 13


def _rawcol(tok):
    return tok + 2 if tok < 256 else tok + 6


def _memset(P, ap, val, w, eng="dve"):
    return P._add(eng, lambda e: e.memset(ap, val), (), w)


def mix_phase(k, l):
    nc, P = k.nc, k.P
    last = l == DEPTH - 1
    with ExitStack() as mx:
        S = lambda n, sh, dt, stack=mx: k.S(n, sh, dt, stack)
        m = K()
        m.gb = S("gb", [128, NT, 16], F32)
        m.nb = S("nbeta", [128, NT, 8], F32)
        m.EX = S("EX", [128, NT, 24], F32)
        m.b_gb = P.buf("gb")
        m.b_hscr = P.bufs_n("hscr", len(TBLK))
        with ExitStack() as ph:
            Sp = lambda n, sh, dt: k.S(n, sh, dt, ph)
            xn = [Sp("xn%d" % i, [128, D], BF16) for i in range(2)]
            b_xn = P.bufs_n("xn", 2)
            junk = Sp("junk", [128, D], BF16)
            b_junk = P.buf("junk")
            ss = [Sp("ss%d" % i, [128, 1], F32) for i in range(2)]
            b_ss = P.bufs_n("ss", 2)
            hblk = [Sp("hblkA%d" % i, [128, KD, 512], BF16) for i in range(2)]
            b_hblk = [P.bufs_n("hblkA%d_" % i, 4) for i in range(2)]
            wab = Sp("wab", [128, KD, 16], BF16)
            b_wab = P.buf("wab")
            wk1 = Sp("wk1", [128, NT, 8], F32)
            wk2 = Sp("wk2", [128, NT, 8], F32)
            b_wk = P.buf("wk")
            P.dma("pool", wab[:], k.w_in[l, :, OFF_A:OFF_A + 16].rearrange("(k p) n -> p k n", p=128), w=[b_wab])
            pab = k.ps[6]
            bpab = k.b_ps[6]
            for bi, (t0, nt) in enumerate(TBLK):
                s = bi % 2
                for j in range(nt // 128):
                    t = t0 // 128 + j
                    premod_tile(k, t, 1, hblk[s][:, :, j * 128:(j + 1) * 128], b_hblk[s][j], xn[t % 2][:], b_xn[t % 2],
                                ss[t % 2][:], b_ss[t % 2], junk[:], b_junk, alt_bank=4)
                    for kk in range(KD):
                        _mm(P, pab[:, t * 16:(t + 1) * 16], hblk[s][:, kk, j * 128:(j + 1) * 128], wab[:, kk, :], kk == 0, kk == KD - 1,
                            [b_hblk[s][j], b_wab], [bpab])
                P.dma("sp", k.hscr[:, :, t0:t0 + nt], hblk[s][:, :, 0:nt], r=b_hblk[s][0:nt // 128], w=[m.b_hscr[bi]])
            gbf = m.gb[:].rearrange("p a b -> p (a b)")
            _cp(P, gbf, pab[:, 0:NT * 16], [bpab], [m.b_gb])
            a3 = m.gb[:, :, 0:8]
            b3 = m.gb[:, :, 8:16]
            gd = Sp("gd", [128, 2, NT * 8], F32)
            P.dma("sp", gd[:].rearrange("p a b -> p (a b)"), k.gdnc_d[:, l * 2 * NT * 8:(l + 1) * 2 * NT * 8], w=[b_wk])
            dtb3 = gd[:, 0, :].rearrange("p (a b) -> p a b", b=8)
            alg3 = gd[:, 1, :].rearrange("p (a b) -> p a b", b=8)
            rw = [m.b_gb, b_wk, k.b_small]
            _tt(P, wk1[:], a3, dtb3, ALU.add, rw, [b_wk])
            _stt(P, wk2[:], wk1[:], -1.0, wk1[:], ALU.mult, ALU.max, rw, [b_wk])
            _act(P, wk2[:], wk2[:], AF.Exp, rw, [b_wk], scale=-1.0)
            _act(P, wk2[:], wk2[:], AF.Ln, rw, [b_wk], bias=1.0)
            _ts(P, wk1[:], wk1[:], 0.0, None, ALU.max, None, rw, [b_wk])
            _tt(P, wk1[:], wk1[:], wk2[:], ALU.add, rw, [b_wk])
            _act(P, wk2[:], alg3, AF.Exp, rw, [b_wk])
            _stt(P, a3, wk1[:], -1.0, wk2[:], ALU.mult, ALU.mult, rw, [m.b_gb])
            _act(P, b3, b3, AF.Sigmoid, rw, [m.b_gb])
            _ts(P, m.nb[:], b3, -1.0, None, ALU.mult, None, rw, [m.b_gb])
            pex = k.ps[5]
            bpex = k.b_ps[5]
            cs = k.cstt
            for t in range(NT):
                base = t * 24
                gf = m.gb[:, t, 0:4]
                gbw = m.gb[:, t, 4:8]
                rr = [m.b_gb, k.b_cst]
                _mm(P, pex[:, base + 0:base + 4], cs[:, C_LE, :], gf, True, True, rr, [bpex])
                _mm(P, pex[:, base + 4:base + 8], cs[:, C_GE, :], gbw, True, True, rr, [bpex])
                _mm(P, pex[:, base + 8:base + 12], cs[:, C_GT, :], gf, True, True, rr, [bpex])
                _mm(P, pex[:, base + 12:base + 16], cs[:, C_LT, :], gbw, True, True, rr, [bpex])
                _mm(P, pex[:, base + 16:base + 24], cs[:, C_ONE, :], m.gb[:, t, 0:8], True, True, rr, [bpex])
            _act(P, m.EX[:].rearrange("p a b -> p (a b)"), pex[:, 0:NT * 24], AF.Exp, [bpex], [m.b_gb])
            P.flush()
        for h in range(4):
            gdn_head(k, m, l, h)
        mla_group(k, m, l)
    out_proj(k, l)


def gdn_head(k, m, l, h):
    nc, P = k.nc, k.P
    cs = k.cstt
    with ExitStack() as hd:
        S = lambda n, sh, dt, stack=hd: k.S(n, sh, dt, stack)
        cT = [S("cT%d" % i, [128, T], BF16) for i in range(3)]
        b_cT = [P.bufs_n("cT%d_" % i, len(TBLK)) for i in range(3)]
        kn_tok = S("kn_tok", [128, NT, 128], BF16)
        v_tok = S("v_tok", [128, NT, 128], BF16)
        b_tok = P.buf("tok")
        zs = S("zs", [128, NT, 128], BF16)
        b_zs = P.buf("zs")
        o_acc = S("o_acc", [128, NT, 128], F32)
        b_o = P.bufs_n("oacc", NT)
        with ExitStack() as g1:
            Sp = lambda n, sh, dt: k.S(n, sh, dt, g1)
            raw = [Sp("raw%d" % i, [128, T + 8], BF16) for i in range(3)]
            b_raw = P.bufs_n("raw", 3)
            acc = Sp("acc", [128, T], F32)
            b_acc = P.buf("acc")
            wq = [Sp("wq%d" % i, [128, KD, 128], BF16) for i in range(4)]
            b_wq = P.bufs_n("wq", 4)
            sqall = Sp("sqall", [128, T], BF16)
            b_sqall = P.buf("sqall")
            rsall = Sp("rsall", [128, T], F32)
            b_rsall = P.buf("rsall")
            for i, c in enumerate((h, 4 + h, 8 + h, 12 + h)):
                P.dma("pool", wq[i][:], k.w_in[l, :, c * 128:(c + 1) * 128].rearrange("(k p) n -> p k n", p=128), w=[b_wq[i]])
            for i in range(3):
                _memset(P, raw[i][:], 0.0, [b_raw[i]])
            hblk = [Sp("hblkG%d" % i, [128, KD, 512], BF16) for i in range(2)]
            b_hblk = P.bufs_n("hblkG", 2)
            for bi, (t0, nt) in enumerate(TBLK):
                s = bi % 2
                P.dma("sp", hblk[s][:, :, 0:nt], k.hscr[:, :, t0:t0 + nt], r=[m.b_hscr[bi]], w=[b_hblk[s]])
                for i in range(3):
                    pp = k.ps[i]
                    bpp = k.b_ps[i]
                    for kk in range(KD):
                        _mm(P, pp[:, 0:nt], wq[i][:, kk, :], hblk[s][:, kk, 0:nt], kk == 0, kk == KD - 1, [b_wq[i], b_hblk[s]], [bpp])
                    rc = _rawcol(t0)
                    _act(P, raw[i][:, rc:rc + nt], pp[:, 0:nt], AF.Copy, [bpp], [b_raw[i]])
                pp = k.ps[3]
                bpp = k.b_ps[3]
                ntl = nt // 128
                for q in range(ntl):
                    for kk in range(KD):
                        _mm(P, pp[:, q * 128:(q + 1) * 128], hblk[s][:, kk, q * 128:(q + 1) * 128], wq[3][:, kk, :], kk == 0, kk == KD - 1,
                            [b_hblk[s], b_wq[3]], [bpp])
                tt0 = t0 // 128
                _act(P, zs[:, tt0:tt0 + ntl, :].rearrange("p a b -> p (a b)"), pp[:, 0:ntl * 128], AF.Silu, [bpp], [b_zs])
            n = 0
            for i in range(3):
                c = (h, 4 + h, 8 + h)[i]
                for (s0, s1, base) in ((0, 256, 0), (256, T, 260)):
                    ln = s1 - s0
                    for j in range(5):
                        cw = k.convw[:, (l * 12 + c) * 5 + j:(l * 12 + c) * 5 + j + 1]
                        src = raw[i][:, base + j:base + j + ln]
                        if j == 0:
                            _ts(P, acc[:, s0:s1], src, cw, None, ALU.mult, None, [b_raw[i], k.b_small], [b_acc])
                        else:
                            _stt(P, acc[:, s0:s1], src, cw, acc[:, s0:s1], ALU.mult, ALU.add, [b_raw[i], k.b_small, b_acc], [b_acc])
                if i == 2:
                    _act(P, cT[2][:], acc[:], AF.Silu, [b_acc], b_cT[2])
                else:
                    _act(P, acc[:], acc[:], AF.Silu, [b_acc], [b_acc])
                    scale = (128.0 ** -0.5) if i == 0 else 1.0
                    _act(P, sqall[:], acc[:], AF.Square, [b_acc], [b_sqall])
                    for bi, (t0, nt) in enumerate(TBLK):
                        pp = k.ps[2 + bi]
                        bpp = k.b_ps[2 + bi]
                        _mm(P, pp[:, 0:nt], k.ones_b[:], sqall[:, t0:t0 + nt], True, True, [b_sqall, k.b_cst], [bpp])
                        _ts(P, rsall[:, t0:t0 + nt], pp[:, 0:nt], EPS, None, ALU.add, None, [bpp], [b_rsall])
                    _act(P, rsall[:], rsall[:], AF.Sqrt, [b_rsall], [b_rsall])
                    P.dve(lambda e: e.reciprocal(out=rsall[:], in_=rsall[:]), [b_rsall], [b_rsall])
                    _stt(P, cT[i][:], acc[:], scale, rsall[:], ALU.mult, ALU.mult, [b_acc, b_rsall], b_cT[i])
            for (src_i, dst) in ((1, kn_tok), (2, v_tok)):
                for t0 in range(0, NT, 8):
                    ntl = min(8, NT - t0)
                    for q in range(ntl):
                        t = t0 + q
                        _tp(P, k.ps_tp[:, q * 128:(q + 1) * 128], cT[src_i][:, t * 128:(t + 1) * 128], k.ident_b[:],
                            b_cT[src_i] + [k.b_cst], [k.b_tp])
                    _cp(P, dst[:, t0:t0 + ntl, :].rearrange("p a b -> p (a b)"), k.ps_tp[:, 0:ntl * 128], [k.b_tp], [b_tok])
            for t in range(NT):
                _memset(P, o_acc[:, t, :], 0.0, [b_o[t]])
            import os
            if k.dbgmode and h == int(os.environ.get("DBG_HEAD", "0")):
                for i in range(3):
                    P.dma("sp", k.dbg3[i], cT[i][:], r=b_cT[i])
                P.dma("sp", k.dbg5[:, 0:NT * 16], m.gb[:].rearrange("p a b -> p (a b)"), r=[m.b_gb])
                P.dma("sp", k.dbg5[:, NT * 16:NT * 40], m.EX[:].rearrange("p a b -> p (a b)"), r=[m.b_gb])
            P.flush()
        with ExitStack() as g2:
            Sp = lambda n, sh, dt: k.S(n, sh, dt, g2)
            NJ = 3
            NS = 4
            lhsE = [[Sp("lhsE%d_%d" % (d, j), [128, 128], F32) for j in range(NJ)] for d in range(2)]
            DMi = [Sp("DMi%d" % j, [128, 256], F32) for j in range(NJ)]
            DMs = [Sp("DMs%d" % j, [128, 256], F32) for j in range(NJ)]
            MM_ = [Sp("MM%d" % j, [128, 512], F32) for j in range(NJ)]
            BB_ = [[Sp("BB%d_%d" % (j, i), [128, 512], F32) for i in range(2)] for j in range(NJ)]
            XX_ = [Sp("XX%d" % j, [128, 512], F32) for j in range(NJ)]
            TT_ = [Sp("TT%d" % j, [128, 512], F32) for j in range(NJ)]
            b_MM_ = [P.buf("MM%d" % j) for j in range(NJ)]
            b_BB_ = [[P.buf("BB%d_%d" % (j, i)) for i in range(2)] for j in range(NJ)]
            tp32 = k.ps_tp[:].bitcast(F32)
            JB = [((k.ps[0], k.b_ps[0]), (k.ps[2], k.b_ps[2])), ((k.ps[3], k.b_ps[3]), (k.ps[4], k.b_ps[4])),
                  ((k.ps[5], k.b_ps[5]), (tp32, k.b_tp))]
            R = [[Sp("R%d_%d" % (d, j), [128, 256], F32) for j in range(NJ)] for d in range(2)]
            wtok = [[Sp("wtok%d_%d" % (d, j), [128, 128], F32) for j in range(NJ)] for d in range(2)]
            bj = lambda nm: [P.buf("%s%d" % (nm, j)) for j in range(NJ)]
            b_XX_, b_TT_, b_DM = bj("XX"), bj("TT"), bj("DM")
            Bj = lambda nm: [[P.buf("%s%d_%d" % (nm, d, j)) for j in range(NJ)] for d in range(2)]
            b_lhsE, b_R, b_wtok = Bj("lhsE"), Bj("R"), Bj("wtok")
            QK = [[Sp("QK%d_%d" % (d, s), [128, 128], BF16) for s in range(NS)] for d in range(2)]
            kdec = [[Sp("kdec%d_%d" % (d, s), [128, 128], F32) for s in range(NS)] for d in range(2)]
            u0b = [[Sp("u0b%d_%d" % (d, s), [128, 128], F32) for s in range(NS)] for d in range(2)]
            nwT = [[Sp("nwT%d_%d" % (d, s), [128, 128], F32) for s in range(NS)] for d in range(2)]
            B = lambda nm: [[P.buf("%s%d_%d" % (nm, d, s)) for s in range(NS)] for d in range(2)]
            b_QK, b_kdec, b_u0b, b_nwT = B("QK"), B("kdec"), B("u0b"), B("nwT")
            Sst = [Sp("Sst%d" % d, [128, 128], F32) for d in range(2)]
            Sbf = [Sp("Sbf%d" % d, [128, 128], BF16) for d in range(2)]
            Usb = [Sp("Usb%d" % d, [128, 128], BF16) for d in range(2)]
            Usf = [Sp("Usf%d" % d, [128, 128], F32) for d in range(2)]
            b_Uf = P.bufs_n("Usf", 2)
            b_S = P.bufs_n("Sst", 2)
            b_Sbf = P.bufs_n("Sbf", 2)
            b_U = P.bufs_n("Usb", 2)
            for d in range(2):
                _memset(P, Sst[d][:], 0.0, [b_S[d]])
                _memset(P, Sbf[d][:], 0.0, [b_Sbf[d]])
            order = [list(range(NT)), [1, 0] + list(range(NT - 1, 1, -1))]
            cb = lambda c: [b_cT[0][_blk_of(c)], b_cT[1][_blk_of(c)]]

            def pre(step, j):
                s = step % NS
                cc = [order[0][step], order[1][step]]
                MM, XX, TT = MM_[j], XX_[j], TT_[j]
                b_MM, b_XX, b_TT = b_MM_[j], b_XX_[j], b_TT_[j]
                BB, b_BB = BB_[j], b_BB_[j]
                pE, bpE = JB[j][0]
                pA, bpA = JB[j][1]
                pB, bpB = pE, bpE
                MM3 = MM[:].rearrange("p (a b) -> p a b", a=4)

                def mask_level(lv):
                    mk = cs[:, C_MK + lv, :].unsqueeze(1).to_broadcast([128, 4, 128])
                    _tt(P, BB[lv % 2][:].rearrange("p (a b) -> p a b", a=4), MM3, mk, ALU.mult, [b_MM, k.b_cst], [b_BB[lv % 2]], eng="pool")

                for d in range(2):
                    c = cc[d]
                    gcol = m.gb[:, c, d * 4 + h:d * 4 + h + 1]
                    msk = cs[:, C_GT, :] if d == 0 else cs[:, C_LT, :]
                    _ts(P, lhsE[d][j][:], msk, gcol, None, ALU.mult, None, [m.b_gb, k.b_cst], [b_lhsE[d][j]])
                yield
                for d in range(2):
                    c = cc[d]
                    rhs = cs[:, C_LE, :] if d == 0 else cs[:, C_GE, :]
                    _mm(P, pE[:, d * 128:(d + 1) * 128], lhsE[d][j][:], rhs, True, True, [b_lhsE[d][j], k.b_cst], [bpE])
                    knc = cT[1][:, c * 128:(c + 1) * 128]
                    _mm(P, pE[:, 256 + d * 128:256 + (d + 1) * 128], knc, knc, True, True, cb(c), [bpE])
                yield
                _act(P, DMi[j][:], pE[:, 0:256], AF.Exp, [bpE], [b_DM[j]])
                yield
                inc2 = cs[:, C_LE:C_GE + 1, :].rearrange("p a b -> p (a b)")
                str2 = cs[:, C_LT:C_GT + 1, :].rearrange("p a b -> p (a b)")
                _tt(P, DMs[j][:], DMi[j][:], str2, ALU.mult, [b_DM[j], k.b_cst], [b_DM[j]])
                _tt(P, DMi[j][:], DMi[j][:], inc2, ALU.mult, [b_DM[j], k.b_cst], [b_DM[j]])
                for d in range(2):
                    c = cc[d]
                    bcol = m.gb[:, c, 8 + d * 4 + h:8 + d * 4 + h + 1]
                    _stt(P, MM[:, d * 128:(d + 1) * 128], pE[:, 256 + d * 128:256 + (d + 1) * 128], bcol, DMs[j][:, d * 128:(d + 1) * 128],
                         ALU.mult, ALU.mult, [bpE, m.b_gb, b_DM[j]], [b_MM])
                yield
                for d in range(2):
                    c = cc[d]
                    knc = cT[1][:, c * 128:(c + 1) * 128]
                    qnc = cT[0][:, c * 128:(c + 1) * 128]
                    _mm(P, pE[:, d * 128:(d + 1) * 128], knc, qnc, True, True, cb(c), [bpE])
                for d in range(2):
                    _tp(P, pA[:, d * 128:(d + 1) * 128], MM[:, d * 128:(d + 1) * 128], k.ident_f, [b_MM, k.b_cst], [bpA])
                yield
                for d in range(2):
                    _tt(P, QK[d][s][:], pE[:, d * 128:(d + 1) * 128], DMi[j][:, d * 128:(d + 1) * 128], ALU.mult, [bpE, b_DM[j]],
                        [b_QK[d][s]])
                _cp(P, MM[:, 256:512], pA[:, 0:256], [bpA], [b_MM], eng="act")
                yield
                mask_level(0)
                for d in range(2):
                    c = cc[d]
                    eG = m.EX[:, c, d * 4 + h:d * 4 + h + 1]
                    ekd = m.EX[:, c, 8 + d * 4 + h:8 + d * 4 + h + 1]
                    _act(P, R[d][j][:, 128:256], kn_tok[:, c, :], AF.Copy, [b_tok, m.b_gb], [b_R[d][j]], scale=eG)
                    _act(P, kdec[d][s][:], kn_tok[:, c, :], AF.Copy, [b_tok, m.b_gb], [b_kdec[d][s]], scale=ekd)
                yield
                id4 = cs[:, C_ID, :].unsqueeze(1).to_broadcast([128, 4, 128])
                _tt(P, XX[:].rearrange("p (a b) -> p a b", a=4), id4, BB[0][:].rearrange("p (a b) -> p a b", a=4), ALU.subtract,
                    [b_BB[0], k.b_cst], [b_XX])
                mask_level(1)
                for d in range(2):
                    c = cc[d]
                    _cp(P, R[d][j][:, 0:128], v_tok[:, c, :], [b_tok], [b_R[d][j]], eng="pool")
                yield
                for lv in range(1, 7):
                    lastlv = lv == 6
                    Bl = BB[lv % 2]
                    bBl = b_BB[lv % 2]
                    for d in range(2):
                        _mm(P, pA[:, d * 128:(d + 1) * 128], Bl[:, (2 + d) * 128:(3 + d) * 128], XX[:, d * 128:(d + 1) * 128], True, True,
                            [bBl, b_XX], [bpA])
                    for d in range(2):
                        _tp(P, pA[:, (2 + d) * 128:(3 + d) * 128], XX[:, d * 128:(d + 1) * 128], k.ident_f, [b_XX, k.b_cst], [bpA])
                    yield
                    _cp(P, TT[:, 0:512], pA[:, 0:512], [bpA], [b_TT], eng="act")
                    if not lastlv:
                        mask_level(lv + 1)
                    yield
                    for d in range(2):
                        _mm(P, pB[:, d * 128:(d + 1) * 128], TT[:, (2 + d) * 128:(3 + d) * 128], TT[:, d * 128:(d + 1) * 128], True, True,
                            [b_TT], [bpB])
                    yield
                    _tt(P, XX[:, 0:256], XX[:, 0:256], pB[:, 0:256], ALU.subtract, [b_XX, bpB], [b_XX])
                    yield
                for d in range(2):
                    _mm(P, pA[:, d * 256:(d + 1) * 256], XX[:, d * 128:(d + 1) * 128], R[d][j][:], True, True, [b_XX, b_R[d][j]], [bpA])
                yield
                for d in range(2):
                    c = cc[d]
                    bcol = m.gb[:, c, 8 + d * 4 + h:8 + d * 4 + h + 1]
                    _act(P, wtok[d][j][:], pA[:, d * 256 + 128:d * 256 + 256], AF.Copy, [bpA], [b_wtok[d][j]], scale=-1.0)
                    _ts(P, u0b[d][s][:], pA[:, d * 256:d * 256 + 128], bcol, None, ALU.mult, None, [bpA, m.b_gb], [b_u0b[d][s]])
                yield
                for d in range(2):
                    _tp(P, pB[:, d * 128:(d + 1) * 128], wtok[d][j][:], k.ident_f, [b_wtok[d][j], k.b_cst], [bpB])
                yield
                for d in range(2):
                    _cp(P, nwT[d][s][:], pB[:, d * 128:(d + 1) * 128], [bpB], [b_nwT[d][s]], eng=("act" if d == 0 else "dve"))
                yield

            def scan(step):
                s = step % NS
                pSs = [k.ps[6], k.ps[1]]
                bpSs = [k.b_ps[6], k.b_ps[1]]
                for d in range(2):
                    c = order[d][step]
                    pS, bpS = pSs[d], bpSs[d]
                    qnc = cT[0][:, c * 128:(c + 1) * 128]
                    _mm(P, pS[:, 0:128], nwT[d][s][:], Sst[d][:], True, True, [b_nwT[d][s], b_S[d]], [bpS])
                    _mm(P, pS[:, 128:256], qnc, Sbf[d][:], True, True, [b_cT[0][_blk_of(c)], b_Sbf[d]], [bpS])
                yield
                for d in range(2):
                    c = order[d][step]
                    pS, bpS = pSs[d], bpSs[d]
                    bcol = m.gb[:, c, 8 + d * 4 + h:8 + d * 4 + h + 1]
                    _stt(P, Usf[d][:], pS[:, 0:128], bcol, u0b[d][s][:], ALU.mult, ALU.add, [bpS, m.b_gb, b_u0b[d][s]], [b_Uf[d]])
                yield
                for d in range(2):
                    _cp(P, Usb[d][:], Usf[d][:], [b_Uf[d]], [b_U[d]], eng="act")
                    pS, bpS = pSs[d], bpSs[d]
                    _mm(P, pS[:, 384:512], kdec[d][s][:], Usf[d][:], True, True, [b_kdec[d][s], b_Uf[d]], [bpS])
                yield
                for d in range(2):
                    pS, bpS = pSs[d], bpSs[d]
                    _mm(P, pS[:, 256:384], QK[d][s][:], Usb[d][:], True, True, [b_QK[d][s], b_U[d]], [bpS])
                yield
                for d in range(2):
                    c = order[d][step]
                    pS, bpS = pSs[d], bpSs[d]
                    eG = m.EX[:, c, d * 4 + h:d * 4 + h + 1]
                    cdec = m.EX[:, c, 16 + d * 4 + h:16 + d * 4 + h + 1]
                    oc = o_acc[:, c, :]
                    _stt(P, Sst[d][:], Sst[d][:], cdec, pS[:, 384:512], ALU.mult, ALU.add, [bpS, m.b_gb, b_S[d]], [b_S[d]])
                    _stt(P, oc, pS[:, 128:256], eG, oc, ALU.mult, ALU.add, [bpS, m.b_gb, b_o[c]], [b_o[c]])
                    _tt(P, oc, pS[:, 256:384], oc, ALU.add, [bpS, b_o[c]], [b_o[c]])
                yield
                for d in range(2):
                    _cp(P, Sbf[d][:], Sst[d][:], [b_S[d]], [b_Sbf[d]], eng="act")
                yield

            pres = {}
            pre_step = {}
            done_pre = set()
            nxt = 0
            scan_pos = 0
            scan_gen = None
            while scan_pos < NT:
                for j in range(NJ):
                    if j not in pres and nxt < NT and nxt < scan_pos + NS:
                        pres[j] = pre(nxt, j)
                        pre_step[j] = nxt
                        nxt += 1
                for j in list(pres):
                    try:
                        next(pres[j])
                    except StopIteration:
                        done_pre.add(pre_step[j])
                        del pres[j]
                if scan_gen is None and scan_pos in done_pre:
                    scan_gen = scan(scan_pos)
                if scan_gen is not None:
                    try:
                        next(scan_gen)
                    except StopIteration:
                        scan_gen = None
                        scan_pos += 1
            P.flush()
        with ExitStack() as g3:
            Sp = lambda n, sh, dt: k.S(n, sh, dt, g3)
            ssn = Sp("ssn", [128, NT], F32)
            b_ssn = P.buf("ssn")
            junk = Sp("junkg", [128, 128], BF16)
            b_junk = P.buf("junkg")
            tmpo = [Sp("tmpo%d" % i, [128, 128], F32) for i in range(2)]
            b_tmpo = P.bufs_n("tmpo", 2)
            obuf = Sp("obuf", [128, NT, 128], BF16)
            b_obuf = P.buf("obuf")
            import os
            if k.dbgmode and h == int(os.environ.get("DBG_HEAD", "0")):
                P.dma("sp", k.dbg4, o_acc[:].rearrange("p a b -> p (a b)"), r=b_o)
            sq3 = Sp("sq3", [128, NT, 128], F32)
            b_sq3 = P.buf("sq3")
            _act(P, sq3[:].rearrange("p a b -> p (a b)"), o_acc[:].rearrange("p a b -> p (a b)"), AF.Square, b_o, [b_sq3])
            P.dve(lambda e: e.reduce_sum(out=ssn[:], in_=sq3[:], axis=AX.X), [b_sq3], [b_ssn])
            _rstd(P, ssn[:], NT, 1.0 / 128, EPS, [b_ssn], [b_ssn])
            gnb = k.gnb[:, l * 128:(l + 1) * 128]
            _tt(P, sq3[:], o_acc[:], gnb.unsqueeze(1).to_broadcast([128, NT, 128]), ALU.mult, b_o + [k.b_small], [b_sq3])
            _tt(P, sq3[:], sq3[:], zs[:], ALU.mult, [b_sq3, b_zs], [b_sq3])
            _tt(P, obuf[:], sq3[:], ssn[:].unsqueeze(2).to_broadcast([128, NT, 128]), ALU.mult, [b_sq3, b_ssn], [b_obuf])
            P.dma("sp", k.oscr[:, h * 128:(h + 1) * 128].rearrange("(t p) c -> p t c", p=128), obuf[:], r=[b_obuf], w=[k.b_oscr])
            P.flush()


def _blk_of(c):
    tok = c * 128
    for bi, (t0, nt) in enumerate(TBLK):
        if t0 <= tok < t0 + nt:
            return bi
    raise ValueError


def mla_group(k, m, l):
    nc, P = k.nc, k.P
    SC = float(96.0 ** -0.5)
    with ExitStack() as ml:
        S = lambda n, sh, dt, stack=ml: k.S(n, sh, dt, stack)
        cn = S("cn", [128, 5, T], BF16)
        b_cn = P.bufs_n("cn", len(TBLK))
        Vall = S("Vall", [128, NT, 8, 65], BF16)
        b_V = P.buf("Vall")
        krr = S("krr", [96, T], BF16)
        b_krr = P.buf("krr")
        cos = S("cosq", [96, T], BF16)
        sin = S("sinq", [96, T], BF16)
        b_rope = P.buf("rope")
        P.dma("pool", cos[:], k.cosq[0:96, :], w=[b_rope])
        P.dma("pool", sin[:], k.sinq[0:96, :], w=[b_rope])
        with ExitStack() as p1:
            Sp = lambda n, sh, dt: k.S(n, sh, dt, p1)
            wc = Sp("wc", [128, KD, 640], BF16)
            b_wc = P.buf("wc")
            wkr = Sp("wkr", [128, KD, 2, 96], BF16)
            b_wkr = P.buf("wkr")
            wv = Sp("wv", [128, 2, 512], BF16)
            b_wv = P.buf("wv")
            rawc = [Sp("rawc%d" % i, [128, 5, 512], F32) for i in range(1)] * 2
            b_rawc = P.bufs_n("rawc", 1) * 2
            sqb = [Sp("sqc%d" % i, [128, 5, 512], BF16) for i in range(1)] * 2
            b_sqb = P.bufs_n("sqc", 1) * 2
            rsb = [Sp("rsc%d" % i, [128, 2, 512], F32) for i in range(1)] * 2
            b_rsb = P.bufs_n("rsc", 1) * 2
            hblk = [Sp("hblkM%d" % i, [128, KD, 512], BF16) for i in range(1)] * 2
            b_hblk = P.bufs_n("hblkM", 1) * 2
            t1 = Sp("t1", [96, 512], F32)
            t2 = Sp("t2", [96, 512], F32)
            b_t = P.buf("t12")
            P.dma("pool", wc[:], k.w_in[l, :, OFF_CQ:OFF_CQ + 640].rearrange("(k p) n -> p k n", p=128), w=[b_wc])
            _memset(P, wkr[:].rearrange("p a b c -> p (a b c)"), 0.0, [b_wkr])
            wsrc = k.w_in[l, :, OFF_KR:OFF_KR + 32].rearrange("(k p) n -> p k n", p=128)
            P.dma("pool", wkr[:, :, 0, 64:96], wsrc, w=[b_wkr])
            P.dma("pool", wkr[:, :, 1, 64:80], wsrc[:, :, 16:32], w=[b_wkr])
            P.dma("pool", wkr[:, :, 1, 80:96], wsrc[:, :, 0:16], w=[b_wkr])
            P.dma("pool", wv[:], k.w_ukv_v[l].rearrange("(k p) n -> p k n", p=128), w=[b_wv])
            _memset(P, Vall[:].rearrange("p a b c -> p (a b c)"), 1.0, [b_V])
            for bi, (t0, nt) in enumerate(TBLK):
                s = bi % 2
                hb = [b_hblk[s]]
                hT_ = hblk[s]
                P.dma("sp", hblk[s][:, :, 0:nt], k.hscr[:, :, t0:t0 + nt], r=[m.b_hscr[bi]], w=[b_hblk[s]])
                for c in range(5):
                    pp = k.ps[c % 2]
                    bpp = k.b_ps[c % 2]
                    for kk in range(KD):
                        _mm(P, pp[:, 0:nt], wc[:, kk, c * 128:(c + 1) * 128], hT_[:, kk, 0:nt], kk == 0, kk == KD - 1, [b_wc] + hb, [bpp])
                    _cp(P, rawc[s][:, c, 0:nt], pp[:, 0:nt], [bpp], [b_rawc[s]])
                    _act(P, sqb[s][:, c, 0:nt], rawc[s][:, c, 0:nt], AF.Square, [b_rawc[s]], [b_sqb[s]])
                for gi, (c0, c1, nfeat) in enumerate(((0, 3, 384.0), (3, 5, 256.0))):
                    pp = k.ps[2 + gi]
                    bpp = k.b_ps[2 + gi]
                    for c in range(c0, c1):
                        _mm(P, pp[:, 0:nt], k.ones_b[:], sqb[s][:, c, 0:nt], c == c0, c == c1 - 1, [b_sqb[s], k.b_cst], [bpp])
                    rs = rsb[s][:, gi, 0:nt]
                    _ts(P, rs, pp[:, 0:nt], 1.0 / nfeat, EPS, ALU.mult, ALU.add, [bpp], [b_rsb[s]])
                    _act(P, rs, rs, AF.Sqrt, [b_rsb[s]], [b_rsb[s]])
                    P.dve(lambda e, a=rs: e.reciprocal(out=a, in_=a), [b_rsb[s]], [b_rsb[s]])
                    for c in range(c0, c1):
                        gcol = (k.qn[:, l * 3 + c:l * 3 + c + 1] if gi == 0 else k.kvn[:, l * 2 + (c - 3):l * 2 + (c - 3) + 1])
                        _stt(P, cn[:, c, t0:t0 + nt], rawc[s][:, c, 0:nt], gcol, rs, ALU.mult, ALU.mult, [b_rawc[s], b_rsb[s], k.b_small],
                             [b_cn[bi]])
                pk, bpk = k.ps[4], k.b_ps[4]
                pks, bpks = k.ps[5], k.b_ps[5]
                for kk in range(KD):
                    _mm(P, pk[0:96, 0:nt], wkr[:, kk, 0, :], hT_[:, kk, 0:nt], kk == 0, kk == KD - 1, [b_wkr] + hb, [bpk])
                for kk in range(KD):
                    _mm(P, pks[0:96, 0:nt], wkr[:, kk, 1, :], hT_[:, kk, 0:nt], kk == 0, kk == KD - 1, [b_wkr] + hb, [bpks])
                _tt(P, t1[64:96, 0:nt], pk[64:96, 0:nt], cos[64:96, t0:t0 + nt], ALU.mult, [bpk, b_rope], [b_t])
                _tt(P, t2[64:96, 0:nt], pks[64:96, 0:nt], sin[64:96, t0:t0 + nt], ALU.mult, [bpks, b_rope], [b_t])
                _tt(P, krr[64:96, t0:t0 + nt], t1[64:96, 0:nt], t2[64:96, 0:nt], ALU.add, [b_t], [b_krr])
                for t in range(t0 // 128, (t0 + nt) // 128):
                    pv, bpv = k.ps[6], k.b_ps[6]
                    for c in range(2):
                        _mm(P, pv[:, 0:512], cn[:, 3 + c, t * 128:(t + 1) * 128], wv[:, c, :], c == 0, c == 1, [b_cn[bi], b_wv], [bpv])
                    _act(P, Vall[:, t, :, 0:64], pv[:, 0:512].rearrange("p (a b) -> p a b", b=64), AF.Copy, [bpv], [b_V])
            P.flush()
        with ExitStack() as p2:
            Sp = lambda n, sh, dt: k.S(n, sh, dt, p2)
            wuq = [Sp("wuq%d" % i, [128, 3, 2, 96], BF16) for i in range(2)]
            b_wuq = P.bufs_n("wuq", 2)
            wuk = [Sp("wuk%d" % i, [128, 2, 64], BF16) for i in range(2)]
            b_wuk = P.bufs_n("wuk", 2)
            Qf = [Sp("Qf%d" % i, [96, T], BF16) for i in range(2)]
            b_Qf = P.bufs_n("Qf", 2)
            Kf = [Sp("Kf%d" % i, [96, T], BF16) for i in range(2)]
            b_Kf = P.bufs_n("Kf", 2)
            t1 = Sp("t1b", [96, 512], F32)
            t2 = Sp("t2b", [96, 512], F32)
            b_t = P.buf("t12b")
            PT = [Sp("PT%d" % i, [128, 512], BF16) for i in range(3)]
            b_PT = P.bufs_n("PT", 3)
            rec = Sp("rec", [128, 4], F32)
            b_rec = P.buf("rec")
            omla = Sp("omla", [128, NT, 512], BF16)
            b_om = P.buf("omla")
            npt = 0
            for hh in range(8):
                s = hh % 2
                P.dma("pool", wuq[s][:, :, 0, :], k.w_uq[l, :, hh * 96:(hh + 1) * 96].rearrange("(k p) n -> p k n", p=128), w=[b_wuq[s]])
                P.dma("pool", wuq[s][:, :, 1, :], k.w_uq_sw[l, :, hh * 96:(hh + 1) * 96].rearrange("(k p) n -> p k n", p=128), w=[b_wuq[s]])
                P.dma("pool", wuk[s][:], k.w_ukv_k[l, :, hh * 64:(hh + 1) * 64].rearrange("(k p) n -> p k n", p=128), w=[b_wuk[s]])
                for bi, (t0, nt) in enumerate(TBLK):
                    pq, bpq = k.ps[0], k.b_ps[0]
                    pqs, bpqs = k.ps[1], k.b_ps[1]
                    pkn, bpkn = k.ps[2], k.b_ps[2]
                    for c in range(3):
                        _mm(P, pq[0:96, 0:nt], wuq[s][:, c, 0, :], cn[:, c, t0:t0 + nt], c == 0, c == 2, [b_wuq[s], b_cn[bi]], [bpq])
                    for c in range(3):
                        _mm(P, pqs[0:96, 0:nt], wuq[s][:, c, 1, :], cn[:, c, t0:t0 + nt], c == 0, c == 2, [b_wuq[s], b_cn[bi]], [bpqs])
                    for c in range(2):
                        _mm(P, pkn[0:64, 0:nt], wuk[s][:, c, :], cn[:, 3 + c, t0:t0 + nt], c == 0, c == 1, [b_wuk[s], b_cn[bi]], [bpkn])
                    _tt(P, t1[:, 0:nt], pq[0:96, 0:nt], cos[:, t0:t0 + nt], ALU.mult, [bpq, b_rope], [b_t])
                    _tt(P, t2[:, 0:nt], pqs[0:96, 0:nt], sin[:, t0:t0 + nt], ALU.mult, [bpqs, b_rope], [b_t])
                    _tt(P, Qf[s][:, t0:t0 + nt], t1[:, 0:nt], t2[:, 0:nt], ALU.add, [b_t], [b_Qf[s]])
                    _act(P, Kf[s][0:64, t0:t0 + nt], pkn[0:64, 0:nt], AF.Copy, [bpkn], [b_Kf[s]])
                _cp(P, Kf[s][64:96, :], krr[64:96, :], [b_krr], [b_Kf[s]])
                for bi, (t0, nt) in enumerate(TBLK):
                    ktiles = [0, 1] if bi == 0 else list(range(NT))
                    nq = nt // 128
                    po, bpo = k.ps[6], k.b_ps[6]
                    po3 = po[:, 0:4 * 65].rearrange("p (a b) -> p a b", b=65)
                    def st_exp(kt):
                        nonlocal npt
                        pst, bpst = k.ps[3 + (npt % 3)], k.b_ps[3 + (npt % 3)]
                        pt, bpt = PT[npt % 3], b_PT[npt % 3]
                        npt += 1
                        _mm(P, pst[:, 0:nt], Kf[s][:, kt * 128:(kt + 1) * 128], Qf[s][:, t0:t0 + nt], True, True, [b_Kf[s], b_Qf[s]], [bpst])
                        _act(P, pt[:, 0:nt], pst[:, 0:nt], AF.Exp, [bpst], [bpt], scale=SC)
                        return pt, bpt

                    nxt_pt = st_exp(ktiles[0])
                    for ki, kt in enumerate(ktiles):
                        pt, bpt = nxt_pt
                        if ki + 1 < len(ktiles):
                            nxt_pt = st_exp(ktiles[ki + 1])
                        for qi in range(nq):
                            P.pe(lambda e, o=po3[:, qi, :], a=pt[:, qi * 128:(qi + 1) * 128], b=Vall[:, kt, hh, :],
                                 st=(ki == 0 and qi == 0), sp=(ki == len(ktiles) - 1):
                                 e.matmul(o, lhsT=a, rhs=b, start=st, stop=sp, skip_group_check=True), [bpt, b_V], [bpo])
                    P.dve(lambda e, o=rec[:, 0:nq], a=po3[:, 0:nq, 64]: e.reciprocal(out=o, in_=a), [bpo], [b_rec])
                    for qi in range(nq):
                        t = t0 // 128 + qi
                        _ts(P, omla[:, t, hh * 64:(hh + 1) * 64], po3[:, qi, 0:64], rec[:, qi:qi + 1], None, ALU.mult, None, [bpo, b_rec], [b_om])
            P.dma("sp", k.oscr[:, 512:1024].rearrange("(t p) c -> p t c", p=128), omla[:], r=[b_om], w=[k.b_oscr])
            P.flush()


def out_proj(k, l):
    nc, P = k.nc, k.P
    last = l == DEPTH - 1
    with ExitStack() as ph:
        S = lambda n, sh, dt: k.S(n, sh, dt, ph)
        wo = S("wo", [128, KD, D], BF16)
        b_wo = P.buf("wo")
        ot = [S("ot%d" % i, [128, D], BF16) for i in range(2)]
        b_ot = P.bufs_n("ot", 2)
        oT = [S("oT%d" % i, [128, KD, 128], BF16) for i in range(2)]
        b_oT = P.bufs_n("oT", 2)
        ysb = [S("ysbo%d" % i, [128, D], F32) for i in range(2)]
        b_ysb = P.bufs_n("ysbo", 2)
        ss2 = [S("ss2o%d" % i, [128, 2], F32) for i in range(2)]
        b_ss2 = P.bufs_n("ss2o", 2)
        junk = S("junko", [128, 512], BF16)
        b_junk = P.buf("junko")
        P.dma("pool", wo[:], k.w_out[l].rearrange("(k p) n -> p k n", p=128), w=[b_wo])
        tiles = list(range(NCT, NT)) if last else list(range(NT))
        for n, t in enumerate(tiles):
            s = n % 2
            P.dma("sp", ot[s][:], k.oscr[t * 128:(t + 1) * 128, :], r=[k.b_oscr], w=[b_ot[s]])
            for kk in range(KD):
                _tp(P, k.ps_tp[:, kk * 128:(kk + 1) * 128], ot[s][:, kk * 128:(kk + 1) * 128], k.ident_b[:], [b_ot[s], k.b_cst], [k.b_tp])
            _cp(P, oT[s][:].rearrange("p a b -> p (a b)"), k.ps_tp[:, :], [k.b_tp], [b_oT[s]], eng="act")
            for half in range(2):
                py, bpy = k.ps[4 + half], k.b_ps[4 + half]
                for kk in range(KD):
                    _mm(P, py[:], oT[s][:, kk, :], wo[:, kk, half * 512:(half + 1) * 512], kk == 0, kk == KD - 1, [b_oT[s], b_wo], [bpy])
                _cp(P, ysb[s][:, half * 512:(half + 1) * 512], py[:], [bpy], [b_ysb[s]])
                _act(P, junk[:], ysb[s][:, half * 512:(half + 1) * 512], AF.Square, [b_ysb[s]], [b_junk, b_ss2[s]],
                     accum_out=ss2[s][:, half:half + 1])
            post_tile(k, t, 1, ysb[s][:], b_ysb[s], ss2[s], b_ss2[s])
        P.flush()


def _fm(v):
    v = np.asarray(v, np.float32)
    n = v.shape[-1] // 128
    lead = v.shape[:-1]
    a = v.reshape(lead + (n, 128))
    a = np.moveaxis(a, -1, 0)
    return np.ascontiguousarray(a.reshape(128, -1))


def _consts():
    c = np.zeros((128, 13, 128), np.float32)
    r = np.arange(128)[:, None]
    q = np.arange(128)[None, :]
    c[:, 0, :] = (r == q)
    c[:, 1, :] = 1.0
    c[:, 2, :] = (r <= q)
    c[:, 3, :] = (r >= q)
    c[:, 4, :] = (r < q)
    c[:, 5, :] = (r > q)
    for lv in range(7):
        sz = 1 << lv
        c[:, 6 + lv, :] = ((r // (2 * sz)) == (q // (2 * sz))) & ((r // sz) != (q // sz))
    return c.reshape(128, 13 * 128)


def _rope_tables():
    S_, GW = 2048, 64
    row = np.repeat(np.arange(S_ // GW), GW).astype(np.float32)
    col = np.tile(np.arange(GW), S_ // GW).astype(np.float32)
    inv = np.power(np.float32(10000.0), -np.arange(0, 16, 2, dtype=np.float32) / np.float32(16)).astype(np.float32)
    ang = np.concatenate([row[:, None] * inv, col[:, None] * inv], -1).astype(np.float32)
    cos, sin = np.cos(ang).T, np.sin(ang).T
    C = np.zeros((128, T), np.float32)
    Sn = np.zeros((128, T), np.float32)
    C[0:96, :] = 1.0
    C[64:80, 256:] = cos
    C[80:96, 256:] = cos
    Sn[64:80, 256:] = -sin
    Sn[80:96, 256:] = sin
    return C, Sn


_CACHE = {}


def kernel(**inp):
    B = inp["x"].shape[0]
    if "nc" not in _CACHE:
        _CACHE["nc"] = build()
    nc, k = _CACHE["nc"]
    shared = {
        "w_ada": np.ascontiguousarray(inp["w_ada"], np.float32),
        "b_fm": _fm(inp["b_ada"]),
        "gpre_fm": _fm(inp["norm_pre"]),
        "gpost_fm": _fm(inp["norm_post"]),
        "ffn_w_gate": np.ascontiguousarray(inp["ffn_w_gate"], np.float32),
        "ffn_w_up": np.ascontiguousarray(inp["ffn_w_up"], np.float32),
        "ffn_w_down": np.ascontiguousarray(inp["ffn_w_down"], np.float32),
        "cst": _consts(),
        "w_in": np.ascontiguousarray(inp["w_in"], np.float32),
        "w_out": np.ascontiguousarray(inp["w_out"], np.float32),
        "qn_fm": _fm(inp["mla_q_norm"]),
        "kvn_fm": _fm(inp["mla_kv_norm"]),
        "w_uq": np.ascontiguousarray(inp["mla_w_uq"], np.float32),
    }
    dtb = np.asarray(inp["gdn_dt_bias"], np.float32).reshape(DEPTH, 1, 1, 8)
    alg = np.asarray(inp["gdn_a_log"], np.float32).reshape(DEPTH, 1, 1, 8)
    g2 = np.concatenate([np.broadcast_to(dtb, (DEPTH, 1, NT, 8)), np.broadcast_to(alg, (DEPTH, 1, NT, 8))], 1)
    shared["gdnc"] = np.ascontiguousarray(np.broadcast_to(g2.reshape(1, -1), (128, DEPTH * 2 * NT * 8)), np.float32)
    cw = np.asarray(inp["gdn_conv"], np.float32)
    cw = cw.transpose(0, 2, 1).reshape(DEPTH, 12, 128, 5).transpose(2, 0, 1, 3)
    shared["convw"] = np.ascontiguousarray(cw.reshape(128, -1))
    gn = np.asarray(inp["gdn_out_norm"], np.float32).reshape(1, -1)
    shared["gnb"] = np.ascontiguousarray(np.broadcast_to(gn, (128, DEPTH * 128)), np.float32)
    perm = np.arange(768).reshape(8, 96)
    perm = np.concatenate([perm[:, 0:64], perm[:, 80:96], perm[:, 64:80]], 1).reshape(-1)
    shared["w_uq_sw"] = np.ascontiguousarray(shared["w_uq"][:, :, perm])
    wkv = np.asarray(inp["mla_w_ukv"], np.float32).reshape(DEPTH, 256, 8, 128)
    shared["w_ukv_k"] = np.ascontiguousarray(wkv[:, :, :, 0:64].reshape(DEPTH, 256, 512))
    shared["w_ukv_v"] = np.ascontiguousarray(wkv[:, :, :, 64:128].reshape(DEPTH, 256, 512))
    shared["cosq"], shared["sinq"] = _rope_tables()
    in_maps = []
    for b in range(B):
        m = dict(shared)
        m["xcat"] = np.ascontiguousarray(np.concatenate([inp["ctx"][b], inp["x"][b]], 0), np.float32)
        cc = np.stack([_fm(inp["c"][b]), _fm(inp["c_ctx"])], -1)
        m["cc_fm"] = np.ascontiguousarray(cc.reshape(128, 16))
        in_maps.append(m)
    res = run_bass_kernel_spmd(nc, in_maps, core_ids=list(range(B)))
    return np.stack([r["out"] for r in res.results], 0)
```

```python
from contextlib import ExitStack
import numpy as np
import concourse.bass as bass
import concourse.mybir as mybir
from concourse.bass_utils import run_bass_kernel_spmd

F32 = mybir.dt.float32
BF16 = mybir.dt.bfloat16
AF = mybir.ActivationFunctionType
ALU = mybir.AluOpType
AX = mybir.AxisListType

ENGS = ("pe", "act", "dve", "pool", "sp")
DMA_K = 8


class Buf:
    __slots__ = ("name", "w", "r", "excl")

    def __init__(self, name="", excl=False):
        self.name = name
        self.w = None
        self.r = {}
        self.excl = excl


class Op:
    __slots__ = ("eng", "fn", "deps", "inc", "sem", "val", "dma")

    def __init__(self, eng, fn, dma):
        self.eng = eng
        self.fn = fn
        self.deps = []
        self.inc = False
        self.sem = None
        self.val = 0
        self.dma = dma


class Prog:
    def __init__(self, nc, stack):
        self.nc = nc
        self.ops = {e: [] for e in ENGS}
        self.dma_hist = {e: [] for e in ENGS}
        self.esem = {e: stack.enter_context(nc.semaphore("es_" + e)) for e in ENGS}
        self.dsem = {
            e: [stack.enter_context(nc.semaphore("ds_%s%d" % (e, i))) for i in range(DMA_K)]
            for e in ("sp", "pool", "act")
        }
        self.cnt = {e: 0 for e in ENGS}
        self.nd = {e: 0 for e in ENGS}
        self.known = {e: {} for e in ENGS}
        self.carry = {}
        self.bufs = []
        self.stats = {e: [0, 0] for e in ENGS}

    def buf(self, name="", excl=False):
        b = Buf(name, excl)
        self.bufs.append(b)
        return b

    def bufs_n(self, name, n):
        return [self.buf("%s%d" % (name, i)) for i in range(n)]

    def _add(self, eng, fn, r, w, dma=False):
        op = Op(eng, fn, dma)
        deps = []
        xr = [b for b in r if b.excl]
        if xr:
            r = [b for b in r if not b.excl]
            w = list(w) + [b for b in xr if b not in w]
        for b in r:
            if b.w is not None:
                deps.append(b.w)
        for b in w:
            if b.w is not None and (dma or b.w.dma or b.w.eng != eng):
                deps.append(b.w)
            for o in b.r.values():
                if dma or o.dma or o.eng != eng:
                    deps.append(o)
        for b in r:
            b.r[eng] = op
        for b in w:
            b.w = op
            b.r = {}
        if dma:
            h = self.dma_hist[eng]
            if len(h) >= DMA_K:
                deps.append(h[-DMA_K])
            h.append(op)
            op.inc = True
        seen = set()
        for d in deps:
            if d is op or id(d) in seen:
                continue
            if (not d.dma) and d.eng == "pe" and eng == "pe" and not dma:
                continue
            seen.add(id(d))
            d.inc = True
            op.deps.append(d)
        self.ops[eng].append(op)
        return op

    def pe(self, fn, r=(), w=()):
        return self._add("pe", fn, r, w)

    def act(self, fn, r=(), w=()):
        return self._add("act", fn, r, w)

    def dve(self, fn, r=(), w=()):
        return self._add("dve", fn, r, w)

    def pool(self, fn, r=(), w=()):
        return self._add("pool", fn, r, w)

    def dma(self, q, out, in_, r=(), w=(), **kw):
        return self._add(q, lambda e: e.dma_start(out=out, in_=in_, **kw), r, w, dma=True)

    def flush(self):
        nc = self.nc
        newcarry = {}
        for e in ENGS:
            lastc = None
            for op in self.ops[e]:
                if not op.dma:
                    lastc = op
            if lastc is not None:
                lastc.inc = True
            for op in self.ops[e]:
                if op.dma:
                    n = self.nd[e]
                    op.sem = self.dsem[e][n % DMA_K]
                    op.val = 16 * (n // DMA_K + 1)
                    self.nd[e] = n + 1
                    newcarry[op.sem.num] = (op.sem, op.val)
                elif op.inc:
                    self.cnt[e] += 1
                    op.sem = self.esem[e]
                    op.val = self.cnt[e]
                    newcarry[op.sem.num] = (op.sem, op.val)
        carry = dict(self.carry)
        with nc.Block() as block:
            def emit(e):
                def body(eng):
                    known = self.known[e]
                    nwait = 0
                    for k, (sm, v) in carry.items():
                        if known.get(k, 0) < v:
                            eng.wait_ge(sm, v)
                            known[k] = v
                            nwait += 1
                    for op in self.ops[e]:
                        need = {}
                        for d in op.deps:
                            k = d.sem.num
                            if need.get(k, (None, 0))[1] < d.val:
                                need[k] = (d.sem, d.val)
                        for k, (sm, v) in need.items():
                            if known.get(k, 0) < v:
                                eng.wait_ge(sm, v)
                                known[k] = v
                                nwait += 1
                        ins = op.fn(eng)
                        if op.inc:
                            ins.then_inc(op.sem, 16 if op.dma else 1)
                    self.stats[e][0] += len(self.ops[e])
                    self.stats[e][1] += nwait
                return body

            block.tensor(emit("pe"))
            block.scalar(emit("act"))
            block.vector(emit("dve"))
            block.gpsimd(emit("pool"))
            block.sync(emit("sp"))
        for k, v in newcarry.items():
            self.carry[k] = v
        self.ops = {e: [] for e in ENGS}
        self.dma_hist = {e: [] for e in ENGS}
        for b in self.bufs:
            b.w = None
            b.r = {}

    def finish(self):
        self.flush()
        nc = self.nc
        carry = dict(self.carry)
        with nc.Block() as block:
            def emit(e):
                def body(eng):
                    known = self.known[e]
                    for k, (sm, v) in carry.items():
                        if known.get(k, 0) < v:
                            eng.wait_ge(sm, v)
                            known[k] = v
                return body

            block.tensor(emit("pe"))
            block.scalar(emit("act"))
            block.vector(emit("dve"))
            block.gpsimd(emit("pool"))
            block.sync(emit("sp"))

D = 1024
KD = 8
NT = 18
NCT = 2
T = NT * 128
DEPTH = 4
DFF = 2816
NF = 22
IN_COLS = 2736
EPS = 1e-6
OFF_Q, OFF_K, OFF_V, OFF_Z, OFF_A, OFF_B, OFF_CQ, OFF_CKV, OFF_KR = 0, 512, 1024, 1536, 2048, 2056, 2064, 2448, 2704


def _mm(P, out, lhsT, rhs, start, stop, r, w):
    return P.pe(lambda e: e.matmul(out, lhsT=lhsT, rhs=rhs, start=start, stop=stop), r, w)


def _tp(P, out, in_, ident, r, w):
    return P.pe(lambda e: e.transpose(out, in_, ident), r, w)


def _act(P, out, in_, func, r, w, bias=None, scale=None, accum_out=None):
    kw = {}
    if bias is not None:
        kw["bias"] = bias
    if scale is not None:
        kw["scale"] = scale
    if accum_out is not None:
        kw["accum_out"] = accum_out
    return P.act(lambda e: e.activation(out=out, in_=in_, func=func, **kw), r, w)


def _ts(P, out, in0, s1, s2, op0, op1, r, w, eng="dve", accum_out=None):
    kw = {}
    if op1 is not None:
        kw["op1"] = op1
    if accum_out is not None:
        kw["accum_out"] = accum_out
    f = lambda e: e.tensor_scalar(out=out, in0=in0, scalar1=s1, scalar2=s2, op0=op0, **kw)
    return P._add(eng, f, r, w)


def _stt(P, out, in0, scalar, in1, op0, op1, r, w):
    return P.dve(lambda e: e.scalar_tensor_tensor(out=out, in0=in0, scalar=scalar, in1=in1, op0=op0, op1=op1), r, w)


def _tt(P, out, in0, in1, op, r, w, eng="dve"):
    return P._add(eng, lambda e: e.tensor_tensor(out=out, in0=in0, in1=in1, op=op), r, w)


def _cp(P, out, in_, r, w, eng="dve"):
    if eng == "act":
        return P.act(lambda e: e.activation(out=out, in_=in_, func=AF.Copy), r, w)
    return P._add(eng, lambda e: e.tensor_copy(out=out, in_=in_), r, w)


def _rstd(P, ss, n, inv_n, eps, r, w):
    _ts(P, ss, ss, inv_n, eps, ALU.mult, ALU.add, r, w)
    _act(P, ss, ss, AF.Sqrt, r, w)
    P.dve(lambda e: e.reciprocal(out=ss, in_=ss), r, w)


class K:
    pass


def build(nlayers=DEPTH, dbg=None, stages=("ada", "ffn0", "mix", "ffn1")):
    nc = bass.Bass("TRN2", target_bir_lowering=False)
    k = K()
    k.nc = nc
    k.dbgmode = dbg

    def din(name, shape):
        return nc.dram_tensor(name, list(shape), F32, kind="ExternalInput").ap()

    k.xcat = din("xcat", [T, D])
    k.cc_fm = din("cc_fm", [128, KD * 2])
    k.w_ada = din("w_ada", [DEPTH, D, 9 * D])
    k.b_fm = din("b_fm", [128, DEPTH * 72])
    k.gpre_fm = din("gpre_fm", [128, DEPTH * 3 * KD])
    k.gpost_fm = din("gpost_fm", [128, DEPTH * 3 * KD])
    k.wg = din("ffn_w_gate", [DEPTH, 2, D, DFF])
    k.wu = din("ffn_w_up", [DEPTH, 2, D, DFF])
    k.wd = din("ffn_w_down", [DEPTH, 2, DFF, D])
    k.cst = din("cst", [128, 128 * 13])
    k.w_in = din("w_in", [DEPTH, D, IN_COLS])
    k.w_out = din("w_out", [DEPTH, D, D])
    k.gdnc_d = din("gdnc", [128, DEPTH * 2 * NT * 8])
    k.convw_d = din("convw", [128, DEPTH * 60])
    k.gnb_d = din("gnb", [128, DEPTH * 128])
    k.qn_d = din("qn_fm", [128, DEPTH * 3])
    k.kvn_d = din("kvn_fm", [128, DEPTH * 2])
    k.w_uq = din("w_uq", [DEPTH, 384, 768])
    k.w_uq_sw = din("w_uq_sw", [DEPTH, 384, 768])
    k.w_ukv_k = din("w_ukv_k", [DEPTH, 256, 512])
    k.w_ukv_v = din("w_ukv_v", [DEPTH, 256, 512])
    k.cosq = din("cosq", [128, T])
    k.sinq = din("sinq", [128, T])
    k.oscr = nc.dram_tensor("oscr", [T, D], BF16, kind="Internal").ap()
    k.hscr = nc.dram_tensor("hscr", [128, KD, T], BF16, kind="Internal").ap()
    k.wscr = [nc.dram_tensor("wscr%d" % i, [NF // 2, 128, KD * 256], BF16, kind="Internal").ap() for i in range(2)]
    k.out = nc.dram_tensor("out", [2048, D], F32, kind="ExternalOutput").ap()
    if dbg:
        k.dbg = nc.dram_tensor("dbg", [T, D], F32, kind="ExternalOutput").ap()
        k.dbg2 = nc.dram_tensor("dbg2", [T, D], BF16, kind="ExternalOutput").ap()
        k.dbg3 = nc.dram_tensor("dbg3", [3, 128, T], BF16, kind="ExternalOutput").ap()
        k.dbg4 = nc.dram_tensor("dbg4", [128, NT * 128], F32, kind="ExternalOutput").ap()
        k.dbg5 = nc.dram_tensor("dbg5", [128, NT * 48], F32, kind="ExternalOutput").ap()

    with ExitStack() as st:
        P = Prog(nc, st)
        k.P = P
        k.st = st

        k.uid = 0

        def S(name, shape, dt, stack=st):
            k.uid += 1
            return stack.enter_context(nc.sbuf_tensor("%s_%d" % (name, k.uid), list(shape), dt))

        k.S = S
        k.ps_tp = st.enter_context(nc.psum_tensor("ps_tp", [128, 1024], BF16))
        k.b_tp = P.buf("ps_tp", excl=True)
        k.ps = [st.enter_context(nc.psum_tensor("ps%d" % i, [128, 512], F32)) for i in range(7)]
        k.b_ps = [P.buf("ps%d" % i, excl=True) for i in range(7)]

        k.x = S("x_res", [128, NT, D], F32)
        k.b_x = P.bufs_n("x", NT)
        k.cstt = S("cstt", [128, 13, 128], F32)
        k.ones_b = S("ones_b", [128, 128], BF16)
        k.convw = S("convw_s", [128, DEPTH * 60], F32)
        k.gnb = S("gnb_s", [128, DEPTH * 128], F32)
        k.qn = S("qn", [128, DEPTH * 3], F32)
        k.kvn = S("kvn", [128, DEPTH * 2], F32)
        k.b_oscr = P.buf("oscr")
        k.ident_b = S("ident_b", [128, 128], BF16)
        k.ones_f = k.cstt[:, 1, :]
        k.ident_f = k.cstt[:, 0, :]
        k.b_cst = P.buf("cst")
        k.s_fm = S("s_fm", [128, KD, 2], BF16)
        k.ccf = S("ccf", [128, KD * 2], F32)
        k.bfm = S("bfm", [128, DEPTH * 72], F32)
        k.gpre = S("gpre", [128, DEPTH * 3 * KD], F32)
        k.gpost = S("gpost", [128, DEPTH * 3 * KD], F32)
        k.b_small = P.buf("small")
        k.m_fm = S("m_fm", [128, 72, 2], F32)
        k.sc1p = S("sc1p", [128, 3, 2, KD], F32)
        k.gatev = S("gatev", [128, 3, 2, KD], F32)
        k.G = S("Gb", [128, 6, D], BF16)
        k.b_mod = P.buf("mod")
        k.b_G = P.buf("G")

        for t in range(NT):
            P.dma("sp", k.x[:, t, :], k.xcat[t * 128:(t + 1) * 128, :], w=[k.b_x[t]])
        P.dma("sp", k.cstt[:], k.cst.rearrange("p (a b) -> p a b", a=13), w=[k.b_cst])
        P.dma("sp", k.convw[:], k.convw_d, w=[k.b_small])
        P.dma("sp", k.gnb[:], k.gnb_d, w=[k.b_small])
        P.dma("sp", k.qn[:], k.qn_d, w=[k.b_small])
        P.dma("sp", k.kvn[:], k.kvn_d, w=[k.b_small])
        P.dma("sp", k.ccf[:], k.cc_fm, w=[k.b_small])
        P.dma("sp", k.bfm[:], k.b_fm, w=[k.b_small])
        P.dma("sp", k.gpre[:], k.gpre_fm, w=[k.b_small])
        P.dma("sp", k.gpost[:], k.gpost_fm, w=[k.b_small])
        _cp(P, k.ident_b[:], k.ident_f, [k.b_cst], [k.b_cst])
        _cp(P, k.ones_b[:], k.ones_f, [k.b_cst], [k.b_cst])
        _act(P, k.s_fm[:].rearrange("p a b -> p (a b)"), k.ccf[:], AF.Silu, [k.b_small], [k.b_small])
        P.flush()

        for l in range(nlayers):
            last = l == DEPTH - 1
            if "ada" in stages:
                ada_phase(k, l)
            if "ffn0" in stages:
                ffn_phase(k, l, 0, 0, list(range(NT)))
            if "mix" in stages:
                mix_phase(k, l)
            if "ffn1" in stages:
                ffn_phase(k, l, 1, 2, list(range(NCT, NT)) if last else list(range(NT)))

        for t in range(NCT, NT):
            P.dma("sp", k.out[(t - NCT) * 128:(t - NCT + 1) * 128, :], k.x[:, t, :], r=[k.b_x[t]])
        if dbg:
            for t in range(NT):
                P.dma("sp", k.dbg[t * 128:(t + 1) * 128, :], k.x[:, t, :], r=[k.b_x[t]])
            P.dma("sp", k.dbg2, k.oscr, r=[k.b_oscr])
        P.finish()
        k.stats = P.stats
    return nc, k


def ada_phase(k, l):
    nc, P = k.nc, k.P
    with ExitStack() as ph:
        wa = [k.S("wa%d" % i, [128, 9 * D], BF16, ph) for i in range(2)]
        b_wa = P.bufs_n("wa", 2)
        diag = [k.S("diag%d" % i, [128, 128], F32, ph) for i in range(2)]
        b_diag = P.bufs_n("diag", 2)
        psm = k.ps[6]
        b_psm = k.b_ps[6]
        for kk in range(KD):
            s = kk % 2
            P.dma("pool", wa[s][:], k.w_ada[l, kk * 128:(kk + 1) * 128, :], w=[b_wa[s]])
            for j in range(72):
                P.pe(lambda e, o=psm[:, j * 2:(j + 1) * 2], a=wa[s][:, j * 128:(j + 1) * 128], r_=k.s_fm[:, kk, :],
                     st=(kk == 0 and j == 0), sp=(kk == KD - 1):
                     e.matmul(o, lhsT=a, rhs=r_, start=st, stop=sp, skip_group_check=True), [b_wa[s], k.b_small], [b_psm])
        psm3 = psm[:, 0:144].rearrange("p (j w) -> p j w", w=2)
        for w in range(2):
            _tt(P, k.m_fm[:, :, w], psm3[:, :, w], k.bfm[:, l * 72:(l + 1) * 72], ALU.add, [b_psm, k.b_small], [k.b_mod])
        for s in range(3):
            wt = 1.0 if s == 1 else 0.5
            for w in range(2):
                gp = k.gpre[:, (l * 3 + s) * KD:(l * 3 + s + 1) * KD]
                go = k.gpost[:, (l * 3 + s) * KD:(l * 3 + s + 1) * KD]
                _stt(P, k.sc1p[:, s, w, :], k.m_fm[:, (3 * s + 1) * KD:(3 * s + 2) * KD, w], 1.0, gp, ALU.add, ALU.mult,
                     [k.b_mod, k.b_small], [k.b_mod])
                _stt(P, k.gatev[:, s, w, :], k.m_fm[:, (3 * s + 2) * KD:(3 * s + 3) * KD, w], wt, go, ALU.mult, ALU.mult,
                     [k.b_mod, k.b_small], [k.b_mod])
        n = 0
        for s in range(3):
            for w in range(2):
                for half in range(2):
                    pb = k.ps[4 + half]
                    bpb = k.b_ps[4 + half]
                    for q in range(4):
                        kk = half * 4 + q
                        dg = diag[n % 2]
                        _ts(P, dg[:], k.ident_f, k.gatev[:, s, w, kk:kk + 1], None, ALU.mult, None,
                            [k.b_mod, k.b_cst], [b_diag[n % 2]])
                        _mm(P, pb[:, q * 128:(q + 1) * 128], k.ones_f, dg[:], True, True, [b_diag[n % 2], k.b_cst], [bpb])
                        n += 1
                    _act(P, k.G[:, s * 2 + w, half * 512:(half + 1) * 512], pb[:], AF.Copy, [bpb], [k.b_G])
        P.flush()


def premod_tile(k, t, slot, hT_dst, b_hT, xn, b_xn, ss, b_ss, junk, b_junk, alt_bank=None):
    P = k.P
    if alt_bank is not None and t % 2 == 1:
        tp_ap, tp_buf = k.ps[alt_bank][:].bitcast(BF16), k.b_ps[alt_bank]
    else:
        tp_ap, tp_buf = k.ps_tp[:], k.b_tp
    w = 1 if t < NCT else 0
    xt = k.x[:, t, :]
    _act(P, junk, xt, AF.Square, [k.b_x[t]], [b_junk, b_ss], accum_out=ss)
    _rstd(P, ss, 1, 1.0 / D, EPS, [b_ss], [b_ss])
    _ts(P, xn, xt, ss, None, ALU.mult, None, [k.b_x[t], b_ss], [b_xn])
    for kk in range(KD):
        _tp(P, tp_ap[:, kk * 128:(kk + 1) * 128], xn[:, kk * 128:(kk + 1) * 128], k.ident_b[:], [b_xn, k.b_cst], [tp_buf])
    for kk in range(KD):
        _act(P, hT_dst[:, kk, :], tp_ap[:, kk * 128:(kk + 1) * 128], AF.Identity, [tp_buf, k.b_mod], [b_hT],
             bias=k.m_fm[:, 3 * slot * KD + kk, w:w + 1], scale=k.sc1p[:, slot, w, kk:kk + 1])


def post_tile(k, t, slot, ysb, b_ysb, ss2, b_ss2):
    P = k.P
    w = 1 if t < NCT else 0
    _tt(P, ss2[:, 0:1], ss2[:, 0:1], ss2[:, 1:2], ALU.add, [b_ss2], [b_ss2])
    _rstd(P, ss2[:, 0:1], 1, 1.0 / D, EPS, [b_ss2], [b_ss2])
    _stt(P, ysb, ysb, ss2[:, 0:1], k.G[:, slot * 2 + w, :], ALU.mult, ALU.mult, [b_ysb, b_ss2, k.b_G], [b_ysb])
    _tt(P, k.x[:, t, :], k.x[:, t, :], ysb, ALU.add, [k.b_x[t], b_ysb], [k.b_x[t]])


def ffn_phase(k, l, f, slot, tiles):
    nc, P = k.nc, k.P
    FB = 2
    nfb = NF // FB
    with ExitStack() as ph:
        S = lambda n, sh, dt: k.S(n, sh, dt, ph)
        wd = S("wd", [128, NF, D], BF16)
        b_wd = P.bufs_n("wd", NF // 2)
        wgs = [S("wg%d" % i, [128, KD, FB * 128], BF16) for i in range(2)]
        wus = [S("wu%d" % i, [128, KD, FB * 128], BF16) for i in range(2)]
        b_wg = P.bufs_n("wg", 2)
        b_wu = P.bufs_n("wu", 2)
        hTs = [S("hT%d" % i, [128, KD, 512], BF16) for i in range(2)]
        b_hTs = [P.bufs_n("hT%d_" % i, 4) for i in range(2)]
        aT = S("aT", [128, NF, 512], BF16)
        b_aT = P.bufs_n("aT", NF)
        xn = [S("xn%d" % i, [128, D], BF16) for i in range(2)]
        b_xn = P.bufs_n("xn", 2)
        junk = S("junk", [128, D], BF16)
        b_junk = P.buf("junk")
        ss = [S("ss%d" % i, [128, 1], F32) for i in range(2)]
        b_ss = P.bufs_n("ss", 2)
        sg = [S("sg%d" % i, [128, 512], BF16) for i in range(2)]
        b_sg = P.bufs_n("sg", 2)
        ysb = [S("ysb%d" % i, [128, D], F32) for i in range(1)] * 2
        b_ysb = P.bufs_n("ysb", 1) * 2
        ss2 = [S("ss2%d" % i, [128, 2], F32) for i in range(2)]
        b_ss2 = P.bufs_n("ss2", 2)

        wd_src = k.wd[l, f]

        def load_wd(c):
            P.dma("pool", wd[:, 2 * c:2 * c + 2, :], wd_src[c * 256:(c + 1) * 256, :].rearrange("(c p) n -> p c n", p=128),
                  w=[b_wd[c]])

        blocks = [tiles[i:i + 4] for i in range(0, len(tiles), 4)]
        nblk = len(blocks)

        b_wscr = [P.bufs_n("wscr%d_" % i, nfb) for i in range(2)]

        def load_w(idx):
            fb = idx % nfb
            s = idx % 2
            if idx < nfb:
                P.dma("pool", wgs[s][:], k.wg[l, f, :, fb * 256:(fb + 1) * 256].rearrange("(k p) n -> p k n", p=128), w=[b_wg[s]])
                P.dma("pool", wus[s][:], k.wu[l, f, :, fb * 256:(fb + 1) * 256].rearrange("(k p) n -> p k n", p=128), w=[b_wu[s]])
                P.dma("sp", k.wscr[0][fb], wgs[s][:].rearrange("p a b -> p (a b)"), r=[b_wg[s]], w=[b_wscr[0][fb]])
                P.dma("sp", k.wscr[1][fb], wus[s][:].rearrange("p a b -> p (a b)"), r=[b_wu[s]], w=[b_wscr[1][fb]])
            else:
                P.dma("sp", wgs[s][:].rearrange("p a b -> p (a b)"), k.wscr[0][fb], r=[b_wscr[0][fb]], w=[b_wg[s]])
                P.dma("sp", wus[s][:].rearrange("p a b -> p (a b)"), k.wscr[1][fb], r=[b_wscr[1][fb]], w=[b_wu[s]])

        total = nblk * nfb
        load_w(0)
        it = 0
        nt_state = [0]

        def premod_blk_tile(bi, j):
            t = blocks[bi][j]
            n_ = nt_state[0]
            premod_tile(k, t, slot, hTs[bi % 2][:, :, j * 128:(j + 1) * 128], b_hTs[bi % 2][j], xn[n_ % 2][:], b_xn[n_ % 2],
                        ss[n_ % 2][:], b_ss[n_ % 2], junk[:], b_junk, alt_bank=6)
            nt_state[0] += 1

        for j in range(len(blocks[0])):
            premod_blk_tile(0, j)
        for bi, blk in enumerate(blocks):
            ntok = len(blk) * 128
            hT = hTs[bi % 2]
            hbufs = [b_hTs[bi % 2][j] for j in range(len(blk))]
            for fb in range(nfb):
                if it + 1 < total:
                    load_w(it + 1)
                if bi == 0:
                    load_wd(fb)
                s = it % 2
                for ci in range(FB):
                    c = fb * FB + ci
                    pset = (it * FB + ci) % 2
                    pg, pu = k.ps[pset * 2], k.ps[pset * 2 + 1]
                    bpg, bpu = k.b_ps[pset * 2], k.b_ps[pset * 2 + 1]
                    for kk in range(KD):
                        _mm(P, pg[:, 0:ntok], wgs[s][:, kk, ci * 128:(ci + 1) * 128], hT[:, kk, 0:ntok], kk == 0, kk == KD - 1,
                            [b_wg[s]] + hbufs, [bpg])
                    for kk in range(KD):
                        _mm(P, pu[:, 0:ntok], wus[s][:, kk, ci * 128:(ci + 1) * 128], hT[:, kk, 0:ntok], kk == 0, kk == KD - 1,
                            [b_wu[s]] + hbufs, [bpu])
                    _act(P, sg[pset][:, 0:ntok], pg[:, 0:ntok], AF.Silu, [bpg], [b_sg[pset]])
                    _tt(P, aT[:, c, 0:ntok], sg[pset][:, 0:ntok], pu[:, 0:ntok], ALU.mult, [b_sg[pset], bpu], [b_aT[c]])
                it += 1
                if bi + 1 < nblk and fb % 2 == 1 and fb // 2 < len(blocks[bi + 1]):
                    premod_blk_tile(bi + 1, fb // 2)
            for j, t in enumerate(blk):
                yi = t % 2
                for half in range(2):
                    py = k.ps[4 + half]
                    bpy = k.b_ps[4 + half]
                    for c in range(NF):
                        _mm(P, py[:], aT[:, c, j * 128:(j + 1) * 128], wd[:, c, half * 512:(half + 1) * 512], c == 0, c == NF - 1,
                            [b_aT[c], b_wd[c // 2]], [bpy])
                    _cp(P, ysb[yi][:, half * 512:(half + 1) * 512], py[:], [bpy], [b_ysb[yi]])
                    _act(P, junk[:, 0:512], ysb[yi][:, half * 512:(half + 1) * 512], AF.Square, [b_ysb[yi]], [b_junk, b_ss2[yi]],
                         accum_out=ss2[yi][:, half:half + 1])
                post_tile(k, t, slot, ysb[yi][:], b_ysb[yi], ss2[yi], b_ss2[yi])
        P.flush()


TBLK = [(0, 256), (256, 512), (768, 512), (1280, 512), (1792, 512)]
C_ID, C_ONE, C_LE, C_GE, C_LT, C_GT, C_MK, C_N = 0, 1, 2, 3, 4, 5, 6, 13


def _rawcol(tok):
    return tok + 2 if tok < 256 else tok + 6


def _memset(P, ap, val, w, eng="dve"):
    return P._add(eng, lambda e: e.memset(ap, val), (), w)


def mix_phase(k, l):
    nc, P = k.nc, k.P
    last = l == DEPTH - 1
    with ExitStack() as mx:
        S = lambda n, sh, dt, stack=mx: k.S(n, sh, dt, stack)
        m = K()
        m.gb = S("gb", [128, NT, 16], F32)
        m.nb = S("nbeta", [128, NT, 8], F32)
        m.EX = S("EX", [128, NT, 24], F32)
        m.b_gb = P.buf("gb")
        m.b_hscr = P.bufs_n("hscr", len(TBLK))
        with ExitStack() as ph:
            Sp = lambda n, sh, dt: k.S(n, sh, dt, ph)
            xn = [Sp("xn%d" % i, [128, D], BF16) for i in range(2)]
            b_xn = P.bufs_n("xn", 2)
            junk = Sp("junk", [128, D], BF16)
            b_junk = P.buf("junk")
            ss = [Sp("ss%d" % i, [128, 1], F32) for i in range(2)]
            b_ss = P.bufs_n("ss", 2)
            hblk = [Sp("hblkA%d" % i, [128, KD, 512], BF16) for i in range(2)]
            b_hblk = [P.bufs_n("hblkA%d_" % i, 4) for i in range(2)]
            wab = Sp("wab", [128, KD, 16], BF16)
            b_wab = P.buf("wab")
            wk1 = Sp("wk1", [128, NT, 8], F32)
            wk2 = Sp("wk2", [128, NT, 8], F32)
            b_wk = P.buf("wk")
            P.dma("pool", wab[:], k.w_in[l, :, OFF_A:OFF_A + 16].rearrange("(k p) n -> p k n", p=128), w=[b_wab])
            pab = k.ps[6]
            bpab = k.b_ps[6]
            for bi, (t0, nt) in enumerate(TBLK):
                s = bi % 2
                for j in range(nt // 128):
                    t = t0 // 128 + j
                    premod_tile(k, t, 1, hblk[s][:, :, j * 128:(j + 1) * 128], b_hblk[s][j], xn[t % 2][:], b_xn[t % 2],
                                ss[t % 2][:], b_ss[t % 2], junk[:], b_junk, alt_bank=4)
                    for kk in range(KD):
                        _mm(P, pab[:, t * 16:(t + 1) * 16], hblk[s][:, kk, j * 128:(j + 1) * 128], wab[:, kk, :], kk == 0, kk == KD - 1,
                            [b_hblk[s][j], b_wab], [bpab])
                P.dma("sp", k.hscr[:, :, t0:t0 + nt], hblk[s][:, :, 0:nt], r=b_hblk[s][0:nt // 128], w=[m.b_hscr[bi]])
            gbf = m.gb[:].rearrange("p a b -> p (a b)")
            _cp(P, gbf, pab[:, 0:NT * 16], [bpab], [m.b_gb])
            a3 = m.gb[:, :, 0:8]
            b3 = m.gb[:, :, 8:16]
            gd = Sp("gd", [128, 2, NT * 8], F32)
            P.dma("sp", gd[:].rearrange("p a b -> p (a b)"), k.gdnc_d[:, l * 2 * NT * 8:(l + 1) * 2 * NT * 8], w=[b_wk])
            dtb3 = gd[:, 0, :].rearrange("p (a b) -> p a b", b=8)
            alg3 = gd[:, 1, :].rearrange("p (a b) -> p a b", b=8)
            rw = [m.b_gb, b_wk, k.b_small]
            _tt(P, wk1[:], a3, dtb3, ALU.add, rw, [b_wk])
            _stt(P, wk2[:], wk1[:], -1.0, wk1[:], ALU.mult, ALU.max, rw, [b_wk])
            _act(P, wk2[:], wk2[:], AF.Exp, rw, [b_wk], scale=-1.0)
            _act(P, wk2[:], wk2[:], AF.Ln, rw, [b_wk], bias=1.0)
            _ts(P, wk1[:], wk1[:], 0.0, None, ALU.max, None, rw, [b_wk])
            _tt(P, wk1[:], wk1[:], wk2[:], ALU.add, rw, [b_wk])
            _act(P, wk2[:], alg3, AF.Exp, rw, [b_wk])
            _stt(P, a3, wk1[:], -1.0, wk2[:], ALU.mult, ALU.mult, rw, [m.b_gb])
            _act(P, b3, b3, AF.Sigmoid, rw, [m.b_gb])
            _ts(P, m.nb[:], b3, -1.0, None, ALU.mult, None, rw, [m.b_gb])
            pex = k.ps[5]
            bpex = k.b_ps[5]
            cs = k.cstt
            for t in range(NT):
                base = t * 24
                gf = m.gb[:, t, 0:4]
                gbw = m.gb[:, t, 4:8]
                rr = [m.b_gb, k.b_cst]
                _mm(P, pex[:, base + 0:base + 4], cs[:, C_LE, :], gf, True, True, rr, [bpex])
                _mm(P, pex[:, base + 4:base + 8], cs[:, C_GE, :], gbw, True, True, rr, [bpex])
                _mm(P, pex[:, base + 8:base + 12], cs[:, C_GT, :], gf, True, True, rr, [bpex])
                _mm(P, pex[:, base + 12:base + 16], cs[:, C_LT, :], gbw, True, True, rr, [bpex])
                _mm(P, pex[:, base + 16:base + 24], cs[:, C_ONE, :], m.gb[:, t, 0:8], True, True, rr, [bpex])
            _act(P, m.EX[:].rearrange("p a b -> p (a b)"), pex[:, 0:NT * 24], AF.Exp, [bpex], [m.b_gb])
            P.flush()
        for h in range(4):
            gdn_head(k, m, l, h)
        mla_group(k, m, l)
    out_proj(k, l)


def gdn_head(k, m, l, h):
    nc, P = k.nc, k.P
    cs = k.cstt
    with ExitStack() as hd:
        S = lambda n, sh, dt, stack=hd: k.S(n, sh, dt, stack)
        cT = [S("cT%d" % i, [128, T], BF16) for i in range(3)]
        b_cT = [P.bufs_n("cT%d_" % i, len(TBLK)) for i in range(3)]
        kn_tok = S("kn_tok", [128, NT, 128], BF16)
        v_tok = S("v_tok", [128, NT, 128], BF16)
        b_tok = P.buf("tok")
        zs = S("zs", [128, NT, 128], BF16)
        b_zs = P.buf("zs")
        o_acc = S("o_acc", [128, NT, 128], F32)
        b_o = P.bufs_n("oacc", NT)
        with ExitStack() as g1:
            Sp = lambda n, sh, dt: k.S(n, sh, dt, g1)
            raw = [Sp("raw%d" % i, [128, T + 8], BF16) for i in range(3)]
            b_raw = P.bufs_n("raw", 3)
            acc = Sp("acc", [128, T], F32)
            b_acc = P.buf("acc")
            wq = [Sp("wq%d" % i, [128, KD, 128], BF16) for i in range(4)]
            b_wq = P.bufs_n("wq", 4)
            sqall = Sp("sqall", [128, T], BF16)
            b_sqall = P.buf("sqall")
            rsall = Sp("rsall", [128, T], F32)
            b_rsall = P.buf("rsall")
            for i, c in enumerate((h, 4 + h, 8 + h, 12 + h)):
                P.dma("pool", wq[i][:], k.w_in[l, :, c * 128:(c + 1) * 128].rearrange("(k p) n -> p k n", p=128), w=[b_wq[i]])
            for i in range(3):
                _memset(P, raw[i][:], 0.0, [b_raw[i]])
            hblk = [Sp("hblkG%d" % i, [128, KD, 512], BF16) for i in range(2)]
            b_hblk = P.bufs_n("hblkG", 2)
            for bi, (t0, nt) in enumerate(TBLK):
                s = bi % 2
                P.dma("sp", hblk[s][:, :, 0:nt], k.hscr[:, :, t0:t0 + nt], r=[m.b_hscr[bi]], w=[b_hblk[s]])
                for i in range(3):
                    pp = k.ps[i]
                    bpp = k.b_ps[i]
                    for kk in range(KD):
                        _mm(P, pp[:, 0:nt], wq[i][:, kk, :], hblk[s][:, kk, 0:nt], kk == 0, kk == KD - 1, [b_wq[i], b_hblk[s]], [bpp])
                    rc = _rawcol(t0)
                    _act(P, raw[i][:, rc:rc + nt], pp[:, 0:nt], AF.Copy, [bpp], [b_raw[i]])
                pp = k.ps[3]
                bpp = k.b_ps[3]
                ntl = nt // 128
                for q in range(ntl):
                    for kk in range(KD):
                        _mm(P, pp[:, q * 128:(q + 1) * 128], hblk[s][:, kk, q * 128:(q + 1) * 128], wq[3][:, kk, :], kk == 0, kk == KD - 1,
                            [b_hblk[s], b_wq[3]], [bpp])
                tt0 = t0 // 128
                _act(P, zs[:, tt0:tt0 + ntl, :].rearrange("p a b -> p (a b)"), pp[:, 0:ntl * 128], AF.Silu, [bpp], [b_zs])
            n = 0
            for i in range(3):
                c = (h, 4 + h, 8 + h)[i]
                for (s0, s1, base) in ((0, 256, 0), (256, T, 260)):
                    ln = s1 - s0
                    for j in range(5):
                        cw = k.convw[:, (l * 12 + c) * 5 + j:(l * 12 + c) * 5 + j + 1]
                        src = raw[i][:, base + j:base + j + ln]
                        if j == 0:
                            _ts(P, acc[:, s0:s1], src, cw, None, ALU.mult, None, [b_raw[i], k.b_small], [b_acc])
                        else:
                            _stt(P, acc[:, s0:s1], src, cw, acc[:, s0:s1], ALU.mult, ALU.add, [b_raw[i], k.b_small, b_acc], [b_acc])
                if i == 2:
                    _act(P, cT[2][:], acc[:], AF.Silu, [b_acc], b_cT[2])
                else:
                    _act(P, acc[:], acc[:], AF.Silu, [b_acc], [b_acc])
                    scale = (128.0 ** -0.5) if i == 0 else 1.0
                    _act(P, sqall[:], acc[:], AF.Square, [b_acc], [b_sqall])
                    for bi, (t0, nt) in enumerate(TBLK):
                        pp = k.ps[2 + bi]
                        bpp = k.b_ps[2 + bi]
                        _mm(P, pp[:, 0:nt], k.ones_b[:], sqall[:, t0:t0 + nt], True, True, [b_sqall, k.b_cst], [bpp])
                        _ts(P, rsall[:, t0:t0 + nt], pp[:, 0:nt], EPS, None, ALU.add, None, [bpp], [b_rsall])
                    _act(P, rsall[:], rsall[:], AF.Sqrt, [b_rsall], [b_rsall])
                    P.dve(lambda e: e.reciprocal(out=rsall[:], in_=rsall[:]), [b_rsall], [b_rsall])
                    _stt(P, cT[i][:], acc[:], scale, rsall[:], ALU.mult, ALU.mult, [b_acc, b_rsall], b_cT[i])
            for (src_i, dst) in ((1, kn_tok), (2, v_tok)):
                for t0 in range(0, NT, 8):
                    ntl = min(8, NT - t0)
                    for q in range(ntl):
                        t = t0 + q
                        _tp(P, k.ps_tp[:, q * 128:(q + 1) * 128], cT[src_i][:, t * 128:(t + 1) * 128], k.ident_b[:],
                            b_cT[src_i] + [k.b_cst], [k.b_tp])
                    _cp(P, dst[:, t0:t0 + ntl, :].rearrange("p a b -> p (a b)"), k.ps_tp[:, 0:ntl * 128], [k.b_tp], [b_tok])
            for t in range(NT):
                _memset(P, o_acc[:, t, :], 0.0, [b_o[t]])
            import os
            if k.dbgmode and h == int(os.environ.get("DBG_HEAD", "0")):
                for i in range(3):
                    P.dma("sp", k.dbg3[i], cT[i][:], r=b_cT[i])
                P.dma("sp", k.dbg5[:, 0:NT * 16], m.gb[:].rearrange("p a b -> p (a b)"), r=[m.b_gb])
                P.dma("sp", k.dbg5[:, NT * 16:NT * 40], m.EX[:].rearrange("p a b -> p (a b)"), r=[m.b_gb])
            P.flush()
        with ExitStack() as g2:
            Sp = lambda n, sh, dt: k.S(n, sh, dt, g2)
            NJ = 3
            NS = 4
            lhsE = [[Sp("lhsE%d_%d" % (d, j), [128, 128], F32) for j in range(NJ)] for d in range(2)]
            DMi = [Sp("DMi%d" % j, [128, 256], F32) for j in range(NJ)]
            DMs = [Sp("DMs%d" % j, [128, 256], F32) for j in range(NJ)]
            MM_ = [Sp("MM%d" % j, [128, 512], F32) for j in range(NJ)]
            BB_ = [[Sp("BB%d_%d" % (j, i), [128, 512], F32) for i in range(2)] for j in range(NJ)]
            XX_ = [Sp("XX%d" % j, [128, 512], F32) for j in range(NJ)]
            TT_ = [Sp("TT%d" % j, [128, 512], F32) for j in range(NJ)]
            b_MM_ = [P.buf("MM%d" % j) for j in range(NJ)]
            b_BB_ = [[P.buf("BB%d_%d" % (j, i)) for i in range(2)] for j in range(NJ)]
            tp32 = k.ps_tp[:].bitcast(F32)
            JB = [((k.ps[0], k.b_ps[0]), (k.ps[2], k.b_ps[2])), ((k.ps[3], k.b_ps[3]), (k.ps[4], k.b_ps[4])),
                  ((k.ps[5], k.b_ps[5]), (tp32, k.b_tp))]
            R = [[Sp("R%d_%d" % (d, j), [128, 256], F32) for j in range(NJ)] for d in range(2)]
            wtok = [[Sp("wtok%d_%d" % (d, j), [128, 128], F32) for j in range(NJ)] for d in range(2)]
            bj = lambda nm: [P.buf("%s%d" % (nm, j)) for j in range(NJ)]
            b_XX_, b_TT_, b_DM = bj("XX"), bj("TT"), bj("DM")
            Bj = lambda nm: [[P.buf("%s%d_%d" % (nm, d, j)) for j in range(NJ)] for d in range(2)]
            b_lhsE, b_R, b_wtok = Bj("lhsE"), Bj("R"), Bj("wtok")
            QK = [[Sp("QK%d_%d" % (d, s), [128, 128], BF16) for s in range(NS)] for d in range(2)]
            kdec = [[Sp("kdec%d_%d" % (d, s), [128, 128], F32) for s in range(NS)] for d in range(2)]
            u0b = [[Sp("u0b%d_%d" % (d, s), [128, 128], F32) for s in range(NS)] for d in range(2)]
            nwT = [[Sp("nwT%d_%d" % (d, s), [128, 128], F32) for s in range(NS)] for d in range(2)]
            B = lambda nm: [[P.buf("%s%d_%d" % (nm, d, s)) for s in range(NS)] for d in range(2)]
            b_QK, b_kdec, b_u0b, b_nwT = B("QK"), B("kdec"), B("u0b"), B("nwT")
            Sst = [Sp("Sst%d" % d, [128, 128], F32) for d in range(2)]
            Sbf = [Sp("Sbf%d" % d, [128, 128], BF16) for d in range(2)]
            Usb = [Sp("Usb%d" % d, [128, 128], BF16) for d in range(2)]
            Usf = [Sp("Usf%d" % d, [128, 128], F32) for d in range(2)]
            b_Uf = P.bufs_n("Usf", 2)
            b_S = P.bufs_n("Sst", 2)
            b_Sbf = P.bufs_n("Sbf", 2)
            b_U = P.bufs_n("Usb", 2)
            for d in range(2):
                _memset(P, Sst[d][:], 0.0, [b_S[d]])
                _memset(P, Sbf[d][:], 0.0, [b_Sbf[d]])
            order = [list(range(NT)), [1, 0] + list(range(NT - 1, 1, -1))]
            cb = lambda c: [b_cT[0][_blk_of(c)], b_cT[1][_blk_of(c)]]

            def pre(step, j):
                s = step % NS
                cc = [order[0][step], order[1][step]]
                MM, XX, TT = MM_[j], XX_[j], TT_[j]
                b_MM, b_XX, b_TT = b_MM_[j], b_XX_[j], b_TT_[j]
                BB, b_BB = BB_[j], b_BB_[j]
                pE, bpE = JB[j][0]
                pA, bpA = JB[j][1]
                pB, bpB = pE, bpE
                MM3 = MM[:].rearrange("p (a b) -> p a b", a=4)

                def mask_level(lv):
                    mk = cs[:, C_MK + lv, :].unsqueeze(1).to_broadcast([128, 2, 128])
                    lo = 0 if lv == 0 else 256
                    _tt(P, BB[lv % 2][:, lo:lo + 256].rearrange("p (a b) -> p a b", a=2), MM[:, lo:lo + 256].rearrange("p (a b) -> p a b", a=2),
                        mk, ALU.mult, [b_MM, k.b_cst], [b_BB[lv % 2]], eng=("pool" if lv % 2 == 0 else "dve"))

                for d in range(2):
                    c = cc[d]
                    gcol = m.gb[:, c, d * 4 + h:d * 4 + h + 1]
                    msk = cs[:, C_GT, :] if d == 0 else cs[:, C_LT, :]
                    _ts(P, lhsE[d][j][:], msk, gcol, None, ALU.mult, None, [m.b_gb, k.b_cst], [b_lhsE[d][j]])
                yield
                for d in range(2):
                    c = cc[d]
                    rhs = cs[:, C_LE, :] if d == 0 else cs[:, C_GE, :]
                    _mm(P, pE[:, d * 128:(d + 1) * 128], lhsE[d][j][:], rhs, True, True, [b_lhsE[d][j], k.b_cst], [bpE])
                    knc = cT[1][:, c * 128:(c + 1) * 128]
                    _mm(P, pE[:, 256 + d * 128:256 + (d + 1) * 128], knc, knc, True, True, cb(c), [bpE])
                yield
                _act(P, DMi[j][:], pE[:, 0:256], AF.Exp, [bpE], [b_DM[j]])
                yield
                inc2 = cs[:, C_LE:C_GE + 1, :].rearrange("p a b -> p (a b)")
                str2 = cs[:, C_LT:C_GT + 1, :].rearrange("p a b -> p (a b)")
                _tt(P, DMs[j][:], DMi[j][:], str2, ALU.mult, [b_DM[j], k.b_cst], [b_DM[j]])
                _tt(P, DMi[j][:], DMi[j][:], inc2, ALU.mult, [b_DM[j], k.b_cst], [b_DM[j]])
                for d in range(2):
                    c = cc[d]
                    bcol = m.gb[:, c, 8 + d * 4 + h:8 + d * 4 + h + 1]
                    _stt(P, MM[:, d * 128:(d + 1) * 128], pE[:, 256 + d * 128:256 + (d + 1) * 128], bcol, DMs[j][:, d * 128:(d + 1) * 128],
                         ALU.mult, ALU.mult, [bpE, m.b_gb, b_DM[j]], [b_MM])
                yield
                for d in range(2):
                    c = cc[d]
                    knc = cT[1][:, c * 128:(c + 1) * 128]
                    qnc = cT[0][:, c * 128:(c + 1) * 128]
                    _mm(P, pE[:, d * 128:(d + 1) * 128], knc, qnc, True, True, cb(c), [bpE])
                for d in range(2):
                    _tp(P, pA[:, d * 128:(d + 1) * 128], MM[:, d * 128:(d + 1) * 128], k.ident_f, [b_MM, k.b_cst], [bpA])
                yield
                for d in range(2):
                    _tt(P, QK[d][s][:], pE[:, d * 128:(d + 1) * 128], DMi[j][:, d * 128:(d + 1) * 128], ALU.mult, [bpE, b_DM[j]],
                        [b_QK[d][s]])
                _cp(P, MM[:, 256:512], pA[:, 0:256], [bpA], [b_MM], eng="act")
                yield
                mask_level(0)
                for d in range(2):
                    c = cc[d]
                    eG = m.EX[:, c, d * 4 + h:d * 4 + h + 1]
                    ekd = m.EX[:, c, 8 + d * 4 + h:8 + d * 4 + h + 1]
                    _act(P, R[d][j][:, 128:256], kn_tok[:, c, :], AF.Copy, [b_tok, m.b_gb], [b_R[d][j]], scale=eG)
                    _act(P, kdec[d][s][:], kn_tok[:, c, :], AF.Copy, [b_tok, m.b_gb], [b_kdec[d][s]], scale=ekd)
                yield
                id2 = cs[:, C_ID, :].unsqueeze(1).to_broadcast([128, 2, 128])
                _tt(P, XX[:, 0:256].rearrange("p (a b) -> p a b", a=2), id2, BB[0][:, 0:256].rearrange("p (a b) -> p a b", a=2), ALU.subtract,
                    [b_BB[0], k.b_cst], [b_XX])
                mask_level(1)
                for d in range(2):
                    c = cc[d]
                    _cp(P, R[d][j][:, 0:128], v_tok[:, c, :], [b_tok], [b_R[d][j]], eng="pool")
                yield
                for lv in range(1, 7):
                    lastlv = lv == 6
                    Bl = BB[lv % 2]
                    bBl = b_BB[lv % 2]
                    for d in range(2):
                        _mm(P, pA[:, d * 128:(d + 1) * 128], Bl[:, (2 + d) * 128:(3 + d) * 128], XX[:, d * 128:(d + 1) * 128], True, True,
                            [bBl, b_XX], [bpA])
                    for d in range(2):
                        _tp(P, pA[:, (2 + d) * 128:(3 + d) * 128], XX[:, d * 128:(d + 1) * 128], k.ident_f, [b_XX, k.b_cst], [bpA])
                    yield
                    _cp(P, TT[:, 0:512], pA[:, 0:512], [bpA], [b_TT], eng="act")
                    if not lastlv:
                        mask_level(lv + 1)
                    yield
                    for d in range(2):
                        _mm(P, pB[:, d * 128:(d + 1) * 128], TT[:, (2 + d) * 128:(3 + d) * 128], TT[:, d * 128:(d + 1) * 128], True, True,
                            [b_TT], [bpB])
                    yield
                    _tt(P, XX[:, 0:256], XX[:, 0:256], pB[:, 0:256], ALU.subtract, [b_XX, bpB], [b_XX])
                    yield
                for d in range(2):
                    _mm(P, pA[:, d * 256:(d + 1) * 256], XX[:, d * 128:(d + 1) * 128], R[d][j][:], True, True, [b_XX, b_R[d][j]], [bpA])
                yield
                for d in range(2):
                    c = cc[d]
                    bcol = m.gb[:, c, 8 + d * 4 + h:8 + d * 4 + h + 1]
                    _act(P, wtok[d][j][:], pA[:, d * 256 + 128:d * 256 + 256], AF.Copy, [bpA], [b_wtok[d][j]], scale=-1.0)
                    _ts(P, u0b[d][s][:], pA[:, d * 256:d * 256 + 128], bcol, None, ALU.mult, None, [bpA, m.b_gb], [b_u0b[d][s]])
                yield
                for d in range(2):
                    _tp(P, pB[:, d * 128:(d + 1) * 128], wtok[d][j][:], k.ident_f, [b_wtok[d][j], k.b_cst], [bpB])
                yield
                for d in range(2):
                    _cp(P, nwT[d][s][:], pB[:, d * 128:(d + 1) * 128], [bpB], [b_nwT[d][s]], eng=("act" if d == 0 else "dve"))
                yield

            def scan(step):
                s = step % NS
                pSs = [k.ps[6], k.ps[1]]
                bpSs = [k.b_ps[6], k.b_ps[1]]
                for d in range(2):
                    c = order[d][step]
                    pS, bpS = pSs[d], bpSs[d]
                    qnc = cT[0][:, c * 128:(c + 1) * 128]
                    _mm(P, pS[:, 0:128], nwT[d][s][:], Sst[d][:], True, True, [b_nwT[d][s], b_S[d]], [bpS])
                    _mm(P, pS[:, 128:256], qnc, Sbf[d][:], True, True, [b_cT[0][_blk_of(c)], b_Sbf[d]], [bpS])
                yield
                for d in range(2):
                    c = order[d][step]
                    pS, bpS = pSs[d], bpSs[d]
                    bcol = m.gb[:, c, 8 + d * 4 + h:8 + d * 4 + h + 1]
                    _stt(P, Usf[d][:], pS[:, 0:128], bcol, u0b[d][s][:], ALU.mult, ALU.add, [bpS, m.b_gb, b_u0b[d][s]], [b_Uf[d]])
                yield
                for d in range(2):
                    _cp(P, Usb[d][:], Usf[d][:], [b_Uf[d]], [b_U[d]], eng="act")
                    pS, bpS = pSs[d], bpSs[d]
                    _mm(P, pS[:, 384:512], kdec[d][s][:], Usf[d][:], True, True, [b_kdec[d][s], b_Uf[d]], [bpS])
                yield
                for d in range(2):
                    pS, bpS = pSs[d], bpSs[d]
                    _mm(P, pS[:, 256:384], QK[d][s][:], Usb[d][:], True, True, [b_QK[d][s], b_U[d]], [bpS])
                yield
                for d in range(2):
                    c = order[d][step]
                    pS, bpS = pSs[d], bpSs[d]
                    eG = m.EX[:, c, d * 4 + h:d * 4 + h + 1]
                    cdec = m.EX[:, c, 16 + d * 4 + h:16 + d * 4 + h + 1]
                    oc = o_acc[:, c, :]
                    _stt(P, Sst[d][:], Sst[d][:], cdec, pS[:, 384:512], ALU.mult, ALU.add, [bpS, m.b_gb, b_S[d]], [b_S[d]])
                    _stt(P, oc, pS[:, 128:256], eG, oc, ALU.mult, ALU.add, [bpS, m.b_gb, b_o[c]], [b_o[c]])
                    _tt(P, oc, pS[:, 256:384], oc, ALU.add, [bpS, b_o[c]], [b_o[c]])
                yield
                for d in range(2):
                    _cp(P, Sbf[d][:], Sst[d][:], [b_S[d]], [b_Sbf[d]], eng="act")
                yield

            pres = {}
            pre_step = {}
            done_pre = set()
            nxt = 0
            scan_pos = 0
            scan_gen = None
            while scan_pos < NT:
                for j in range(NJ):
                    if j not in pres and nxt < NT and nxt < scan_pos + NS:
                        pres[j] = pre(nxt, j)
                        pre_step[j] = nxt
                        nxt += 1
                for j in list(pres):
                    try:
                        next(pres[j])
                    except StopIteration:
                        done_pre.add(pre_step[j])
                        del pres[j]
                if scan_gen is None and scan_pos in done_pre:
                    scan_gen = scan(scan_pos)
                if scan_gen is not None:
                    try:
                        next(scan_gen)
                    except StopIteration:
                        scan_gen = None
                        scan_pos += 1
            P.flush()
        with ExitStack() as g3:
            Sp = lambda n, sh, dt: k.S(n, sh, dt, g3)
            ssn = Sp("ssn", [128, NT], F32)
            b_ssn = P.buf("ssn")
            junk = Sp("junkg", [128, 128], BF16)
            b_junk = P.buf("junkg")
            tmpo = [Sp("tmpo%d" % i, [128, 128], F32) for i in range(2)]
            b_tmpo = P.bufs_n("tmpo", 2)
            obuf = Sp("obuf", [128, NT, 128], BF16)
            b_obuf = P.buf("obuf")
            import os
            if k.dbgmode and h == int(os.environ.get("DBG_HEAD", "0")):
                P.dma("sp", k.dbg4, o_acc[:].rearrange("p a b -> p (a b)"), r=b_o)
            sq3 = Sp("sq3", [128, NT, 128], F32)
            b_sq3 = P.buf("sq3")
            _act(P, sq3[:].rearrange("p a b -> p (a b)"), o_acc[:].rearrange("p a b -> p (a b)"), AF.Square, b_o, [b_sq3])
            P.dve(lambda e: e.reduce_sum(out=ssn[:], in_=sq3[:], axis=AX.X), [b_sq3], [b_ssn])
            _rstd(P, ssn[:], NT, 1.0 / 128, EPS, [b_ssn], [b_ssn])
            gnb = k.gnb[:, l * 128:(l + 1) * 128]
            _tt(P, sq3[:], o_acc[:], gnb.unsqueeze(1).to_broadcast([128, NT, 128]), ALU.mult, b_o + [k.b_small], [b_sq3])
            _tt(P, sq3[:], sq3[:], zs[:], ALU.mult, [b_sq3, b_zs], [b_sq3])
            _tt(P, obuf[:], sq3[:], ssn[:].unsqueeze(2).to_broadcast([128, NT, 128]), ALU.mult, [b_sq3, b_ssn], [b_obuf])
            P.dma("sp", k.oscr[:, h * 128:(h + 1) * 128].rearrange("(t p) c -> p t c", p=128), obuf[:], r=[b_obuf], w=[k.b_oscr])
            P.flush()


def _blk_of(c):
    tok = c * 128
    for bi, (t0, nt) in enumerate(TBLK):
        if t0 <= tok < t0 + nt:
            return bi
    raise ValueError


def mla_group(k, m, l):
    nc, P = k.nc, k.P
    SC = float(96.0 ** -0.5)
    with ExitStack() as ml:
        S = lambda n, sh, dt, stack=ml: k.S(n, sh, dt, stack)
        cn = S("cn", [128, 5, T], BF16)
        b_cn = P.bufs_n("cn", len(TBLK))
        Vall = S("Vall", [128, NT, 8, 65], BF16)
        b_V = P.buf("Vall")
        krr = S("krr", [96, T], BF16)
        b_krr = P.buf("krr")
        cos = S("cosq", [96, T], BF16)
        sin = S("sinq", [96, T], BF16)
        b_rope = P.buf("rope")
        P.dma("pool", cos[:], k.cosq[0:96, :], w=[b_rope])
        P.dma("pool", sin[:], k.sinq[0:96, :], w=[b_rope])
        with ExitStack() as p1:
            Sp = lambda n, sh, dt: k.S(n, sh, dt, p1)
            wc = Sp("wc", [128, KD, 640], BF16)
            b_wc = P.buf("wc")
            wkr = Sp("wkr", [128, KD, 2, 96], BF16)
            b_wkr = P.buf("wkr")
            wv = Sp("wv", [128, 2, 512], BF16)
            b_wv = P.buf("wv")
            rawc = [Sp("rawc%d" % i, [128, 5, 512], F32) for i in range(1)] * 2
            b_rawc = P.bufs_n("rawc", 1) * 2
            sqb = [Sp("sqc%d" % i, [128, 5, 512], BF16) for i in range(1)] * 2
            b_sqb = P.bufs_n("sqc", 1) * 2
            rsb = [Sp("rsc%d" % i, [128, 2, 512], F32) for i in range(1)] * 2
            b_rsb = P.bufs_n("rsc", 1) * 2
            hblk = [Sp("hblkM%d" % i, [128, KD, 512], BF16) for i in range(1)] * 2
            b_hblk = P.bufs_n("hblkM", 1) * 2
            t1 = Sp("t1", [96, 512], F32)
            t2 = Sp("t2", [96, 512], F32)
            b_t = P.buf("t12")
            P.dma("pool", wc[:], k.w_in[l, :, OFF_CQ:OFF_CQ + 640].rearrange("(k p) n -> p k n", p=128), w=[b_wc])
            _memset(P, wkr[:].rearrange("p a b c -> p (a b c)"), 0.0, [b_wkr])
            wsrc = k.w_in[l, :, OFF_KR:OFF_KR + 32].rearrange("(k p) n -> p k n", p=128)
            P.dma("pool", wkr[:, :, 0, 64:96], wsrc, w=[b_wkr])
            P.dma("pool", wkr[:, :, 1, 64:80], wsrc[:, :, 16:32], w=[b_wkr])
            P.dma("pool", wkr[:, :, 1, 80:96], wsrc[:, :, 0:16], w=[b_wkr])
            P.dma("pool", wv[:], k.w_ukv_v[l].rearrange("(k p) n -> p k n", p=128), w=[b_wv])
            _memset(P, Vall[:].rearrange("p a b c -> p (a b c)"), 1.0, [b_V])
            for bi, (t0, nt) in enumerate(TBLK):
                s = bi % 2
                hb = [b_hblk[s]]
                hT_ = hblk[s]
                P.dma("sp", hblk[s][:, :, 0:nt], k.hscr[:, :, t0:t0 + nt], r=[m.b_hscr[bi]], w=[b_hblk[s]])
                for c in range(5):
                    pp = k.ps[c % 2]
                    bpp = k.b_ps[c % 2]
                    for kk in range(KD):
                        _mm(P, pp[:, 0:nt], wc[:, kk, c * 128:(c + 1) * 128], hT_[:, kk, 0:nt], kk == 0, kk == KD - 1, [b_wc] + hb, [bpp])
                    _cp(P, rawc[s][:, c, 0:nt], pp[:, 0:nt], [bpp], [b_rawc[s]])
                    _act(P, sqb[s][:, c, 0:nt], rawc[s][:, c, 0:nt], AF.Square, [b_rawc[s]], [b_sqb[s]])
                for gi, (c0, c1, nfeat) in enumerate(((0, 3, 384.0), (3, 5, 256.0))):
                    pp = k.ps[2 + gi]
                    bpp = k.b_ps[2 + gi]
                    for c in range(c0, c1):
                        _mm(P, pp[:, 0:nt], k.ones_b[:], sqb[s][:, c, 0:nt], c == c0, c == c1 - 1, [b_sqb[s], k.b_cst], [bpp])
                    rs = rsb[s][:, gi, 0:nt]
                    _ts(P, rs, pp[:, 0:nt], 1.0 / nfeat, EPS, ALU.mult, ALU.add, [bpp], [b_rsb[s]])
                    _act(P, rs, rs, AF.Sqrt, [b_rsb[s]], [b_rsb[s]])
                    P.dve(lambda e, a=rs: e.reciprocal(out=a, in_=a), [b_rsb[s]], [b_rsb[s]])
                    for c in range(c0, c1):
                        gcol = (k.qn[:, l * 3 + c:l * 3 + c + 1] if gi == 0 else k.kvn[:, l * 2 + (c - 3):l * 2 + (c - 3) + 1])
                        _stt(P, cn[:, c, t0:t0 + nt], rawc[s][:, c, 0:nt], gcol, rs, ALU.mult, ALU.mult, [b_rawc[s], b_rsb[s], k.b_small],
                             [b_cn[bi]])
                pk, bpk = k.ps[4], k.b_ps[4]
                pks, bpks = k.ps[5], k.b_ps[5]
                for kk in range(KD):
                    _mm(P, pk[0:96, 0:nt], wkr[:, kk, 0, :], hT_[:, kk, 0:nt], kk == 0, kk == KD - 1, [b_wkr] + hb, [bpk])
                for kk in range(KD):
                    _mm(P, pks[0:96, 0:nt], wkr[:, kk, 1, :], hT_[:, kk, 0:nt], kk == 0, kk == KD - 1, [b_wkr] + hb, [bpks])
                _tt(P, t1[64:96, 0:nt], pk[64:96, 0:nt], cos[64:96, t0:t0 + nt], ALU.mult, [bpk, b_rope], [b_t])
                _tt(P, t2[64:96, 0:nt], pks[64:96, 0:nt], sin[64:96, t0:t0 + nt], ALU.mult, [bpks, b_rope], [b_t])
                _tt(P, krr[64:96, t0:t0 + nt], t1[64:96, 0:nt], t2[64:96, 0:nt], ALU.add, [b_t], [b_krr])
                for t in range(t0 // 128, (t0 + nt) // 128):
                    pv, bpv = k.ps[6], k.b_ps[6]
                    for c in range(2):
                        _mm(P, pv[:, 0:512], cn[:, 3 + c, t * 128:(t + 1) * 128], wv[:, c, :], c == 0, c == 1, [b_cn[bi], b_wv], [bpv])
                    _act(P, Vall[:, t, :, 0:64], pv[:, 0:512].rearrange("p (a b) -> p a b", b=64), AF.Copy, [bpv], [b_V])
            P.flush()
        with ExitStack() as p2:
            Sp = lambda n, sh, dt: k.S(n, sh, dt, p2)
            wuq = [Sp("wuq%d" % i, [128, 3, 2, 96], BF16) for i in range(2)]
            b_wuq = P.bufs_n("wuq", 2)
            wuk = [Sp("wuk%d" % i, [128, 2, 64], BF16) for i in range(2)]
            b_wuk = P.bufs_n("wuk", 2)
            Qf = [Sp("Qf%d" % i, [96, T], BF16) for i in range(2)]
            b_Qf = P.bufs_n("Qf", 2)
            Kf = [Sp("Kf%d" % i, [96, T], BF16) for i in range(2)]
            b_Kf = P.bufs_n("Kf", 2)
            t1 = Sp("t1b", [96, 512], F32)
            t2 = Sp("t2b", [96, 512], F32)
            b_t = P.buf("t12b")
            PT = [Sp("PT%d" % i, [128, 512], BF16) for i in range(3)]
            b_PT = P.bufs_n("PT", 3)
            rec = Sp("rec", [128, 4], F32)
            b_rec = P.buf("rec")
            omla = Sp("omla", [128, NT, 512], BF16)
            b_om = P.buf("omla")
            npt = 0
            for hh in range(8):
                s = hh % 2
                P.dma("pool", wuq[s][:, :, 0, :], k.w_uq[l, :, hh * 96:(hh + 1) * 96].rearrange("(k p) n -> p k n", p=128), w=[b_wuq[s]])
                P.dma("pool", wuq[s][:, :, 1, :], k.w_uq_sw[l, :, hh * 96:(hh + 1) * 96].rearrange("(k p) n -> p k n", p=128), w=[b_wuq[s]])
                P.dma("pool", wuk[s][:], k.w_ukv_k[l, :, hh * 64:(hh + 1) * 64].rearrange("(k p) n -> p k n", p=128), w=[b_wuk[s]])
                for bi, (t0, nt) in enumerate(TBLK):
                    pq, bpq = k.ps[0], k.b_ps[0]
                    pqs, bpqs = k.ps[1], k.b_ps[1]
                    pkn, bpkn = k.ps[2], k.b_ps[2]
                    for c in range(3):
                        _mm(P, pq[0:96, 0:nt], wuq[s][:, c, 0, :], cn[:, c, t0:t0 + nt], c == 0, c == 2, [b_wuq[s], b_cn[bi]], [bpq])
                    for c in range(3):
                        _mm(P, pqs[0:96, 0:nt], wuq[s][:, c, 1, :], cn[:, c, t0:t0 + nt], c == 0, c == 2, [b_wuq[s], b_cn[bi]], [bpqs])
                    for c in range(2):
                        _mm(P, pkn[0:64, 0:nt], wuk[s][:, c, :], cn[:, 3 + c, t0:t0 + nt], c == 0, c == 1, [b_wuk[s], b_cn[bi]], [bpkn])
                    _tt(P, t1[:, 0:nt], pq[0:96, 0:nt], cos[:, t0:t0 + nt], ALU.mult, [bpq, b_rope], [b_t])
                    _tt(P, t2[:, 0:nt], pqs[0:96, 0:nt], sin[:, t0:t0 + nt], ALU.mult, [bpqs, b_rope], [b_t])
                    _tt(P, Qf[s][:, t0:t0 + nt], t1[:, 0:nt], t2[:, 0:nt], ALU.add, [b_t], [b_Qf[s]])
                    _act(P, Kf[s][0:64, t0:t0 + nt], pkn[0:64, 0:nt], AF.Copy, [bpkn], [b_Kf[s]])
                _cp(P, Kf[s][64:96, :], krr[64:96, :], [b_krr], [b_Kf[s]])
                for bi, (t0, nt) in enumerate(TBLK):
                    ktiles = [0, 1] if bi == 0 else list(range(NT))
                    nq = nt // 128
                    po, bpo = k.ps[6], k.b_ps[6]
                    po3 = po[:, 0:4 * 65].rearrange("p (a b) -> p a b", b=65)
                    def st_exp(kt):
                        nonlocal npt
                        pst, bpst = k.ps[3 + (npt % 3)], k.b_ps[3 + (npt % 3)]
                        pt, bpt = PT[npt % 3], b_PT[npt % 3]
                        npt += 1
                        _mm(P, pst[:, 0:nt], Kf[s][:, kt * 128:(kt + 1) * 128], Qf[s][:, t0:t0 + nt], True, True, [b_Kf[s], b_Qf[s]], [bpst])
                        _act(P, pt[:, 0:nt], pst[:, 0:nt], AF.Exp, [bpst], [bpt], scale=SC)
                        return pt, bpt

                    nxt_pt = st_exp(ktiles[0])
                    for ki, kt in enumerate(ktiles):
                        pt, bpt = nxt_pt
                        if ki + 1 < len(ktiles):
                            nxt_pt = st_exp(ktiles[ki + 1])
                        for qi in range(nq):
                            P.pe(lambda e, o=po3[:, qi, :], a=pt[:, qi * 128:(qi + 1) * 128], b=Vall[:, kt, hh, :],
                                 st=(ki == 0 and qi == 0), sp=(ki == len(ktiles) - 1):
                                 e.matmul(o, lhsT=a, rhs=b, start=st, stop=sp, skip_group_check=True), [bpt, b_V], [bpo])
                    P.dve(lambda e, o=rec[:, 0:nq], a=po3[:, 0:nq, 64]: e.reciprocal(out=o, in_=a), [bpo], [b_rec])
                    for qi in range(nq):
                        t = t0 // 128 + qi
                        _ts(P, omla[:, t, hh * 64:(hh + 1) * 64], po3[:, qi, 0:64], rec[:, qi:qi + 1], None, ALU.mult, None, [bpo, b_rec], [b_om])
            P.dma("sp", k.oscr[:, 512:1024].rearrange("(t p) c -> p t c", p=128), omla[:], r=[b_om], w=[k.b_oscr])
            P.flush()


def out_proj(k, l):
    nc, P = k.nc, k.P
    last = l == DEPTH - 1
    with ExitStack() as ph:
        S = lambda n, sh, dt: k.S(n, sh, dt, ph)
        wo = S("wo", [128, KD, D], BF16)
        b_wo = P.buf("wo")
        ot = [S("ot%d" % i, [128, D], BF16) for i in range(2)]
        b_ot = P.bufs_n("ot", 2)
        oT = [S("oT%d" % i, [128, KD, 128], BF16) for i in range(2)]
        b_oT = P.bufs_n("oT", 2)
        ysb = [S("ysbo%d" % i, [128, D], F32) for i in range(2)]
        b_ysb = P.bufs_n("ysbo", 2)
        ss2 = [S("ss2o%d" % i, [128, 2], F32) for i in range(2)]
        b_ss2 = P.bufs_n("ss2o", 2)
        junk = S("junko", [128, 512], BF16)
        b_junk = P.buf("junko")
        P.dma("pool", wo[:], k.w_out[l].rearrange("(k p) n -> p k n", p=128), w=[b_wo])
        tiles = list(range(NCT, NT)) if last else list(range(NT))
        for n, t in enumerate(tiles):
            s = n % 2
            P.dma("sp", ot[s][:], k.oscr[t * 128:(t + 1) * 128, :], r=[k.b_oscr], w=[b_ot[s]])
            for kk in range(KD):
                _tp(P, k.ps_tp[:, kk * 128:(kk + 1) * 128], ot[s][:, kk * 128:(kk + 1) * 128], k.ident_b[:], [b_ot[s], k.b_cst], [k.b_tp])
            _cp(P, oT[s][:].rearrange("p a b -> p (a b)"), k.ps_tp[:, :], [k.b_tp], [b_oT[s]], eng="act")
            for half in range(2):
                py, bpy = k.ps[4 + half], k.b_ps[4 + half]
                for kk in range(KD):
                    _mm(P, py[:], oT[s][:, kk, :], wo[:, kk, half * 512:(half + 1) * 512], kk == 0, kk == KD - 1, [b_oT[s], b_wo], [bpy])
                _cp(P, ysb[s][:, half * 512:(half + 1) * 512], py[:], [bpy], [b_ysb[s]])
                _act(P, junk[:], ysb[s][:, half * 512:(half + 1) * 512], AF.Square, [b_ysb[s]], [b_junk, b_ss2[s]],
                     accum_out=ss2[s][:, half:half + 1])
            post_tile(k, t, 1, ysb[s][:], b_ysb[s], ss2[s], b_ss2[s])
        P.flush()


def _fm(v):
    v = np.asarray(v, np.float32)
    n = v.shape[-1] // 128
    lead = v.shape[:-1]
    a = v.reshape(lead + (n, 128))
    a = np.moveaxis(a, -1, 0)
    return np.ascontiguousarray(a.reshape(128, -1))


def _consts():
    c = np.zeros((128, 13, 128), np.float32)
    r = np.arange(128)[:, None]
    q = np.arange(128)[None, :]
    c[:, 0, :] = (r == q)
    c[:, 1, :] = 1.0
    c[:, 2, :] = (r <= q)
    c[:, 3, :] = (r >= q)
    c[:, 4, :] = (r < q)
    c[:, 5, :] = (r > q)
    for lv in range(7):
        sz = 1 << lv
        c[:, 6 + lv, :] = ((r // (2 * sz)) == (q // (2 * sz))) & ((r // sz) != (q // sz))
    return c.reshape(128, 13 * 128)


def _rope_tables():
    S_, GW = 2048, 64
    row = np.repeat(np.arange(S_ // GW), GW).astype(np.float32)
    col = np.tile(np.arange(GW), S_ // GW).astype(np.float32)
    inv = np.power(np.float32(10000.0), -np.arange(0, 16, 2, dtype=np.float32) / np.float32(16)).astype(np.float32)
    ang = np.concatenate([row[:, None] * inv, col[:, None] * inv], -1).astype(np.float32)
    cos, sin = np.cos(ang).T, np.sin(ang).T
    C = np.zeros((128, T), np.float32)
    Sn = np.zeros((128, T), np.float32)
    C[0:96, :] = 1.0
    C[64:80, 256:] = cos
    C[80:96, 256:] = cos
    Sn[64:80, 256:] = -sin
    Sn[80:96, 256:] = sin
    return C, Sn


_CACHE = {}


def kernel(**inp):
    B = inp["x"].shape[0]
    if "nc" not in _CACHE:
        _CACHE["nc"] = build()
    nc, k = _CACHE["nc"]
    shared = {
        "w_ada": np.ascontiguousarray(inp["w_ada"], np.float32),
        "b_fm": _fm(inp["b_ada"]),
        "gpre_fm": _fm(inp["norm_pre"]),
        "gpost_fm": _fm(inp["norm_post"]),
        "ffn_w_gate": np.ascontiguousarray(inp["ffn_w_gate"], np.float32),
        "ffn_w_up": np.ascontiguousarray(inp["ffn_w_up"], np.float32),
        "ffn_w_down": np.ascontiguousarray(inp["ffn_w_down"], np.float32),
        "cst": _consts(),
        "w_in": np.ascontiguousarray(inp["w_in"], np.float32),
        "w_out": np.ascontiguousarray(inp["w_out"], np.float32),
        "qn_fm": _fm(inp["mla_q_norm"]),
        "kvn_fm": _fm(inp["mla_kv_norm"]),
        "w_uq": np.ascontiguousarray(inp["mla_w_uq"], np.float32),
    }
    dtb = np.asarray(inp["gdn_dt_bias"], np.float32).reshape(DEPTH, 1, 1, 8)
    alg = np.asarray(inp["gdn_a_log"], np.float32).reshape(DEPTH, 1, 1, 8)
    g2 = np.concatenate([np.broadcast_to(dtb, (DEPTH, 1, NT, 8)), np.broadcast_to(alg, (DEPTH, 1, NT, 8))], 1)
    shared["gdnc"] = np.ascontiguousarray(np.broadcast_to(g2.reshape(1, -1), (128, DEPTH * 2 * NT * 8)), np.float32)
    cw = np.asarray(inp["gdn_conv"], np.float32)
    cw = cw.transpose(0, 2, 1).reshape(DEPTH, 12, 128, 5).transpose(2, 0, 1, 3)
    shared["convw"] = np.ascontiguousarray(cw.reshape(128, -1))
    gn = np.asarray(inp["gdn_out_norm"], np.float32).reshape(1, -1)
    shared["gnb"] = np.ascontiguousarray(np.broadcast_to(gn, (128, DEPTH * 128)), np.float32)
    perm = np.arange(768).reshape(8, 96)
    perm = np.concatenate([perm[:, 0:64], perm[:, 80:96], perm[:, 64:80]], 1).reshape(-1)
    shared["w_uq_sw"] = np.ascontiguousarray(shared["w_uq"][:, :, perm])
    wkv = np.asarray(inp["mla_w_ukv"], np.float32).reshape(DEPTH, 256, 8, 128)
    shared["w_ukv_k"] = np.ascontiguousarray(wkv[:, :, :, 0:64].reshape(DEPTH, 256, 512))
    shared["w_ukv_v"] = np.ascontiguousarray(wkv[:, :, :, 64:128].reshape(DEPTH, 256, 512))
    shared["cosq"], shared["sinq"] = _rope_tables()
    in_maps = []
    for b in range(B):
        m = dict(shared)
        m["xcat"] = np.ascontiguousarray(np.concatenate([inp["ctx"][b], inp["x"][b]], 0), np.float32)
        cc = np.stack([_fm(inp["c"][b]), _fm(inp["c_ctx"])], -1)
        m["cc_fm"] = np.ascontiguousarray(cc.reshape(128, 16))
        in_maps.append(m)
    res = run_bass_kernel_spmd(nc, in_maps, core_ids=list(range(B)))
    return np.stack([r["out"] for r in res.results], 0)
```
